# Optimizing a Trainium2 kernel written in Bass

```python
import math
import jax, jax.numpy as jnp
from jax import lax
import numpy as np

D_MODEL = 1024
BATCH = 32
SEQ = 2048
DEPTH = 4
DEC_BATCH = 32
DEC_SEQ = 64
PAST_LEN = 4096

CHUNK = 64
N_LEFT_CHUNKS = 8
WIN = N_LEFT_CHUNKS * CHUNK
BAND = WIN + CHUNK
N_HEADS = 16
HEAD_DIM = D_MODEL // N_HEADS
REL_CLIP = 256
SSM_EXPAND = 2
D_INNER = SSM_EXPAND * D_MODEL
SSM_HEAD_DIM = 64
SSM_HEADS = D_INNER // SSM_HEAD_DIM
SSM_GROUPS = 8
HEADS_PER_GROUP = SSM_HEADS // SSM_GROUPS
D_STATE = 128
CONV_WIDTH = 4
CONV_CH = D_INNER + 2 * SSM_GROUPS * D_STATE
SSD_CHUNK = 64
D_FF = 4 * D_MODEL
N_ATT_LAYERS = (DEPTH + 1) // 2
N_SSM_LAYERS = DEPTH // 2
ALPHA = (2 * DEPTH) ** 0.25
BETA = (8 * DEPTH) ** -0.25
LN_EPS = 1e-5
RMS_EPS = 1e-5
NEG_INF = -1e30

kernel_name = "hybrid_chunkband_attn_mamba2_deepnorm_stream_step"


def _layer_norm(x, g, b):
    xf = x.astype(jnp.float32)
    mu = jnp.mean(xf, axis=-1, keepdims=True)
    var = jnp.mean(jnp.square(xf - mu), axis=-1, keepdims=True)
    return ((xf - mu) * lax.rsqrt(var + LN_EPS) * g + b).astype(x.dtype)


def _band_attention(q, k, v, k_past, v_past, rel_table):
    bsz, t = q.shape[0], q.shape[1]
    p = k_past.shape[1]
    n_chunks = -(-t // CHUNK)
    pad_new = n_chunks * CHUNK - t
    qp = jnp.pad(q, ((0, 0), (0, pad_new), (0, 0), (0, 0)))
    kv_pad = ((0, 0), (WIN - p, pad_new), (0, 0), (0, 0))
    kbuf = jnp.pad(jnp.concatenate([k_past.astype(k.dtype), k], axis=1), kv_pad)
    vbuf = jnp.pad(jnp.concatenate([v_past.astype(v.dtype), v], axis=1), kv_pad)
    j = jnp.arange(BAND)
    r = jnp.arange(CHUNK)
    rel = jnp.clip(WIN + r[:, None] - j[None, :], -REL_CLIP, REL_CLIP) + REL_CLIP
    bias = rel_table[:, rel].astype(jnp.float32)
    scale = HEAD_DIM ** -0.5

    def one_chunk(c):
        start = c * CHUNK
        qc = lax.dynamic_slice_in_dim(qp, start, CHUNK, axis=1)
        kc = lax.dynamic_slice_in_dim(kbuf, start, BAND, axis=1)
        vc = lax.dynamic_slice_in_dim(vbuf, start, BAND, axis=1)
        s = jnp.einsum('bqhd,bkhd->bhqk', qc, kc).astype(jnp.float32) * scale + bias
        kidx = start + j
        valid = (kidx >= WIN - p) & (kidx < WIN + t)
        s = jnp.where(valid, s, NEG_INF)
        w = jax.nn.softmax(s, axis=-1).astype(vc.dtype)
        return jnp.einsum('bhqk,bkhd->bqhd', w, vc)

    out = lax.map(one_chunk, jnp.arange(n_chunks))
    out = jnp.moveaxis(out, 0, 1).reshape(bsz, n_chunks * CHUNK, N_HEADS, HEAD_DIM)
    return out[:, :t]


def _attention_mixer(x, k_past, v_past, w_qkv, rel_table, w_o):
    bsz, t, _ = x.shape
    qkv = (x @ w_qkv).reshape(bsz, t, 3, N_HEADS, HEAD_DIM)
    q, k, v = qkv[:, :, 0], qkv[:, :, 1], qkv[:, :, 2]
    o = _band_attention(q, k, v, k_past, v_past, rel_table)
    return (o.reshape(bsz, t, D_MODEL) @ w_o).astype(x.dtype), k, v


def _ssd_scan(xs, dt, a, bm, cm, h0):
    bsz, t = xs.shape[0], xs.shape[1]
    n_chunks = -(-t // SSD_CHUNK)
    pad = n_chunks * SSD_CHUNK - t

    def to_chunks(u):
        u = jnp.pad(u, [(0, 0), (0, pad)] + [(0, 0)] * (u.ndim - 2))
        u = u.reshape((bsz, n_chunks, SSD_CHUNK) + u.shape[2:])
        return jnp.moveaxis(u, 1, 0)

    causal = jnp.tril(jnp.ones((SSD_CHUNK, SSD_CHUNK), dtype=bool))

    def step(h, inp):
        xc, dtc, bc, cc = inp
        cum = jnp.cumsum(dtc * a, axis=1)
        seg = cum[:, :, None] - cum[:, None, :]
        decay = jnp.exp(jnp.where(causal[None, :, :, None, None], seg, -jnp.inf))
        cb = jnp.einsum('btgn,bsgn->btsg', cc, bc)
        y = jnp.einsum('btsg,btsgr,bsgr,bsgrp->btgrp', cb, decay, dtc, xc)
        y = y + jnp.einsum('btgn,bgrpn->btgrp', cc, h) * jnp.exp(cum)[..., None]
        last = cum[:, -1]
        w_in = jnp.exp(last[:, None] - cum) * dtc
        h = h * jnp.exp(last)[..., None, None] + jnp.einsum('bsgn,bsgr,bsgrp->bgrpn', bc, w_in, xc)
        return h, y

    h, ys = lax.scan(step, h0, (to_chunks(xs), to_chunks(dt), to_chunks(bm), to_chunks(cm)))
    y = jnp.moveaxis(ys, 0, 1).reshape((bsz, n_chunks * SSD_CHUNK) + xs.shape[2:])[:, :t]
    return y, h


def _ssm_mixer(x, conv_state, ssm_state, w_in, conv_w, conv_b, dt_bias, a_log, d_skip, norm_w, w_out):
    bsz, t, _ = x.shape
    g, r, p, n = SSM_GROUPS, HEADS_PER_GROUP, SSM_HEAD_DIM, D_STATE
    proj = x @ w_in
    z = proj[..., :D_INNER]
    xbc = proj[..., D_INNER:D_INNER + CONV_CH]
    dt_raw = proj[..., D_INNER + CONV_CH:]
    xpad = jnp.concatenate([conv_state.astype(xbc.dtype), xbc], axis=1)
    conv = conv_b
    for kk in range(CONV_WIDTH):
        conv = conv + xpad[:, kk:kk + t] * conv_w[kk]
    new_conv = xpad[:, -(CONV_WIDTH - 1):]
    xbc = jax.nn.silu(conv)
    xs = xbc[..., :D_INNER].reshape(bsz, t, g, r, p)
    bm = xbc[..., D_INNER:D_INNER + g * n].reshape(bsz, t, g, n)
    cm = xbc[..., D_INNER + g * n:].reshape(bsz, t, g, n)
    dt = jax.nn.softplus((dt_raw + dt_bias).astype(jnp.float32)).reshape(bsz, t, g, r)
    a = -jnp.exp(a_log.astype(jnp.float32)).reshape(g, r)
    h0 = ssm_state.reshape(bsz, g, r, p, n).astype(jnp.float32)
    y, h = _ssd_scan(xs, dt, a, bm, cm, h0)
    y = y + d_skip.reshape(g, r)[..., None] * xs
    y = y.reshape(bsz, t, D_INNER) * jax.nn.silu(z.astype(jnp.float32))
    yg = y.reshape(bsz, t, g, D_INNER // g)
    yg = yg * lax.rsqrt(jnp.mean(jnp.square(yg), axis=-1, keepdims=True) + RMS_EPS)
    y = (yg.reshape(bsz, t, D_INNER) * norm_w).astype(x.dtype)
    return (y @ w_out).astype(x.dtype), new_conv, h.reshape(bsz, SSM_HEADS, SSM_HEAD_DIM, D_STATE)


def _mlp(x, w_up, w_down):
    return (jnp.square(jax.nn.relu(x @ w_up)) @ w_down).astype(x.dtype)


def setup_inputs(seed: int = 0) -> dict:
    key = jax.random.key(seed)
    ks = jax.random.split(key, 24)
    att_cache = min(WIN, PAST_LEN)

    def nrm(k, shape, s):
        return jax.random.normal(k, shape, jnp.float32) * s

    dt0 = jnp.exp(jax.random.uniform(ks[12], (N_SSM_LAYERS, SSM_HEADS), jnp.float32,
                                     math.log(1e-3), math.log(1e-1)))
    return {
        'x_prompt': nrm(ks[0], (BATCH, SEQ, D_MODEL), 1.0),
        'x_sample': nrm(ks[1], (DEC_BATCH, DEC_SEQ, D_MODEL), 1.0),
        'cache_k': nrm(ks[2], (N_ATT_LAYERS, DEC_BATCH, att_cache, N_HEADS, HEAD_DIM), 1.0),
        'cache_v': nrm(ks[3], (N_ATT_LAYERS, DEC_BATCH, att_cache, N_HEADS, HEAD_DIM), 1.0),
        'state_ssm': nrm(ks[4], (N_SSM_LAYERS, DEC_BATCH, SSM_HEADS, SSM_HEAD_DIM, D_STATE), 0.1),
        'state_conv': nrm(ks[5], (N_SSM_LAYERS, DEC_BATCH, CONV_WIDTH - 1, CONV_CH), 1.0),
        'w_qkv': nrm(ks[6], (N_ATT_LAYERS, D_MODEL, 3 * D_MODEL), D_MODEL ** -0.5),
        'rel_bias': nrm(ks[7], (N_ATT_LAYERS, N_HEADS, 2 * REL_CLIP + 1), 0.5),
        'w_attn_out': nrm(ks[8], (N_ATT_LAYERS, D_MODEL, D_MODEL), BETA * D_MODEL ** -0.5),
        'w_ssm_in': nrm(ks[9], (N_SSM_LAYERS, D_MODEL, D_INNER + CONV_CH + SSM_HEADS), D_MODEL ** -0.5),
        'conv_w': nrm(ks[10], (N_SSM_LAYERS, CONV_WIDTH, CONV_CH), CONV_WIDTH ** -0.5),
        'conv_b': nrm(ks[11], (N_SSM_LAYERS, CONV_CH), 0.02),
        'dt_bias': dt0 + jnp.log(-jnp.expm1(-dt0)),
        'a_log': jnp.log(jax.random.uniform(ks[13], (N_SSM_LAYERS, SSM_HEADS), jnp.float32, 1.0, 16.0)),
        'd_skip': 1.0 + nrm(ks[14], (N_SSM_LAYERS, SSM_HEADS), 0.1),
        'ssm_norm_w': 1.0 + nrm(ks[15], (N_SSM_LAYERS, D_INNER), 0.05),
        'w_ssm_out': nrm(ks[16], (N_SSM_LAYERS, D_INNER, D_MODEL), BETA * D_INNER ** -0.5),
        'ln_mix_g': 1.0 + nrm(ks[17], (DEPTH, D_MODEL), 0.05),
        'ln_mix_b': nrm(ks[18], (DEPTH, D_MODEL), 0.02),
        'w_ff_up': nrm(ks[19], (DEPTH, D_MODEL, D_FF), D_MODEL ** -0.5),
        'w_ff_down': nrm(ks[20], (DEPTH, D_FF, D_MODEL), BETA * D_FF ** -0.5),
        'ln_ff_g': 1.0 + nrm(ks[21], (DEPTH, D_MODEL), 0.05),
        'ln_ff_b': nrm(ks[22], (DEPTH, D_MODEL), 0.02),
    }


def reference(x_prompt, x_sample, cache_k, cache_v, state_ssm, state_conv,
              w_qkv, rel_bias, w_attn_out, w_ssm_in, conv_w, conv_b, dt_bias, a_log,
              d_skip, ssm_norm_w, w_ssm_out, ln_mix_g, ln_mix_b, w_ff_up, w_ff_down,
              ln_ff_g, ln_ff_b):
    bp = x_prompt.shape[0]
    keep = min(WIN, x_prompt.shape[1])
    xp, xs = x_prompt, x_sample
    kp_l, vp_l, ks_l, vs_l = [], [], [], []
    hp_l, cp_l, hs_l, cs_l = [], [], [], []
    for i in range(DEPTH):
        if i % 2 == 0:
            li = i // 2
            empty = jnp.zeros((bp, 0, N_HEADS, HEAD_DIM), xp.dtype)
            mp, kp, vp = _attention_mixer(xp, empty, empty, w_qkv[li], rel_bias[li], w_attn_out[li])
            ms, ksm, vsm = _attention_mixer(xs, cache_k[li], cache_v[li], w_qkv[li], rel_bias[li], w_attn_out[li])
            kp_l.append(kp[:, -keep:]); vp_l.append(vp[:, -keep:])
            ks_l.append(ksm); vs_l.append(vsm)
        else:
            li = i // 2
            args = (w_ssm_in[li], conv_w[li], conv_b[li], dt_bias[li], a_log[li], d_skip[li],
                    ssm_norm_w[li], w_ssm_out[li])
            conv0 = jnp.zeros((bp, CONV_WIDTH - 1, CONV_CH), xp.dtype)
            h00 = jnp.zeros((bp, SSM_HEADS, SSM_HEAD_DIM, D_STATE), jnp.float32)
            mp, cp, hp = _ssm_mixer(xp, conv0, h00, *args)
            ms, csm, hsm = _ssm_mixer(xs, state_conv[li], state_ssm[li], *args)
            hp_l.append(hp); cp_l.append(cp)
            hs_l.append(hsm); cs_l.append(csm)
        xp = _layer_norm(ALPHA * xp + mp, ln_mix_g[i], ln_mix_b[i])
        xs = _layer_norm(ALPHA * xs + ms, ln_mix_g[i], ln_mix_b[i])
        xp = _layer_norm(ALPHA * xp + _mlp(xp, w_ff_up[i], w_ff_down[i]), ln_ff_g[i], ln_ff_b[i])
        xs = _layer_norm(ALPHA * xs + _mlp(xs, w_ff_up[i], w_ff_down[i]), ln_ff_g[i], ln_ff_b[i])
    k_prompt = jnp.stack(kp_l)
    v_prompt = jnp.stack(vp_l)
    ssm_prompt = jnp.stack(hp_l)
    conv_prompt = jnp.stack(cp_l)
    k_sample = jnp.stack(ks_l)
    v_sample = jnp.stack(vs_l)
    ssm_sample = jnp.stack(hs_l)
    conv_sample = jnp.stack(cs_l)
    return (xp, xs, k_prompt, v_prompt, ssm_prompt, conv_prompt,
            k_sample, v_sample, ssm_sample, conv_sample)
```

```python
import numpy as np
import concourse.bass as bass
import concourse.mybir as mybir
from concourse.bass_utils import run_bass_kernel_spmd

F32 = mybir.dt.float32
BF16 = mybir.dt.bfloat16
AF = mybir.ActivationFunctionType
ALU = mybir.AluOpType

D = 1024
NH = 16
HD = 64
WIN = 512
DFF = 4096
DIN = 2048
NG = 8
DST = 128
CCH = 4096
WIN_COLS = 6176
ALPHA = float(8 ** 0.25)
LN_EPS = 1e-5
RMS_EPS = 1e-5
NCORES = 8


class Buf:
    __slots__ = ("w", "r", "ps")

    def __init__(self, ps=False):
        self.w = None
        self.r = {}
        self.ps = ps


class Prog:
    def __init__(self, nc):
        self.nc = nc
        self.E = {"pe": nc.tensor, "act": nc.scalar, "dve": nc.vector, "pool": nc.gpsimd, "sp": nc.sync}
        self.ops = {e: [] for e in self.E}
        self.cnt = {e: 0 for e in self.E}
        self.sem = {e: nc.alloc_semaphore(name="sem_" + e) for e in self.E}
        self.known = {e: {} for e in self.E}
        self.nds = 40
        self.dsem = [nc.alloc_semaphore(name="dsem%d" % i) for i in range(self.nds)]
        self.dcnt = [0] * self.nds
        self.dpool = {"sp": list(range(0, 28)), "pool": list(range(28, 40))}
        self.dnext = {"sp": 0, "pool": 0}
        self.phase = []

    def _semh(self, k):
        return self.sem[k] if isinstance(k, str) else self.dsem[k[1]]

    def _deps(self, reads, writes):
        deps = []
        for b in reads:
            deps.append(b.w)
        for b in writes:
            deps.append(b.w)
            deps.extend(b.r.items())
        return deps

    def _waits(self, eng, deps):
        need = {}
        for t in deps:
            if t is None:
                continue
            k, v = t
            if k == eng and eng == "pe":
                continue
            if need.get(k, 0) < v:
                need[k] = v
        out = []
        kn = self.known[eng]
        for k, v in need.items():
            if kn.get(k, 0) >= v:
                continue
            kn[k] = v
            out.append((self._semh(k), v))
        return out

    def _commit(self, tick, reads, writes):
        k, v = tick
        for b in reads:
            if b.r.get(k, 0) < v:
                b.r[k] = v
        for b in writes:
            b.w = tick
            b.r = {}

    def op(self, eng, fn, reads=(), writes=(), inc=True):
        psr = [b for b in reads if b.ps]
        if psr:
            writes = list(writes) + psr
        waits = self._waits(eng, self._deps(reads, writes))
        if inc:
            self.cnt[eng] += 1
            tick = (eng, self.cnt[eng])
        else:
            tick = (eng, self.cnt[eng] + 1)
        sem = self.sem[eng]

        def run(e, waits=waits, fn=fn, sem=sem, inc=inc):
            for s, v in waits:
                e.wait_ge(s, v)
            if isinstance(fn, tuple):
                r = getattr(e, fn[0])(**fn[1])
            else:
                r = fn(e)
            if inc:
                r.then_inc(sem, 1)

        self.ops[eng].append(run)
        self._commit(tick, reads, writes)

    def dma(self, eng, out, in_, reads=(), writes=(), arena=False, slow=False):
        pl = self.dpool[eng]
        i = pl[self.dnext[eng]]
        self.dnext[eng] = (self.dnext[eng] + 1) % len(pl)
        deps = self._deps(reads, writes)
        if arena:
            deps.extend(self.phase)
        if self.dcnt[i] > 0:
            deps.append((("d", i), self.dcnt[i]))
        waits = self._waits(eng, deps)
        self.dcnt[i] += 16
        tick = (("d", i), self.dcnt[i])
        sem = self.dsem[i]

        def run(e, waits=waits, sem=sem, out=out, in_=in_, slow=slow):
            for s, v in waits:
                e.wait_ge(s, v)
            if slow:
                e.dma_start(out=out, in_=in_, allow_slow_non_contiguous=True).then_inc(sem, 16)
            else:
                e.dma_start(out=out, in_=in_).then_inc(sem, 16)

        self.ops[eng].append(run)
        self._commit(tick, reads, writes)

    def barrier(self, engs=("pe", "act", "dve")):
        comp = ("pe", "act", "dve")
        self.phase = [(c, self.cnt[c]) for c in comp if self.cnt[c] > 0]
        for e in engs:
            deps = [(c, self.cnt[c]) for c in comp if c != e and self.cnt[c] > 0]
            deps += [(("d", i), self.dcnt[i]) for i in range(self.nds) if self.dcnt[i] > 0]
            waits = self._waits(e, deps)
            if waits:
                def run(eh, waits=waits):
                    for s, v in waits:
                        eh.wait_ge(s, v)
                self.ops[e].append(run)

    def finish(self):
        deps = [(("d", i), self.dcnt[i]) for i in range(self.nds) if self.dcnt[i] > 0]
        deps += [(c, self.cnt[c]) for c in ("pe", "act", "dve", "pool") if self.cnt[c] > 0]
        waits = self._waits("sp", deps)

        def run(eh, waits=waits):
            for s, v in waits:
                eh.wait_ge(s, v)
        self.ops["sp"].append(run)

    def emit(self):
        nc = self.nc
        with nc.Block() as block:
            @block.sync
            def _(e):
                for f in self.ops["sp"]:
                    f(e)

            @block.tensor
            def _(e):
                for f in self.ops["pe"]:
                    f(e)

            @block.scalar
            def _(e):
                for f in self.ops["act"]:
                    f(e)

            @block.vector
            def _(e):
                for f in self.ops["dve"]:
                    f(e)

            @block.gpsimd
            def _(e):
                for f in self.ops["pool"]:
                    f(e)


class Cfg:
    def __init__(self, NPS, SEQ, TB, NSS, DSEQ=64, layers=4, debug=False):
        self.NPS, self.SEQ, self.TB, self.NSS, self.DSEQ = NPS, SEQ, TB, NSS, DSEQ
        self.layers = layers
        self.stage = "full"
        self.skip = set()
        assert SEQ % TB == 0 and TB % 128 == 0
        assert TB >= WIN or SEQ == TB
        assert DSEQ == 64
        self.NT = max(TB, NSS * 128)
        self.NTT = self.NT // 128
        self.KEEP = min(WIN, SEQ)


class Seg:
    def __init__(self, kind, s, t0, ntiles, tile0, nvalid, Lt, final, idx):
        self.kind, self.s, self.t0, self.ntiles, self.tile0 = kind, s, t0, ntiles, tile0
        self.nvalid, self.Lt, self.final, self.idx = nvalid, Lt, final, idx


def bc_ap(t, off, n, parts=128):
    return bass.AP(t, off, [[0, parts], [1, n]])


def build(cfg):
    nc = bass.Bass("TRN2", target_bir_lowering=False)
    P = Prog(nc)
    NPS, SEQ, TB, NSS, DSEQ = cfg.NPS, cfg.SEQ, cfg.TB, cfg.NSS, cfg.DSEQ
    NT, NTT, KEEP = cfg.NT, cfg.NTT, cfg.KEEP

    def din(name, shape):
        return nc.dram_tensor(name, list(shape), F32, kind="ExternalInput")

    def dout(name, shape):
        return nc.dram_tensor(name, list(shape), F32, kind="ExternalOutput")

    def dscr(name, shape):
        return nc.dram_tensor(name, list(shape), F32, kind="Internal")

    x_prompt = din("x_prompt", [NPS, SEQ, D])
    x_sample = din("x_sample", [NSS, DSEQ, D])
    cache_k = din("cache_k", [2, NSS, WIN, D])
    cache_v = din("cache_v", [2, NSS, WIN, D])
    state_ssm = din("state_ssm", [2, NSS, DIN, DST])
    state_conv = din("state_conv", [2, NSS, 3, CCH])
    w_qkv = din("w_qkv", [2, D, 3 * D])
    w_attn_out = din("w_attn_out", [2, D, D])
    w_ssm_in = din("w_ssm_in", [2, D, WIN_COLS])
    w_ssm_out = din("w_ssm_out", [2, DIN, D])
    w_ff_up = din("w_ff_up", [4, D, DFF])
    w_ff_down = din("w_ff_down", [4, DFF, D])
    biasT = din("biasT", [2, NH, 128, 640])
    lnp = din("lnp", [4, 4, D])
    conv_wT = din("conv_wT", [2, 128, 32, 4])
    conv_bT = din("conv_bT", [2, 128, 32])
    ssm_vec = din("ssm_vec", [2, 3, 32])
    ssm_norm_w = din("ssm_norm_w", [2, DIN])
    consts = din("consts", [6, 128, 128])
    amask = din("amask", [128, 640])

    y_prompt = dout("y_prompt", [NPS, SEQ, D])
    y_sample = dout("y_sample", [NSS, DSEQ, D])
    k_prompt = dout("k_prompt", [2, NPS, KEEP, D])
    v_prompt = dout("v_prompt", [2, NPS, KEEP, D])
    ssm_prompt = dout("ssm_prompt", [2, NPS, DIN, DST])
    conv_prompt = dout("conv_prompt", [2, NPS, 3, CCH])
    k_sample = dout("k_sample", [2, NSS, DSEQ, D])
    v_sample = dout("v_sample", [2, NSS, DSEQ, D])
    ssm_sample = dout("ssm_sample", [2, NSS, DIN, DST])
    conv_sample = dout("conv_sample", [2, NSS, 3, CCH])

    kscr = dscr("kscr", [2, WIN, D])
    vscr = dscr("vscr", [2, WIN, D])
    hscr = dscr("hscr", [2, DIN, DST])
    cscr = dscr("cscr", [2, 3, CCH])
    kscr_b = [Buf(), Buf()]
    vscr_b = [Buf(), Buf()]
    hscr_b = [Buf(), Buf()]
    cscr_b = [Buf(), Buf()]

    sb_off = [(int(nc.sbuf_base) + 63) // 64 * 64]
    sb_top = int(nc.sbuf_top)
    uid = [0]

    def salloc(shape, dtype, off=None):
        nbytes = int(np.prod(shape[1:])) * (4 if dtype == F32 else 2)
        nbytes = (nbytes + 31) // 32 * 32
        if off is None:
            off = sb_off[0]
            sb_off[0] += nbytes
        uid[0] += 1
        assert off + nbytes <= sb_top, ("SBUF overflow", off, nbytes, sb_top)
        t = nc.alloc_sbuf_tensor_at("t%d" % uid[0], list(shape), dtype, offset=off)
        return t.ap(), off + nbytes

    def fix(shape, dtype):
        return salloc(shape, dtype)[0]

    xres = fix([128, NTT, D], F32)
    xres_b = [Buf() for _ in range(NTT)]
    xT = fix([128, 8, NT], BF16)
    xT_b = [Buf() for _ in range(NTT)]
    NSLOT = 4
    wring = [fix([128, 4096], BF16) for _ in range(NSLOT)]
    wring_b = [Buf() for _ in range(NSLOT)]
    lngb = fix([128, 2, D], F32)
    lngb_b = Buf()
    cst_f = fix([128, 6, 128], F32)
    cst_b = Buf()
    ident_bf = fix([128, 128], BF16)
    ones_bf = fix([128, 128], BF16)
    cstbf_b = Buf()
    xbf = fix([128, 2, D], BF16)
    xbf_b = [Buf(), Buf()]
    lnst = fix([128, 4, 16], F32)
    lnst_b = [Buf() for _ in range(4)]
    epsc = fix([128, 4], F32)
    onec = epsc[:, 2:3]
    epsc_b = Buf()
    negm = fix([128, 4, 128], BF16)
    negm_b = Buf()
    tri_bf = fix([128, 128], BF16)
    negtri_bf = fix([128, 128], BF16)
    arena0 = sb_off[0]

    psum = nc.alloc_psum_tensor("psum", [128, 8, 512], F32).ap()
    ps_b = [Buf(ps=True) for _ in range(8)]
    ps_rr = [0]

    def bank():
        i = ps_rr[0]
        ps_rr[0] = (i + 1) % 4
        return psum[:, i, :], ps_b[i]

    pair_rr = [0]

    def bankpair():
        i = pair_rr[0]
        pair_rr[0] = (i + 1) % 2
        return psum[:, 4 + 2 * i:6 + 2 * i, :], ps_b[4 + 2 * i]

    wslot = [0]

    def load_w(src_ap, view_shape):
        i = wslot[0]
        wslot[0] = (i + 1) % NSLOT
        n = int(np.prod(view_shape[1:]))
        assert n <= 4096
        dst = wring[i][:, 0:n]
        if len(view_shape) == 3:
            dst = dst.rearrange("p (a b) -> p a b", a=view_shape[1])
        elif len(view_shape) == 4:
            dst = dst.rearrange("p (a b c) -> p a b c", a=view_shape[1], b=view_shape[2])
        P.dma("pool", dst, src_ap, writes=[wring_b[i]])
        return dst, wring_b[i]

    def mm(out, lhsT, rhs, start, stop, reads, writes, inc=None):
        if inc is None:
            inc = stop
        P.op("pe", ("matmul", dict(out=out, lhsT=lhsT, rhs=rhs, start=start, stop=stop)),
             reads=reads, writes=writes, inc=inc)

    evac_rr = [0]

    def evac_engine():
        evac_rr[0] ^= 1
        return "act" if evac_rr[0] else "dve"

    def copy_op(eng, out, in_, reads, writes, scale=None):
        if eng == "act":
            if scale is None:
                P.op("act", ("activation", dict(out=out, in_=in_, func=AF.Copy)), reads=reads, writes=writes)
            else:
                P.op("act", ("activation", dict(out=out, in_=in_, func=AF.Copy, scale=float(scale))),
                     reads=reads, writes=writes)
        else:
            if scale is None:
                P.op("dve", ("tensor_copy", dict(out=out, in_=in_)), reads=reads, writes=writes)
            else:
                P.op("dve", ("tensor_scalar", dict(out=out, in0=in_, scalar1=float(scale), scalar2=None,
                                                      op0=ALU.mult)), reads=reads, writes=writes)

    P.dma("sp", cst_f, consts.ap().rearrange("c p f -> p c f"), writes=[cst_b])
    P.op("dve", ("tensor_copy", dict(out=ident_bf, in_=cst_f[:, 0, :])), reads=[cst_b], writes=[cstbf_b])
    P.op("dve", ("tensor_copy", dict(out=ones_bf, in_=cst_f[:, 2, :])), reads=[cst_b], writes=[cstbf_b])

    def make_xT(tt, nrows=128):
        sl = tt % 2
        P.op("act", ("activation", dict(out=xbf[:, sl, :], in_=xres[:, tt, :], func=AF.Copy)),
             reads=[xres_b[tt]], writes=[xbf_b[sl]])
        pb, pbuf = bank()
        pbv = pb.bitcast(BF16)
        for kc in range(8):
            P.op("pe", ("transpose", dict(out=pbv[:, kc * 128:(kc + 1) * 128],
                                                    in_=xbf[:, sl, kc * 128:(kc + 1) * 128], identity=ident_bf)),
                 reads=[xbf_b[sl], cstbf_b], writes=[pbuf], inc=(kc == 7))
        P.op("dve", ("tensor_copy", dict(out=xT[:, :, tt * 128:(tt + 1) * 128],
                                            in_=pbv.rearrange("p (k t) -> p k t", k=8))),
             reads=[pbuf], writes=[xT_b[tt]])

    def ln_s1(tt, prescale):
        sl = tt % 4
        st = lnst[:, sl, :]
        stb = lnst_b[sl]
        if prescale:
            P.op("dve", ("tensor_scalar", dict(out=xres[:, tt, :], in0=xres[:, tt, :], scalar1=ALPHA, scalar2=None, op0=ALU.mult)),
                 reads=[xres_b[tt]], writes=[xres_b[tt]])
        P.op("dve", ("bn_stats", dict(out=st[:, 0:6], in_=xres[:, tt, 0:512])), reads=[xres_b[tt]], writes=[stb])
        P.op("dve", ("bn_stats", dict(out=st[:, 6:12], in_=xres[:, tt, 512:1024])), reads=[xres_b[tt]], writes=[stb])
        P.op("dve", ("bn_aggr", dict(out=st[:, 12:14], in_=st[:, 0:12])), reads=[stb], writes=[stb])
        P.op("act", ("activation", dict(out=st[:, 14:15], in_=st[:, 13:14], func=AF.Ln, bias=epsc[:, 0:1])),
             reads=[stb, epsc_b], writes=[stb])
        P.op("act", ("activation", dict(out=st[:, 14:15], in_=st[:, 14:15], func=AF.Exp, scale=-0.5)), reads=[stb], writes=[stb])
        P.op("dve", ("scalar_tensor_tensor", dict(out=st[:, 15:16], in0=st[:, 12:13], scalar=-1.0, in1=st[:, 14:15],
                                                     op0=ALU.mult, op1=ALU.mult)), reads=[stb], writes=[stb])

    def ln_s2(tt):
        sl = tt % 4
        st = lnst[:, sl, :]
        stb = lnst_b[sl]
        xt = xres[:, tt, :]
        P.op("act", ("activation", dict(out=xt, in_=xt, func=AF.Identity, scale=st[:, 14:15], bias=st[:, 15:16])),
             reads=[stb, xres_b[tt]], writes=[xres_b[tt]])
        P.op("dve", ("tensor_tensor", dict(out=xt, in0=xt, in1=lngb[:, 0, :], op=ALU.mult)),
             reads=[xres_b[tt], lngb_b], writes=[xres_b[tt]])
        P.op("dve", ("tensor_tensor", dict(out=xt, in0=xt, in1=lngb[:, 1, :], op=ALU.add)),
             reads=[xres_b[tt], lngb_b], writes=[xres_b[tt]])

    def ln_phase(tiles, prescale, s3):
        n = len(tiles)
        for i in range(n + 2):
            if i < n:
                ln_s1(tiles[i], prescale)
            if 0 <= i - 1 < n:
                ln_s2(tiles[i - 1])
            if 0 <= i - 2 < n:
                s3(tiles[i - 2])

    P.op("dve", ("tensor_scalar", dict(out=negm, in0=bass.AP(cst_f.tensor, cst_f[:, 1, :].offset, [list(cst_f[:, 1, :].ap[0]), [0, 4], [1, 128]]),
                                       scalar1=30000.0, scalar2=-30000.0, op0=ALU.mult, op1=ALU.add)), reads=[cst_b], writes=[negm_b])
    P.op("dve", ("tensor_copy", dict(out=tri_bf, in_=cst_f[:, 1, :])), reads=[cst_b], writes=[negm_b])
    P.op("dve", ("tensor_scalar", dict(out=negtri_bf, in0=cst_f[:, 1, :], scalar1=-1.0, scalar2=None, op0=ALU.mult)), reads=[cst_b], writes=[negm_b])
    P.op("dve", ("memset", dict(ap=epsc[:, 0:1], constant=LN_EPS)), writes=[epsc_b])
    P.op("dve", ("memset", dict(ap=epsc[:, 1:2], constant=RMS_EPS)), writes=[epsc_b])
    P.op("dve", ("memset", dict(ap=epsc[:, 2:3], constant=1.0)), writes=[epsc_b])

    def load_ln(layer, which):
        off = (layer * 4 + which * 2) * D
        P.dma("sp", lngb, bass.AP(lnp, off, [[0, 128], [D, 2], [1, D]]), writes=[lngb_b])

    HG = 2
    HGW = DFF // HG
    hT, a_end = salloc([128, HGW // 128, NT], BF16, arena0)
    hT_b = Buf()
    rtmp, a_end = salloc([128, 2, 512], F32, a_end)
    rtmp_b = [Buf(), Buf()]
    mlp_end = a_end

    def mlp(layer, nt):
        ntt = nt // 128
        tgs = [(t0, min(512, nt - t0)) for t0 in range(0, nt, 512)]
        rr = 0
        for g in range(HG):
            for wt in range(HGW // 512):
                c0 = g * HGW + wt * 512
                w, wb = load_w(w_ff_up.ap()[layer, :, c0:c0 + 512].rearrange("(k p) c -> p k c", p=128), [128, 8, 512])
                for sub in range(4):
                    for (t0, tn) in tgs:
                        pb, pbuf = bank()
                        for kc in range(8):
                            mm(pb[:, 0:tn], w[:, kc, sub * 128:(sub + 1) * 128], xT[:, kc, t0:t0 + tn],
                               kc == 0, kc == 7, reads=[wb] + xT_b[t0 // 128:(t0 + tn + 127) // 128], writes=[pbuf])
                        sl = rr % 2
                        rr += 1
                        P.op("act", ("activation", dict(out=rtmp[:, sl, 0:tn], in_=pb[:, 0:tn], func=AF.Relu)),
                             reads=[pbuf], writes=[rtmp_b[sl]])
                        fi = wt * 4 + sub
                        P.op("dve", ("tensor_tensor", dict(
                            out=hT[:, fi, t0:t0 + tn], in0=rtmp[:, sl, 0:tn], in1=rtmp[:, sl, 0:tn], op=ALU.mult)),
                            reads=[rtmp_b[sl]], writes=[hT_b])
            for ch in range(2):
                ws = []
                for kh in range(HGW // 1024):
                    r0 = g * HGW + kh * 1024
                    ws.append(load_w(w_ff_down.ap()[layer, r0:r0 + 1024, ch * 512:(ch + 1) * 512]
                                     .rearrange("(k p) c -> p k c", p=128), [128, 8, 512]))
                for tt in range(ntt):
                    pb, pbuf = bank()
                    nk = (HGW // 1024) * 8
                    for ki in range(nk):
                        w, wb = ws[ki // 8]
                        mm(pb, hT[:, ki, tt * 128:(tt + 1) * 128], w[:, ki % 8, :], ki == 0, ki == nk - 1,
                           reads=[wb, hT_b], writes=[pbuf])
                    xs_ = xres[:, tt, ch * 512:(ch + 1) * 512]
                    sc = ALPHA if g == 0 else 1.0
                    P.op("dve", ("scalar_tensor_tensor", dict(
                        out=xs_, in0=xs_, scalar=sc, in1=pb, op0=ALU.mult, op1=ALU.add)),
                        reads=[pbuf, xres_b[tt]], writes=[xres_b[tt]])

    KTW = max(WIN + TB, NSS * (WIN + 128))
    NVT = max(4 + TB // 128, NSS * 5)
    a_ = arena0
    OT, a_ = salloc([128, 8, NT], BF16, a_)
    OT_b = [Buf() for _ in range(8)]
    QT, a_ = salloc([128, 2, NT], BF16, a_)
    QT_b = [Buf(), Buf()]
    KT, a_ = salloc([128, 2, KTW], BF16, a_)
    KT_b = [Buf(), Buf()]
    VV, a_ = salloc([128, 2, NVT, 128], BF16, a_)
    VV_b = [Buf(), Buf()]
    bias2, a_ = salloc([128, 2, 2, 640], F32, a_)
    bias2_b = [Buf(), Buf()]
    stmp, a_ = salloc([128, 4, 640], F32, a_)
    stmp_b = [Buf() for _ in range(4)]
    PT, a_ = salloc([128, 4, 640], BF16, a_)
    PT_b = [Buf() for _ in range(4)]
    ktok, a_ = salloc([128, 2, 4, 128], BF16, a_)
    ktok_b = [Buf(), Buf()]
    kvout, a_ = salloc([128, 2, 4, 2, 128], F32, a_)
    kvout_b = [Buf(), Buf()]
    rsum, a_ = salloc([128, 4, 128], F32, a_)
    rsum_b = [Buf() for _ in range(4)]
    maskc, a_ = salloc([128, 640], F32, a_)
    maskc_b = Buf()
    attn_end = a_
    kscr_hb = [[Buf() for _ in range(8)] for _ in range(2)]
    vscr_hb = [[Buf() for _ in range(8)] for _ in range(2)]
    cnt_att = [0]

    def attention_layer(li, kind, segs, ntt):
        nt = ntt * 128
        P.dma("sp", maskc, amask.ap(), writes=[maskc_b], arena=True)
        for hp in range(8):
            bf = hp % 2
            i_ = wslot[0]
            wslot[0] = (i_ + 1) % NSLOT
            w = wring[i_][:, 0:3072].rearrange("p (a b c) -> p a b c", a=8, b=3)
            wb = wring_b[i_]
            for which in range(3):
                P.dma("pool", w[:, :, which, :],
                      bass.AP(w_qkv, li * D * 3 * D + which * D + hp * 128, [[3 * D, 128], [128 * 3 * D, 8], [1, 128]]),
                      writes=[wb])
            P.dma("sp", bias2[:, bf, :, :], biasT.ap()[li, 2 * hp:2 * hp + 2, :, :].rearrange("h p f -> p h f"),
                  writes=[bias2_b[bf]], arena=True)
            for hh in range(2):
                P.op("dve", ("tensor_tensor", dict(out=bias2[:, bf, hh, :], in0=bias2[:, bf, hh, :], in1=maskc,
                                                             op=ALU.add)), reads=[bias2_b[bf], maskc_b], writes=[bias2_b[bf]])
            for sg in segs:
                if sg.Lt == 0 or "tail" in cfg.skip:
                    continue
                if sg.kind == "s":
                    ksrc, vsrc = cache_k.ap()[li, sg.s], cache_v.ap()[li, sg.s]
                    kb_, vb_ = [], []
                else:
                    ksrc, vsrc = kscr.ap()[li], vscr.ap()[li]
                    kb_, vb_ = [kscr_hb[li][hp]], [vscr_hb[li][hp]]
                koff = sg.idx * (WIN + 128) if sg.kind == "s" else 0
                vt0 = sg.idx * 5 if sg.kind == "s" else 0
                c = cnt_att[0] % 2
                cnt_att[0] += 1
                P.dma("pool", ktok[:, c, :, :], ksrc[:, hp * 128:(hp + 1) * 128].rearrange("(k p) f -> p k f", p=128),
                      reads=kb_, writes=[ktok_b[c]], arena=True)
                pb, pbuf = bank()
                pbv = pb.bitcast(BF16)
                for k4 in range(4):
                    P.op("pe", ("transpose", dict(out=pbv[:, k4 * 128:(k4 + 1) * 128], in_=ktok[:, c, k4, :],
                                                                  identity=ident_bf)),
                         reads=[ktok_b[c], cstbf_b], writes=[pbuf], inc=(k4 == 3))
                copy_op(evac_engine(), KT[:, bf, koff:koff + WIN], pbv[:, 0:WIN], [pbuf], [KT_b[bf]])
                P.dma("pool", VV[:, bf, vt0:vt0 + 4, :], vsrc[:, hp * 128:(hp + 1) * 128].rearrange("(k p) f -> p k f", p=128),
                      reads=vb_, writes=[VV_b[bf]], arena=True)
            for t0 in range(0, nt if "qk" not in cfg.skip else 0, 512):
                tn = min(512, nt - t0)
                xb_ = xT_b[t0 // 128:(t0 + tn) // 128]
                for which in range(2):
                    pb, pbuf = bank()
                    for kc in range(8):
                        mm(pb[:, 0:tn], w[:, kc, which, :], xT[:, kc, t0:t0 + tn], kc == 0, kc == 7, reads=[wb] + xb_, writes=[pbuf])
                    if which == 0:
                        copy_op(evac_engine(), QT[:, bf, t0:t0 + tn], pb[:, 0:tn], [pbuf], [QT_b[bf]], scale=HD ** -0.5)
                    else:
                        if kind == "p":
                            sg = segs[0]
                            dst = KT[:, bf, sg.Lt + t0: sg.Lt + t0 + tn]
                            src = pb[:, 0:tn]
                        else:
                            n_sg = tn // 128
                            s0 = t0 // 128
                            dst = KT[:, bf, s0 * (WIN + 128): (s0 + n_sg) * (WIN + 128)].rearrange(
                                "p (s c) -> p s c", c=WIN + 128)[:, :, WIN:WIN + 128]
                            src = pb[:, 0:tn].rearrange("p (s c) -> p s c", c=128)
                        copy_op(evac_engine(), dst, src, [pbuf], [KT_b[bf]])
            for sg in (segs if "v" not in cfg.skip else []):
                n_out = min(4, sg.ntiles)
                for ti in range(sg.ntiles):
                    tt = sg.tile0 + ti
                    vt = (sg.idx * 5 + 4) if sg.kind == "s" else (sg.Lt // 128 + ti)
                    is_out = ti >= sg.ntiles - n_out
                    oi = ti - (sg.ntiles - n_out) if sg.kind == "p" else sg.idx
                    pb, pbuf = bank()
                    for which in ((1, 2) if is_out else (2,)):
                        for kc in range(8):
                            mm(pb[:, (which - 1) * 128:which * 128], xT[:, kc, tt * 128:(tt + 1) * 128], w[:, kc, which, :],
                               kc == 0, kc == 7, reads=[wb, xT_b[tt]], writes=[pbuf])
                    copy_op("act", VV[:, bf, vt, :], pb[:, 128:256], [pbuf], [VV_b[bf]])
                    if is_out and "kvcopy" not in cfg.skip:
                        P.op("dve", ("tensor_copy", dict(out=kvout[:, bf, oi, :, :],
                                                                          in_=pb[:, 0:256].rearrange("p (a b) -> p a b", a=2))),
                             reads=[pbuf], writes=[kvout_b[bf]])
                if sg.kind == "p" and "vdma" not in cfg.skip:
                    if sg.final:
                        kd, vd = k_prompt.ap()[li, sg.s], v_prompt.ap()[li, sg.s]
                        kw_, vw_ = [], []
                    else:
                        kd, vd = kscr.ap()[li], vscr.ap()[li]
                        kw_, vw_ = [kscr_hb[li][hp]], [vscr_hb[li][hp]]
                    if sg.ntiles >= 4:
                        P.dma("sp", kd[:, hp * 128:(hp + 1) * 128].rearrange("(k p) f -> p k f", p=128), kvout[:, bf, :, 0, :],
                              reads=[kvout_b[bf]], writes=kw_)
                        P.dma("sp", vd[:, hp * 128:(hp + 1) * 128].rearrange("(k p) f -> p k f", p=128), kvout[:, bf, :, 1, :],
                              reads=[kvout_b[bf]], writes=vw_)
                    else:
                        n_ = sg.ntiles
                        P.dma("sp", kd[0:n_ * 128, hp * 128:(hp + 1) * 128].rearrange("(k p) f -> p k f", p=128),
                              kvout[:, bf, 0:n_, 0, :], reads=[kvout_b[bf]], writes=kw_)
                        P.dma("sp", vd[0:n_ * 128, hp * 128:(hp + 1) * 128].rearrange("(k p) f -> p k f", p=128),
                              kvout[:, bf, 0:n_, 1, :], reads=[kvout_b[bf]], writes=vw_)
            if kind == "s" and "v" not in cfg.skip and "vdma" not in cfg.skip:
                for sg in segs:
                    P.dma("sp", k_sample.ap()[li, sg.s, :, hp * 128:(hp + 1) * 128], kvout[0:DSEQ, bf, sg.idx, 0, :],
                          reads=[kvout_b[bf]])
                    P.dma("sp", v_sample.ap()[li, sg.s, :, hp * 128:(hp + 1) * 128], kvout[0:DSEQ, bf, sg.idx, 1, :],
                          reads=[kvout_b[bf]])
            for sg in (segs if "band" not in cfg.skip else []):
                koff = sg.idx * (WIN + 128) if sg.kind == "s" else 0
                vt0 = sg.idx * 5 if sg.kind == "s" else 0
                st_all = {}

                def att_S(ti, sg=sg, koff=koff, st_all=st_all):
                    tt = sg.tile0 + ti
                    qc0 = tt * 128
                    js = [j for j in range(5) if sg.Lt + 128 * ti - 512 + 128 * j >= 0]
                    heads = []
                    for hh in range(2):
                        hr = slice(hh * 64, (hh + 1) * 64)
                        sc, scb = psum[:, 4 + 2 * hh:6 + 2 * hh, :], ps_b[4 + 2 * hh]
                        scf = sc.rearrange("p a b -> p (a b)")
                        c = (ti % 2) * 2 + hh
                        nks = {}
                        for j in js:
                            kp = sg.Lt + 128 * ti - 512 + 128 * j
                            nk = 128 if j < 4 else sg.nvalid
                            nks[j] = nk
                            mm(scf[0:nk, j * 128:(j + 1) * 128], KT[hr, bf, koff + kp:koff + kp + nk], QT[hr, bf, qc0:qc0 + 128],
                               True, True, reads=[KT_b[bf], QT_b[bf]], writes=[scb], inc=(j == js[-1]))
                        heads.append((hr, scf, scb, c, nks))
                    st_all[ti] = dict(js=js, heads=heads, qc0=qc0)

                def att_AE(ti, sg=sg, st_all=st_all):
                    st = st_all[ti]
                    js = st["js"]
                    j0 = js[0]
                    for hh in range(2):
                        hr, scf, scb, c, nks = st["heads"][hh]
                        groups = []
                        if sg.nvalid == 128:
                            groups.append((j0, 5, 128))
                        else:
                            if j0 < 4:
                                groups.append((j0, 4, 128))
                            groups.append((4, 5, sg.nvalid))
                        for (ja, jb, nk) in groups:
                            cs = slice(ja * 128, jb * 128)
                            P.op("dve", ("tensor_tensor", dict(out=stmp[0:nk, c, cs], in0=scf[0:nk, cs], in1=bias2[0:nk, bf, hh, cs], op=ALU.add)),
                                 reads=[scb, bias2_b[bf]], writes=[stmp_b[c]])
                            P.op("act", ("activation", dict(out=PT[0:nk, c, cs], in_=stmp[0:nk, c, cs], func=AF.Exp)),
                                 reads=[stmp_b[c]], writes=[PT_b[c]])

                def att_PV(ti, sg=sg, vt0=vt0, st_all=st_all):
                    st = st_all[ti]
                    js = st["js"]
                    obs = []
                    for hh in range(2):
                        hr, scf, scb, c, nks = st["heads"][hh]
                        ob, obuf = bank()
                        obs.append((ob, obuf))
                        for idx_, j in enumerate(js):
                            kp = sg.Lt + 128 * ti - 512 + 128 * j
                            nk = nks[j]
                            vt = vt0 + kp // 128
                            mm(ob[:, 0:128], VV[0:nk, bf, vt, :], PT[0:nk, c, j * 128:(j + 1) * 128],
                               idx_ == 0, idx_ == len(js) - 1, reads=[VV_b[bf], PT_b[c]], writes=[obuf], inc=False)
                        for idx_, j in enumerate(js):
                            nk = nks[j]
                            mm(ob[:, 128:256], ones_bf[0:nk, :], PT[0:nk, c, j * 128:(j + 1) * 128],
                               idx_ == 0, idx_ == len(js) - 1, reads=[cstbf_b, PT_b[c]], writes=[obuf], inc=(idx_ == len(js) - 1))
                    st["obs"] = obs

                def att_NORM(ti, st_all=st_all):
                    st = st_all[ti]
                    qc0 = st["qc0"]
                    for hh in range(2):
                        hr, scf, scb, c, nks = st["heads"][hh]
                        ob, obuf = st["obs"][hh]
                        P.op("act", ("activation", dict(out=rsum[hr, c, :], in_=ob[hr, 128:256], func=AF.Ln)), reads=[obuf], writes=[rsum_b[c]])
                        P.op("act", ("activation", dict(out=rsum[hr, c, :], in_=rsum[hr, c, :], func=AF.Exp, scale=-1.0)),
                             reads=[rsum_b[c]], writes=[rsum_b[c]])
                        P.op("dve", ("tensor_tensor", dict(out=OT[hr, hp, qc0:qc0 + 128], in0=ob[hr, 0:128], in1=rsum[hr, c, :], op=ALU.mult)),
                             reads=[obuf, rsum_b[c]], writes=[OT_b[hp]])
                    del st_all[ti]

                att_S(0)
                att_AE(0)
                for ti in range(sg.ntiles):
                    if ti + 1 < sg.ntiles:
                        att_S(ti + 1)
                    att_PV(ti)
                    if ti + 1 < sg.ntiles:
                        att_AE(ti + 1)
                    att_NORM(ti)
        for ch in range(2 if "oproj" not in cfg.skip else 0):
            w, wb = load_w(w_attn_out.ap()[li, :, ch * 512:(ch + 1) * 512].rearrange("(k p) c -> p k c", p=128), [128, 8, 512])
            for tt in range(ntt):
                pb, pbuf = bank()
                for kc in range(8):
                    mm(pb, OT[:, kc, tt * 128:(tt + 1) * 128], w[:, kc, :], kc == 0, kc == 7, reads=[wb, OT_b[kc]], writes=[pbuf])
                xs_ = xres[:, tt, ch * 512:(ch + 1) * 512]
                P.op("dve", ("scalar_tensor_tensor", dict(out=xs_, in0=xs_, scalar=ALPHA, in1=pb,
                                                                           op0=ALU.mult, op1=ALU.add)),
                     reads=[pbuf, xres_b[tt]], writes=[xres_b[tt]])

    def bc_mid(a, n):
        return bass.AP(a.tensor, a.offset, [list(a.ap[0]), [0, n]] + [list(x) for x in a.ap[1:]])

    def bc_last(a, n):
        return bass.AP(a.tensor, a.offset, [list(x) for x in a.ap] + [[0, n]])

    PWT = max(3 + TB, NSS * 131)
    a_ = arena0
    sv = {}
    for nm in ("dt", "dA", "cum", "ncum", "dw", "ecl", "ecum"):
        sv[nm], a_ = salloc([128, NTT, 32], F32, a_)
    sv_b = {nm: [Buf() for _ in range(NTT)] for nm in sv}
    mdw, a_ = salloc([128, NTT, 3, 32], F32, a_)
    X3, a_ = salloc([128, 2, 3, 256], BF16, a_)
    X3_b = [Buf(), Buf()]
    dAh, a_ = salloc([128, NTT, 32], BF16, a_)
    dAl, a_ = salloc([128, NTT, 32], BF16, a_)
    stmp32, a_ = salloc([128, 2, 32], F32, a_)
    stmp32_b = Buf()
    cw, a_ = salloc([128, 32, 4], F32, a_)
    cb, a_ = salloc([128, 32], F32, a_)
    vecb, a_ = salloc([128, 3, 32], F32, a_)
    abc, a_ = salloc([128, 32], F32, a_)
    par_b = Buf()
    zs, a_ = salloc([128, NTT, 256], F32, a_)
    zs_b = [Buf() for _ in range(NTT)]
    NSEGM = max(1, NSS)
    pre, a_ = salloc([128, 4, PWT], BF16, a_)
    pre_b = [Buf() for _ in range(4)]
    dg, a_ = salloc([128, 16, 128], BF16, a_)
    dg_b = Buf()
    hal, a_ = salloc([128, 4, NSEGM, 3], F32, a_)
    hal_b = [Buf() for _ in range(4)]
    hout3, a_ = salloc([128, 4, NSEGM, 3], F32, a_)
    hout3_b = [Buf() for _ in range(4)]
    xcT, a_ = salloc([128, 4, NT], BF16, a_)
    xcT_b = [Buf() for _ in range(4)]
    tokm, a_ = salloc([128, NTT, 384], BF16, a_)
    tokm_b = [Buf() for _ in range(NTT)]
    ynT, a_ = salloc([128, 4, NT], BF16, a_)
    ynT_b = [Buf() for _ in range(NTT)]
    nwg, a_ = salloc([128, 256], F32, a_)
    nwg_b = Buf()
    cbm, a_ = salloc([128, 2, 128], F32, a_)
    cbm_b = [Buf(), Buf()]
    Rg, a_ = salloc([128, 2, 512], F32, a_)
    Rg_b = [Buf(), Buf()]
    eh, a_ = salloc([128, 2, 512], F32, a_)
    eh_b = [Buf(), Buf()]
    MT, a_ = salloc([128, 2, 512], BF16, a_)
    MT_b = [Buf(), Buf()]
    Xdt, a_ = salloc([128, 2, 256], BF16, a_)
    Xdt_b = [Buf(), Buf()]
    Xw, a_ = salloc([128, 2, 256], BF16, a_)
    Xw_b = [Buf(), Buf()]
    xsD, a_ = salloc([128, 2, 256], BF16, a_)
    xsD_b = [Buf(), Buf()]
    Ysb, a_ = salloc([128, 2, 256], F32, a_)
    Ysb_b = [Buf(), Buf()]
    junk, a_ = salloc([128, 2, 256], F32, a_)
    junk_b = [Buf(), Buf()]
    yn, a_ = salloc([128, 2, 256], BF16, a_)
    yn_b = [Buf(), Buf()]
    gst, a_ = salloc([128, 2, 4], F32, a_)
    gst_b = [Buf(), Buf()]
    Hs, a_ = salloc([128, 256], F32, a_)
    Hs_b = Buf()
    Hb, a_ = salloc([128, 256], BF16, a_)
    Hb_b = Buf()
    hld, a_ = salloc([128, max(1, NSS), 2, 128], F32, a_)
    hld_b = [Buf() for _ in range(max(1, NSS))]
    hout, a_ = salloc([128, 2, 128], F32, a_)
    hout_b = Buf()
    ssm_end = a_
    hscr_gb = [[Buf() for _ in range(8)] for _ in range(2)]
    cscr_gb = [[Buf() for _ in range(8)] for _ in range(2)]
    tri_f = cst_f[:, 1, :]
    ones_f = cst_f[:, 2, :]
    ident_f = cst_f[:, 0, :]

    def ssm_layer(li, kind, segs, ntt):
        nt = ntt * 128
        P.dma("sp", cw, conv_wT.ap()[li], writes=[par_b], arena=True)
        P.dma("sp", cb, conv_bT.ap()[li], writes=[par_b], arena=True)
        P.dma("sp", vecb, bass.AP(ssm_vec, li * 96, [[0, 128], [1, 96]]), writes=[par_b], arena=True)
        P.op("act", ("activation", dict(out=abc, in_=vecb[:, 1, :], func=AF.Exp)), reads=[par_b], writes=[par_b])
        P.op("dve", ("tensor_scalar", dict(out=abc, in0=abc, scalar1=-1.0, scalar2=None, op0=ALU.mult)), reads=[par_b], writes=[par_b])
        wdt, wdt_b = load_w(w_ssm_in.ap()[li, :, 6144:6176].rearrange("(k p) c -> p k c", p=128), [128, 8, 32])
        tile_nv = {}
        for sg in segs:
            for ti in range(sg.ntiles):
                tile_nv[sg.tile0 + ti] = sg.nvalid
        for tt in range(ntt):
            nv = tile_nv[tt]
            pb, pbuf = bank()
            for kc in range(8):
                mm(pb[:, 0:32], xT[:, kc, tt * 128:(tt + 1) * 128], wdt[:, kc, :], kc == 0, kc == 7, reads=[wdt_b, xT_b[tt]], writes=[pbuf])
            dt_, dA_, cum_, ncum_, dw_, ecl_ = (sv[k][:, tt, :] for k in ("dt", "dA", "cum", "ncum", "dw", "ecl"))
            P.op("dve", ("tensor_tensor", dict(out=dt_, in0=pb[:, 0:32], in1=vecb[:, 0, :], op=ALU.add)),
                 reads=[pbuf, par_b], writes=[sv_b["dt"][tt]])
            P.op("act", ("activation", dict(out=dt_, in_=dt_, func=AF.Exp)), reads=[sv_b["dt"][tt]], writes=[sv_b["dt"][tt]])
            P.op("act", ("activation", dict(out=dt_, in_=dt_, func=AF.Ln, bias=onec[:, 0:1])), reads=[sv_b["dt"][tt], epsc_b], writes=[sv_b["dt"][tt]])
            P.op("dve", ("tensor_tensor", dict(out=dA_, in0=dt_, in1=abc, op=ALU.mult)), reads=[sv_b["dt"][tt], par_b], writes=[sv_b["dA"][tt]])
            P.op("dve", ("tensor_copy", dict(out=dAh[:, tt, :], in_=dA_)), reads=[sv_b["dA"][tt]], writes=[sv_b["dA"][tt]])
            P.op("dve", ("tensor_tensor", dict(out=stmp32[:, 0, :], in0=dA_, in1=dAh[:, tt, :], op=ALU.subtract)),
                 reads=[sv_b["dA"][tt]], writes=[stmp32_b])
            P.op("dve", ("tensor_copy", dict(out=dAl[:, tt, :], in_=stmp32[:, 0, :])), reads=[stmp32_b], writes=[sv_b["dA"][tt]])
            pb2, pbuf2 = bank()
            mm(pb2[:, 0:32], tri_f, dA_, True, True, reads=[cst_b, sv_b["dA"][tt]], writes=[pbuf2], inc=False)
            mm(pb2[:, 32:64], ones_f[0:nv, :], sv["dA"][0:nv, tt, :], True, True, reads=[cst_b, sv_b["dA"][tt]], writes=[pbuf2])
            P.op("act", ("activation", dict(out=cum_, in_=pb2[:, 0:32], func=AF.Copy)), reads=[pbuf2], writes=[sv_b["cum"][tt]])
            P.op("act", ("activation", dict(out=ncum_, in_=pb2[:, 0:32], func=AF.Copy, scale=-1.0)), reads=[pbuf2], writes=[sv_b["ncum"][tt]])
            P.op("act", ("activation", dict(out=ecl_, in_=pb2[:, 32:64], func=AF.Exp)), reads=[pbuf2], writes=[sv_b["ecl"][tt]])
            P.op("act", ("activation", dict(out=sv["ecum"][:, tt, :], in_=pb2[:, 0:32], func=AF.Exp)), reads=[pbuf2], writes=[sv_b["ecum"][tt]])
            P.op("dve", ("tensor_tensor", dict(out=dw_, in0=pb2[:, 32:64], in1=cum_, op=ALU.subtract)),
                 reads=[pbuf2, sv_b["cum"][tt]], writes=[sv_b["dw"][tt]])
            P.op("dve", ("tensor_scalar", dict(out=dw_, in0=dw_, scalar1=0.0, scalar2=None, op0=ALU.min)), reads=[sv_b["dw"][tt]], writes=[sv_b["dw"][tt]])
            P.op("act", ("activation", dict(out=dw_, in_=dw_, func=AF.Exp)), reads=[sv_b["dw"][tt]], writes=[sv_b["dw"][tt]])
            P.op("dve", ("tensor_tensor", dict(out=dw_, in0=dw_, in1=dt_, op=ALU.mult)), reads=[sv_b["dw"][tt], sv_b["dt"][tt]], writes=[sv_b["dw"][tt]])
            P.op("dve", ("tensor_copy", dict(out=mdw[:, tt, 0, :], in_=dt_)), reads=[sv_b["dt"][tt]], writes=[sv_b["dw"][tt]])
            P.op("dve", ("tensor_copy", dict(out=mdw[:, tt, 1, :], in_=dw_)), reads=[sv_b["dw"][tt]], writes=[sv_b["dw"][tt]])
            P.op("dve", ("tensor_copy", dict(out=mdw[:, tt, 2, :], in_=vecb[:, 2, :])), reads=[par_b], writes=[sv_b["dw"][tt]])

        for g in range(NG):
            i_ = wslot[0]
            wslot[0] = (i_ + 1) % NSLOT
            wa = wring[i_][:, 0:4096].rearrange("p (a b c) -> p a b c", a=8, b=2)
            wa_b = wring_b[i_]
            for which, c0 in ((0, g * 256), (1, DIN + g * 256)):
                P.dma("pool", wa[:, :, which, :], bass.AP(w_ssm_in, li * D * WIN_COLS + c0, [[WIN_COLS, 128], [128 * WIN_COLS, 8], [1, 256]]),
                      writes=[wa_b])
            i_ = wslot[0]
            wslot[0] = (i_ + 1) % NSLOT
            wbc = wring[i_][:, 0:2048].rearrange("p (a b c) -> p a b c", a=8, b=2)
            wbc_b = wring_b[i_]
            for which, c0 in ((0, 2 * DIN + g * 128), (1, 2 * DIN + 1024 + g * 128)):
                P.dma("pool", wbc[:, :, which, :], bass.AP(w_ssm_in, li * D * WIN_COLS + c0, [[WIN_COLS, 128], [128 * WIN_COLS, 8], [1, 128]]),
                      writes=[wbc_b])
            P.dma("sp", nwg, bass.AP(ssm_norm_w, li * DIN + g * 256, [[0, 128], [1, 256]]), writes=[nwg_b], arena=True)
            gfts = [2 * g, 2 * g + 1, 16 + g, 24 + g]
            ch0s = [g * 256, g * 256 + 128, DIN + g * 128, DIN + 1024 + g * 128]

            for sg in segs:
                if sg.Lt == 0:
                    continue
                if sg.kind == "s":
                    h_src, h_rb = state_ssm.ap()[li, sg.s], []
                else:
                    h_src, h_rb = hscr.ap()[li], [hscr_gb[li][g]]
                P.dma("sp", hld[:, sg.idx, :, :], h_src[g * 256:(g + 1) * 256, :].rearrange("(k p) n -> p k n", p=128),
                      reads=h_rb, writes=[hld_b[sg.idx]], arena=True)
            for tt in range(ntt):
                pb, pbuf = bank()
                for kc in range(8):
                    mm(pb[:, 0:256], xT[:, kc, tt * 128:(tt + 1) * 128], wa[:, kc, 0, :], kc == 0, kc == 7, reads=[wa_b, xT_b[tt]], writes=[pbuf])
                P.op("act", ("activation", dict(out=zs[:, tt, :], in_=pb[:, 0:256], func=AF.Silu)), reads=[pbuf], writes=[zs_b[tt]])
            for fi in range(4):
                for k in range(4):
                    P.op("dve", ("tensor_scalar", dict(out=dg[:, fi * 4 + k, :], in0=ident_bf, scalar1=cw[:, gfts[fi], k:k + 1], scalar2=None,
                                                       op0=ALU.mult)), reads=[cstbf_b, par_b], writes=[dg_b])
            for sg in segs:
                off = sg.idx * 131 if sg.kind == "s" else 0
                for fi in range(4):
                    if sg.Lt == 0:
                        P.op("dve", ("memset", dict(ap=pre[:, fi, off:off + 3], constant=0.0)), writes=[pre_b[fi]])
                    else:
                        if sg.kind == "s":
                            src_t, base, rb = state_conv, (li * NSS + sg.s) * 3 * CCH, []
                        else:
                            src_t, base, rb = cscr, li * 3 * CCH, [cscr_gb[li][g]]
                        P.dma("sp", hal[:, fi, sg.idx, :], bass.AP(src_t, base + ch0s[fi], [[1, 128], [CCH, 3]]),
                              reads=rb, writes=[hal_b[fi]], arena=True, slow=True)
                        P.op("dve", ("tensor_copy", dict(out=pre[:, fi, off:off + 3], in_=hal[:, fi, sg.idx, :])),
                             reads=[hal_b[fi]], writes=[pre_b[fi]])
            for fi in range(4):
                for t0 in range(0, nt, 512):
                    tn = min(512, nt - t0)
                    pb, pbuf = bank()
                    for kc in range(8):
                        lw = wa[:, kc, 1, fi * 128:(fi + 1) * 128] if fi < 2 else wbc[:, kc, fi - 2, :]
                        mm(pb[:, 0:tn], lw, xT[:, kc, t0:t0 + tn], kc == 0, kc == 7,
                           reads=[wa_b if fi < 2 else wbc_b] + xT_b[t0 // 128:(t0 + tn) // 128], writes=[pbuf])
                    if kind == "p":
                        dst, src = pre[:, fi, 3 + t0:3 + t0 + tn], pb[:, 0:tn]
                        if t0 + tn == nt:
                            P.op("dve", ("tensor_copy", dict(out=hout3[:, fi, 0, :], in_=pb[:, tn - 3:tn])), reads=[pbuf], writes=[hout3_b[fi]])
                    else:
                        n_sg, s0 = tn // 128, t0 // 128
                        dst = pre[:, fi, s0 * 131:(s0 + n_sg) * 131].rearrange("p (s c) -> p s c", c=131)[:, :, 3:131]
                        src = pb[:, 0:tn].rearrange("p (s c) -> p s c", c=128)
                        P.op("dve", ("tensor_copy", dict(out=hout3[:, fi, s0:s0 + n_sg, :], in_=src[:, :, DSEQ - 3:DSEQ])),
                             reads=[pbuf], writes=[hout3_b[fi]])
                    copy_op("act", dst, src, [pbuf], [pre_b[fi]])
            for sg in segs:
                if sg.kind == "s":
                    dst_t, base, wbf = conv_sample, (li * NSS + sg.s) * 3 * CCH, []
                elif sg.final:
                    dst_t, base, wbf = conv_prompt, (li * NPS + sg.s) * 3 * CCH, []
                else:
                    dst_t, base, wbf = cscr, li * 3 * CCH, [cscr_gb[li][g]]
                for fi in range(4):
                    P.dma("sp", bass.AP(dst_t, base + ch0s[fi], [[1, 128], [CCH, 3]]), hout3[:, fi, sg.idx, :],
                          reads=[hout3_b[fi]], writes=wbf, slow=True)
            for fi in range(4):
                gf = gfts[fi]
                for t0 in range(0, nt, 512):
                    tn = min(512, nt - t0)
                    pb, pbuf = bank()
                    for k in range(4):
                        if kind == "p":
                            rhs_ = pre[:, fi, t0 + k:t0 + k + tn]
                            out_ = pb[:, 0:tn]
                        else:
                            n_sg, s0 = tn // 128, t0 // 128
                            rhs_ = pre[:, fi, s0 * 131:(s0 + n_sg) * 131].rearrange("p (s c) -> p s c", c=131)[:, :, k:k + 128]
                            out_ = pb[:, 0:tn].rearrange("p (s c) -> p s c", c=128)
                        mm(out_, dg[:, fi * 4 + k, :], rhs_, k == 0, k == 3, reads=[dg_b, pre_b[fi]], writes=[pbuf])
                    P.op("act", ("activation", dict(out=xcT[:, fi, t0:t0 + tn], in_=pb[:, 0:tn], func=AF.Silu, bias=cb[:, gf:gf + 1])),
                         reads=[pbuf, par_b], writes=[xcT_b[fi]])
            for tt in range(ntt):
                pb, pbuf = bank()
                pbv = pb.bitcast(BF16)
                for fi in range(3):
                    P.op("pe", ("transpose", dict(out=pbv[:, fi * 128:(fi + 1) * 128], in_=xcT[:, fi, tt * 128:(tt + 1) * 128], identity=ident_bf)),
                         reads=[xcT_b[fi], cstbf_b], writes=[pbuf], inc=(fi == 2))
                copy_op(evac_engine(), tokm[:, tt, :], pbv[:, 0:384], [pbuf], [tokm_b[tt]])
            for sg in segs:
                if sg.kind == "s":
                    h_src, h_rb = state_ssm.ap()[li, sg.s], []
                else:
                    h_src, h_rb = hscr.ap()[li], [hscr_gb[li][g]]
                if sg.Lt == 0:
                    P.op("dve", ("memset", dict(ap=Hs, constant=0.0)), writes=[Hs_b])
                else:
                    pb, pbuf = bank()
                    for k2 in range(2):
                        P.op("pe", ("transpose", dict(out=pb[:, k2 * 128:(k2 + 1) * 128], in_=hld[:, sg.idx, k2, :], identity=ident_f)),
                             reads=[hld_b[sg.idx], cst_b], writes=[pbuf], inc=(k2 == 1))
                    P.op("dve", ("tensor_copy", dict(out=Hs, in_=pb[:, 0:256])), reads=[pbuf], writes=[Hs_b])
                P.op("act", ("activation", dict(out=Hb, in_=Hs, func=AF.Copy)), reads=[Hs_b], writes=[Hb_b])
                h4 = slice(4 * g, 4 * g + 4)

                def ssd_iter(t1, ta, tb, tc, sg=sg, h4=h4):
                    if ta is not None:
                        tt = sg.tile0 + ta
                        nv = sg.nvalid
                        p_ = ta % 2
                        cs = slice(tt * 128, (tt + 1) * 128)
                        xs3 = tokm[:, tt, 0:256].rearrange("p (h d) -> p h d", h=4)
                        eh3 = eh[:, p_, :].rearrange("p (h t) -> p h t", h=4)
                        pb, pbuf = psum[:, 0 + p_, :], ps_b[0 + p_]
                        pc, pcbuf = psum[:, 2 + p_, :], ps_b[2 + p_]
                        py, pybuf = psum[:, 4 + p_, :], ps_b[4 + p_]
                    if tb is not None:
                        ttb = sg.tile0 + tb
                        q_ = tb % 2
                        pyb, pybbuf = psum[:, 4 + q_, :], ps_b[4 + q_]
                    if tc is not None:
                        ttc = sg.tile0 + tc
                        c_ = tc % 2
                        csc = slice(ttc * 128, (ttc + 1) * 128)
                        pt, ptbuf = psum[:, 6, :], ps_b[6]
                        ptv = pt.bitcast(BF16)
                        for k2 in range(2):
                            P.op("pe", ("transpose", dict(out=ptv[:, k2 * 128:(k2 + 1) * 128], in_=yn[:, c_, k2 * 128:(k2 + 1) * 128], identity=ident_bf)),
                                 reads=[yn_b[c_], cstbf_b], writes=[ptbuf], inc=(k2 == 1))
                    if t1 is not None:
                        tt1 = sg.tile0 + t1
                        r_ = t1 % 2
                        cs1 = slice(tt1 * 128, (tt1 + 1) * 128)
                        mm(psum[:, 0 + r_, 0:128], xcT[:, 2, cs1], xcT[:, 3, cs1], True, True, reads=[xcT_b[2], xcT_b[3]], writes=[ps_b[0 + r_]])
                        pcb, pcbbuf = psum[:, 2 + r_, :], ps_b[2 + r_]
                        for h in range(4):
                            hc = 4 * g + h
                            for pi, dX in enumerate((dAh, dAl)):
                                P.op("pe", ("matmul", dict(out=pcb[:, h * 128:(h + 1) * 128], lhsT=bc_last(dX[:, tt1, hc:hc + 1], 128).rearrange("p a b -> p (a b)") if False else bass.AP(dX.tensor, dX[:, tt1, hc:hc + 1].offset, [list(dX[:, tt1, hc:hc + 1].ap[0]), [0, 128]]),
                                                           rhs=tri_bf, start=(pi == 0 and h == 0), stop=False, skip_group_check=True)),
                                     reads=[sv_b["dA"][tt1], negm_b], writes=[pcbbuf], inc=False)
                        pcb3 = pcb.rearrange("p (h t) -> p h t", h=4)
                        for dX in (dAh, dAl):
                            P.op("pe", ("matmul", dict(out=pcb3, lhsT=negtri_bf, rhs=bc_last(dX[:, tt1, h4], 128), start=False, stop=False,
                                                       skip_group_check=True)),
                                 reads=[sv_b["dA"][tt1], negm_b], writes=[pcbbuf], inc=False)
                        P.op("pe", ("matmul", dict(out=pcb, lhsT=ident_bf, rhs=negm.rearrange("p h t -> p (h t)"), start=False, stop=True,
                                                   skip_group_check=True)),
                             reads=[cstbf_b, negm_b], writes=[pcbbuf])
                    if tb is not None:
                        for h in range(4):
                            P.op("act", ("activation", dict(out=junk[:, q_, h * 64:(h + 1) * 64], in_=pyb[:, 256 + h * 64:256 + (h + 1) * 64],
                                                            func=AF.Identity, scale=sv["ecum"][:, ttb, 4 * g + h:4 * g + h + 1])),
                                 reads=[pybbuf, sv_b["ecum"][ttb]], writes=[junk_b[q_]])
                    if ta is not None:
                        P.op("act", ("activation", dict(out=eh[:, p_, :], in_=pc, func=AF.Exp)), reads=[pcbuf], writes=[eh_b[p_]])
                        if ta > 0:
                            P.op("act", ("activation", dict(out=Hb, in_=Hs, func=AF.Copy)), reads=[Hs_b], writes=[Hb_b])
                        xs_ap = tokm[:, tt, 0:256]
                        in0_ = bass.AP(xs_ap.tensor, xs_ap.offset, [list(xs_ap.ap[0]), [0, 3], [64, 4], [1, 64]])
                        m_ap = mdw[:, tt, :, 4 * g:4 * g + 4]
                        in1_ = bass.AP(m_ap.tensor, m_ap.offset, [list(m_ap.ap[0]), [32, 3], [1, 4], [0, 64]])
                        P.op("dve", ("tensor_tensor", dict(out=X3[:, p_, :, :].rearrange("p j (h d) -> p j h d", h=4), in0=in0_, in1=in1_, op=ALU.mult)),
                             reads=[tokm_b[tt], sv_b["dw"][tt]], writes=[X3_b[p_]])
                    if tc is not None:
                        copy_op("act", ynT[:, 2 * (g % 2):2 * (g % 2) + 2, csc], ptv[:, 0:256].rearrange("p (k t) -> p k t", k=2), [ptbuf], [ynT_b[ttc]])
                    if tb is not None:
                        P.op("dve", ("tensor_tensor", dict(out=Ysb[:, q_, :], in0=pyb[:, 0:256], in1=junk[:, q_, :], op=ALU.add)),
                             reads=[pybbuf, junk_b[q_]], writes=[Ysb_b[q_]])
                        P.op("dve", ("tensor_tensor", dict(out=Ysb[:, q_, :], in0=Ysb[:, q_, :], in1=zs[:, ttb, :], op=ALU.mult)),
                             reads=[Ysb_b[q_], zs_b[ttb]], writes=[Ysb_b[q_]])
                        P.op("act", ("activation", dict(out=junk[:, q_, :], in_=Ysb[:, q_, :], func=AF.Square, accum_out=gst[:, q_, 0:1])),
                             reads=[Ysb_b[q_]], writes=[junk_b[q_], gst_b[q_]])
                        P.op("act", ("activation", dict(out=gst[:, q_, 1:2], in_=gst[:, q_, 0:1], func=AF.Ln, scale=1.0 / 256.0, bias=epsc[:, 1:2])),
                             reads=[gst_b[q_], epsc_b], writes=[gst_b[q_]])
                        P.op("act", ("activation", dict(out=gst[:, q_, 1:2], in_=gst[:, q_, 1:2], func=AF.Exp, scale=-0.5)),
                             reads=[gst_b[q_]], writes=[gst_b[q_]])
                    if ta is not None:
                        P.op("dve", ("tensor_tensor", dict(out=MT[:, p_, :].rearrange("p (h t) -> p h t", h=4), in0=eh3,
                                                           in1=bc_mid(pb[:, 0:128], 4), op=ALU.mult)),
                             reads=[eh_b[p_], pbuf], writes=[MT_b[p_]])
                        for h in range(4):
                            hs_ = slice(h * 64, (h + 1) * 64)
                            mm(py[:, hs_], MT[:, p_, h * 128:(h + 1) * 128], X3[:, p_, 0, hs_], True, False,
                               reads=[MT_b[p_], X3_b[p_]], writes=[pybuf], inc=False)
                            mm(py[:, hs_], ident_bf, X3[:, p_, 2, hs_], False, True,
                               reads=[cstbf_b, X3_b[p_]], writes=[pybuf], inc=False)
                        mm(py[:, 256:512], xcT[:, 3, cs], Hb, True, True, reads=[xcT_b[3], Hb_b], writes=[pybuf])
                        ph, phbuf = psum[:, 7, :], ps_b[7]
                        mm(ph[:, 0:256], tokm[0:nv, tt, 256:384], X3[0:nv, p_, 1, :], True, True, reads=[tokm_b[tt], X3_b[p_]], writes=[phbuf])
                    if tb is not None:
                        P.op("dve", ("scalar_tensor_tensor", dict(out=yn[:, q_, :], in0=Ysb[:, q_, :], scalar=gst[:, q_, 1:2], in1=nwg,
                                                                  op0=ALU.mult, op1=ALU.mult)),
                             reads=[Ysb_b[q_], gst_b[q_], nwg_b], writes=[yn_b[q_]])
                    if ta is not None:
                        Hs3 = Hs.rearrange("p (h d) -> p h d", h=4)
                        P.op("dve", ("tensor_tensor", dict(out=Hs3, in0=Hs3, in1=bc_last(sv["ecl"][:, tt, h4], 64), op=ALU.mult)),
                             reads=[Hs_b, sv_b["ecl"][tt]], writes=[Hs_b])
                        P.op("dve", ("tensor_tensor", dict(out=Hs, in0=Hs, in1=ph[:, 0:256], op=ALU.add)), reads=[Hs_b, phbuf], writes=[Hs_b])

                n_ = sg.ntiles
                rng = lambda v: v if 0 <= v < n_ else None
                for k in range(n_ + 3):
                    ssd_iter(rng(k), rng(k - 1), rng(k - 2), rng(k - 3))
                if sg.kind == "s":
                    h_dst, h_wb = ssm_sample.ap()[li, sg.s], []
                elif sg.final:
                    h_dst, h_wb = ssm_prompt.ap()[li, sg.s], []
                else:
                    h_dst, h_wb = hscr.ap()[li], [hscr_gb[li][g]]
                pb, pbuf = bank()
                for k2 in range(2):
                    P.op("pe", ("transpose", dict(out=pb[:, k2 * 128:(k2 + 1) * 128], in_=Hs[:, k2 * 128:(k2 + 1) * 128], identity=ident_f)),
                         reads=[Hs_b, cst_b], writes=[pbuf], inc=(k2 == 1))
                P.op("dve", ("tensor_copy", dict(out=hout, in_=pb[:, 0:256].rearrange("p (k n) -> p k n", k=2))), reads=[pbuf], writes=[hout_b])
                P.dma("sp", h_dst[g * 256:(g + 1) * 256, :].rearrange("(k p) n -> p k n", p=128), hout, reads=[hout_b], writes=h_wb)
            if g % 2 == 1:
                wo, wo_b = load_w(w_ssm_out.ap()[li, (g - 1) * 256:(g + 1) * 256, :].rearrange("(k p) c -> p k c", p=128), [128, 4, 1024])
                for tt in range(ntt):
                    for ch in range(2):
                        pb, pbuf = bank()
                        for k4 in range(4):
                            mm(pb, ynT[:, k4, tt * 128:(tt + 1) * 128], wo[:, k4, ch * 512:(ch + 1) * 512], k4 == 0, k4 == 3,
                               reads=[wo_b, ynT_b[tt]], writes=[pbuf])
                        xs_ = xres[:, tt, ch * 512:(ch + 1) * 512]
                        P.op("dve", ("scalar_tensor_tensor", dict(out=xs_, in0=xs_, scalar=(ALPHA if g == 1 else 1.0), in1=pb,
                                                                  op0=ALU.mult, op1=ALU.add)), reads=[pbuf, xres_b[tt]], writes=[xres_b[tt]])

    blocks = []
    nb = SEQ // TB
    for s_ in range(NPS):
        for b in range(nb):
            blocks.append(("p", [Seg("p", s_, b * TB, TB // 128, 0, 128, 0 if b == 0 else WIN, b == nb - 1, 0)]))
    blocks.append(("s", [Seg("s", s_, 0, 1, s_, DSEQ, WIN, True, s_) for s_ in range(NSS)]))

    for kind, segs in blocks:
        ntt = sum(sg.ntiles for sg in segs)
        nt = ntt * 128
        for sg in segs:
            for ti in range(sg.ntiles):
                tt = sg.tile0 + ti
                if kind == "p":
                    P.dma("sp", xres[:, tt, :], x_prompt.ap()[sg.s, sg.t0 + ti * 128: sg.t0 + (ti + 1) * 128, :],
                          writes=[xres_b[tt]])
                else:
                    P.op("dve", ("memset", dict(ap=xres[:, tt, :], constant=0.0)), writes=[xres_b[tt]])
                    P.dma("sp", xres[0:DSEQ, tt, :], x_sample.ap()[sg.s, :, :], writes=[xres_b[tt]])
                make_xT(tt)
        for layer in range(cfg.layers):
            P.barrier()
            mixed = False
            if layer % 2 == 0 and cfg.stage in ("attn", "full"):
                attention_layer(layer // 2, kind, segs, ntt)
                mixed = True
            if layer % 2 == 1 and cfg.stage in ("ssm", "full"):
                ssm_layer(layer // 2, kind, segs, ntt)
                mixed = True
            P.barrier()
            load_ln(layer, 0)
            ln_phase(list(range(ntt)), not mixed, make_xT)
            P.barrier()
            mlp(layer, nt)
            load_ln(layer, 1)
            last = layer == cfg.layers - 1
            tile_seg = {}
            for sg in segs:
                for ti in range(sg.ntiles):
                    tile_seg[sg.tile0 + ti] = (sg, ti)

            def store_y(tt, tile_seg=tile_seg, kind=kind):
                sg, ti = tile_seg[tt]
                if kind == "p":
                    P.dma("sp", y_prompt.ap()[sg.s, sg.t0 + ti * 128: sg.t0 + (ti + 1) * 128, :], xres[:, tt, :],
                          reads=[xres_b[tt]])
                else:
                    P.dma("sp", y_sample.ap()[sg.s, :, :], xres[0:DSEQ, tt, :], reads=[xres_b[tt]])

            ln_phase(list(range(ntt)), False, store_y if last else make_xT)
            P.barrier()

    P.finish()
    P.emit()
    return nc


def _consts():
    c = np.zeros((6, 128, 128), np.float32)
    i = np.arange(128)
    c[0] = np.eye(128, dtype=np.float32)
    c[1] = (i[:, None] <= i[None, :]).astype(np.float32)
    c[2] = 1.0
    same = (i[:, None] // 64) == (i[None, :] // 64)
    c[3] = c[1] * same
    c[4] = same.astype(np.float32)
    return c


def _bias_index():
    k = np.arange(128)[:, None]
    col = np.arange(640)[None, :]
    j, r = col // 128, col % 128
    q = 128 * (4 - j) + r
    idx = np.minimum(q - k, 256) + 256
    valid = np.where(k < 64, q < 576, q >= 64)
    mask = np.where(valid, 0.0, -1e30).astype(np.float32)
    return idx, mask


def prepare(inputs, cfg):
    f = lambda a: np.ascontiguousarray(np.asarray(a, dtype=np.float32))
    NPS, NSS = cfg.NPS, cfg.NSS
    rel = f(inputs["rel_bias"])
    idx, amask = _bias_index()
    biasT = np.ascontiguousarray(rel[:, :, idx])
    lnp = np.ascontiguousarray(np.stack([f(inputs["ln_mix_g"]), f(inputs["ln_mix_b"]),
                                         f(inputs["ln_ff_g"]), f(inputs["ln_ff_b"])], axis=1))
    conv_wT = np.ascontiguousarray(f(inputs["conv_w"]).reshape(2, 4, 32, 128).transpose(0, 3, 2, 1))
    conv_bT = np.ascontiguousarray(f(inputs["conv_b"]).reshape(2, 32, 128).transpose(0, 2, 1))
    ssm_vec = np.ascontiguousarray(np.stack([f(inputs["dt_bias"]), f(inputs["a_log"]), f(inputs["d_skip"])], axis=1))
    shared = {
        "w_qkv": f(inputs["w_qkv"]), "w_attn_out": f(inputs["w_attn_out"]), "w_ssm_in": f(inputs["w_ssm_in"]),
        "w_ssm_out": f(inputs["w_ssm_out"]), "w_ff_up": f(inputs["w_ff_up"]), "w_ff_down": f(inputs["w_ff_down"]),
        "biasT": biasT, "lnp": lnp, "conv_wT": conv_wT, "conv_bT": conv_bT, "ssm_vec": ssm_vec,
        "ssm_norm_w": f(inputs["ssm_norm_w"]), "consts": _consts(), "amask": amask,
    }
    xp, xs = f(inputs["x_prompt"]), f(inputs["x_sample"])
    ck, cv = f(inputs["cache_k"]), f(inputs["cache_v"])
    ss, sc = f(inputs["state_ssm"]), f(inputs["state_conv"])
    maps = []
    for c in range(NCORES):
        m = dict(shared)
        m["x_prompt"] = np.ascontiguousarray(xp[c * NPS:(c + 1) * NPS])
        m["x_sample"] = np.ascontiguousarray(xs[c * NSS:(c + 1) * NSS])
        m["cache_k"] = np.ascontiguousarray(ck[:, c * NSS:(c + 1) * NSS].reshape(2, NSS, WIN, D))
        m["cache_v"] = np.ascontiguousarray(cv[:, c * NSS:(c + 1) * NSS].reshape(2, NSS, WIN, D))
        m["state_ssm"] = np.ascontiguousarray(ss[:, c * NSS:(c + 1) * NSS].reshape(2, NSS, DIN, DST))
        m["state_conv"] = np.ascontiguousarray(sc[:, c * NSS:(c + 1) * NSS])
        maps.append(m)
    return maps


def assemble(results, cfg):
    NPS, NSS, SEQ, DSEQ, KEEP = cfg.NPS, cfg.NSS, cfg.SEQ, cfg.DSEQ, cfg.KEEP
    cat0 = lambda k: np.concatenate([r[k] for r in results], axis=0)
    cat1 = lambda k: np.concatenate([r[k] for r in results], axis=1)
    return (
        cat0("y_prompt"), cat0("y_sample"),
        cat1("k_prompt").reshape(2, NCORES * NPS, KEEP, NH, HD),
        cat1("v_prompt").reshape(2, NCORES * NPS, KEEP, NH, HD),
        cat1("ssm_prompt").reshape(2, NCORES * NPS, 32, 64, DST),
        cat1("conv_prompt"),
        cat1("k_sample").reshape(2, NCORES * NSS, DSEQ, NH, HD),
        cat1("v_sample").reshape(2, NCORES * NSS, DSEQ, NH, HD),
        cat1("ssm_sample").reshape(2, NCORES * NSS, 32, 64, DST),
        cat1("conv_sample"),
    )


def run(inputs, cfg, trace=False):
    nc = build(cfg)
    maps = prepare(inputs, cfg)
    res = run_bass_kernel_spmd(nc, maps, core_ids=list(range(NCORES)), trace=trace)
    out = assemble(res.results, cfg)
    if trace:
        return out, res
    return out


def kernel(**inputs):
    cfg = Cfg(NPS=4, SEQ=2048, TB=1024, NSS=4)
    return run(inputs, cfg)
```

```python
import numpy as np
import concourse.bass as bass
import concourse.mybir as mybir
from concourse.bass_utils import run_bass_kernel_spmd

F32 = mybir.dt.float32
BF16 = mybir.dt.bfloat16
AF = mybir.ActivationFunctionType
ALU = mybir.AluOpType

D = 1024
NH = 16
HD = 64
WIN = 512
DFF = 4096
DIN = 2048
NG = 8
DST = 128
CCH = 4096
WIN_COLS = 6176
ALPHA = float(8 ** 0.25)
LN_EPS = 1e-5
RMS_EPS = 1e-5
NCORES = 8


class Buf:
    __slots__ = ("w", "r", "ps")

    def __init__(self, ps=False):
        self.w = None
        self.r = {}
        self.ps = ps


class Prog:
    def __init__(self, nc):
        self.nc = nc
        self.E = {"pe": nc.tensor, "act": nc.scalar, "dve": nc.vector, "pool": nc.gpsimd, "sp": nc.sync}
        self.ops = {e: [] for e in self.E}
        self.cnt = {e: 0 for e in self.E}
        self.sem = {e: nc.alloc_semaphore(name="sem_" + e) for e in self.E}
        self.known = {e: {} for e in self.E}
        self.nds = 40
        self.dsem = [nc.alloc_semaphore(name="dsem%d" % i) for i in range(self.nds)]
        self.dcnt = [0] * self.nds
        self.dpool = {"sp": list(range(0, 28)), "pool": list(range(28, 40))}
        self.dnext = {"sp": 0, "pool": 0}
        self.phase = []

    def _semh(self, k):
        return self.sem[k] if isinstance(k, str) else self.dsem[k[1]]

    def _deps(self, reads, writes):
        deps = []
        for b in reads:
            deps.append(b.w)
        for b in writes:
            deps.append(b.w)
            deps.extend(b.r.items())
        return deps

    def _waits(self, eng, deps):
        need = {}
        for t in deps:
            if t is None:
                continue
            k, v = t
            if k == eng and eng == "pe":
                continue
            if need.get(k, 0) < v:
                need[k] = v
        out = []
        kn = self.known[eng]
        for k, v in need.items():
            if kn.get(k, 0) >= v:
                continue
            kn[k] = v
            out.append((self._semh(k), v))
        return out

    def _commit(self, tick, reads, writes):
        k, v = tick
        for b in reads:
            if b.r.get(k, 0) < v:
                b.r[k] = v
        for b in writes:
            b.w = tick
            b.r = {}

    def op(self, eng, fn, reads=(), writes=(), inc=True):
        psr = [b for b in reads if b.ps]
        if psr:
            writes = list(writes) + psr
        waits = self._waits(eng, self._deps(reads, writes))
        if inc:
            self.cnt[eng] += 1
            tick = (eng, self.cnt[eng])
        else:
            tick = (eng, self.cnt[eng] + 1)
        sem = self.sem[eng]

        def run(e, waits=waits, fn=fn, sem=sem, inc=inc):
            for s, v in waits:
                e.wait_ge(s, v)
            if isinstance(fn, tuple):
                r = getattr(e, fn[0])(**fn[1])
            else:
                r = fn(e)
            if inc:
                r.then_inc(sem, 1)

        self.ops[eng].append(run)
        self._commit(tick, reads, writes)

    def dma(self, eng, out, in_, reads=(), writes=(), arena=False, slow=False):
        pl = self.dpool[eng]
        i = pl[self.dnext[eng]]
        self.dnext[eng] = (self.dnext[eng] + 1) % len(pl)
        deps = self._deps(reads, writes)
        if arena:
            deps.extend(self.phase)
        if self.dcnt[i] > 0:
            deps.append((("d", i), self.dcnt[i]))
        waits = self._waits(eng, deps)
        self.dcnt[i] += 16
        tick = (("d", i), self.dcnt[i])
        sem = self.dsem[i]

        def run(e, waits=waits, sem=sem, out=out, in_=in_, slow=slow):
            for s, v in waits:
                e.wait_ge(s, v)
            if slow:
                e.dma_start(out=out, in_=in_, allow_slow_non_contiguous=True).then_inc(sem, 16)
            else:
                e.dma_start(out=out, in_=in_).then_inc(sem, 16)

        self.ops[eng].append(run)
        self._commit(tick, reads, writes)

    def barrier(self, engs=("pe", "act", "dve")):
        comp = ("pe", "act", "dve")
        self.phase = [(c, self.cnt[c]) for c in comp if self.cnt[c] > 0]
        for e in engs:
            deps = [(c, self.cnt[c]) for c in comp if c != e and self.cnt[c] > 0]
            deps += [(("d", i), self.dcnt[i]) for i in range(self.nds) if self.dcnt[i] > 0]
            waits = self._waits(e, deps)
            if waits:
                def run(eh, waits=waits):
                    for s, v in waits:
                        eh.wait_ge(s, v)
                self.ops[e].append(run)

    def finish(self):
        deps = [(("d", i), self.dcnt[i]) for i in range(self.nds) if self.dcnt[i] > 0]
        deps += [(c, self.cnt[c]) for c in ("pe", "act", "dve", "pool") if self.cnt[c] > 0]
        waits = self._waits("sp", deps)

        def run(eh, waits=waits):
            for s, v in waits:
                eh.wait_ge(s, v)
        self.ops["sp"].append(run)

    def emit(self):
        nc = self.nc
        with nc.Block() as block:
            @block.sync
            def _(e):
                for f in self.ops["sp"]:
                    f(e)

            @block.tensor
            def _(e):
                for f in self.ops["pe"]:
                    f(e)

            @block.scalar
            def _(e):
                for f in self.ops["act"]:
                    f(e)

            @block.vector
            def _(e):
                for f in self.ops["dve"]:
                    f(e)

            @block.gpsimd
            def _(e):
                for f in self.ops["pool"]:
                    f(e)


class Cfg:
    def __init__(self, NPS, SEQ, TB, NSS, DSEQ=64, layers=4, debug=False):
        self.NPS, self.SEQ, self.TB, self.NSS, self.DSEQ = NPS, SEQ, TB, NSS, DSEQ
        self.layers = layers
        self.stage = "full"
        self.skip = set()
        assert SEQ % TB == 0 and TB % 128 == 0
        assert TB >= WIN or SEQ == TB
        assert DSEQ == 64
        self.NT = max(TB, NSS * 128)
        self.NTT = self.NT // 128
        self.KEEP = min(WIN, SEQ)


class Seg:
    def __init__(self, kind, s, t0, ntiles, tile0, nvalid, Lt, final, idx):
        self.kind, self.s, self.t0, self.ntiles, self.tile0 = kind, s, t0, ntiles, tile0
        self.nvalid, self.Lt, self.final, self.idx = nvalid, Lt, final, idx


def bc_ap(t, off, n, parts=128):
    return bass.AP(t, off, [[0, parts], [1, n]])


def build(cfg):
    nc = bass.Bass("TRN2", target_bir_lowering=False)
    P = Prog(nc)
    NPS, SEQ, TB, NSS, DSEQ = cfg.NPS, cfg.SEQ, cfg.TB, cfg.NSS, cfg.DSEQ
    NT, NTT, KEEP = cfg.NT, cfg.NTT, cfg.KEEP

    def din(name, shape):
        return nc.dram_tensor(name, list(shape), F32, kind="ExternalInput")

    def dout(name, shape):
        return nc.dram_tensor(name, list(shape), F32, kind="ExternalOutput")

    def dscr(name, shape):
        return nc.dram_tensor(name, list(shape), F32, kind="Internal")

    x_prompt = din("x_prompt", [NPS, SEQ, D])
    x_sample = din("x_sample", [NSS, DSEQ, D])
    cache_k = din("cache_k", [2, NSS, WIN, D])
    cache_v = din("cache_v", [2, NSS, WIN, D])
    state_ssm = din("state_ssm", [2, NSS, DIN, DST])
    state_conv = din("state_conv", [2, NSS, 3, CCH])
    w_qkv = din("w_qkv", [2, D, 3 * D])
    w_attn_out = din("w_attn_out", [2, D, D])
    w_ssm_in = din("w_ssm_in", [2, D, WIN_COLS])
    w_ssm_out = din("w_ssm_out", [2, DIN, D])
    w_ff_up = din("w_ff_up", [4, D, DFF])
    w_ff_down = din("w_ff_down", [4, DFF, D])
    biasT = din("biasT", [2, NH, 128, 640])
    lnp = din("lnp", [4, 4, D])
    conv_wT = din("conv_wT", [2, 128, 32, 4])
    conv_bT = din("conv_bT", [2, 128, 32])
    ssm_vec = din("ssm_vec", [2, 3, 32])
    ssm_norm_w = din("ssm_norm_w", [2, DIN])
    consts = din("consts", [6, 128, 128])
    amask = din("amask", [128, 640])

    y_prompt = dout("y_prompt", [NPS, SEQ, D])
    y_sample = dout("y_sample", [NSS, DSEQ, D])
    k_prompt = dout("k_prompt", [2, NPS, KEEP, D])
    v_prompt = dout("v_prompt", [2, NPS, KEEP, D])
    ssm_prompt = dout("ssm_prompt", [2, NPS, DIN, DST])
    conv_prompt = dout("conv_prompt", [2, NPS, 3, CCH])
    k_sample = dout("k_sample", [2, NSS, DSEQ, D])
    v_sample = dout("v_sample", [2, NSS, DSEQ, D])
    ssm_sample = dout("ssm_sample", [2, NSS, DIN, DST])
    conv_sample = dout("conv_sample", [2, NSS, 3, CCH])

    kscr = dscr("kscr", [2, WIN, D])
    vscr = dscr("vscr", [2, WIN, D])
    hscr = dscr("hscr", [2, DIN, DST])
    cscr = dscr("cscr", [2, 3, CCH])
    kscr_b = [Buf(), Buf()]
    vscr_b = [Buf(), Buf()]
    hscr_b = [Buf(), Buf()]
    cscr_b = [Buf(), Buf()]

    sb_off = [(int(nc.sbuf_base) + 63) // 64 * 64]
    sb_top = int(nc.sbuf_top)
    uid = [0]

    def salloc(shape, dtype, off=None):
        nbytes = int(np.prod(shape[1:])) * (4 if dtype == F32 else 2)
        nbytes = (nbytes + 31) // 32 * 32
        if off is None:
            off = sb_off[0]
            sb_off[0] += nbytes
        uid[0] += 1
        assert off + nbytes <= sb_top, ("SBUF overflow", off, nbytes, sb_top)
        t = nc.alloc_sbuf_tensor_at("t%d" % uid[0], list(shape), dtype, offset=off)
        return t.ap(), off + nbytes

    def fix(shape, dtype):
        return salloc(shape, dtype)[0]

    xres = fix([128, NTT, D], F32)
    xres_b = [Buf() for _ in range(NTT)]
    xT = fix([128, 8, NT], BF16)
    xT_b = [Buf() for _ in range(NTT)]
    NSLOT = 4
    wring = [fix([128, 4096], BF16) for _ in range(NSLOT)]
    wring_b = [Buf() for _ in range(NSLOT)]
    lngb = fix([128, 2, D], F32)
    lngb_b = Buf()
    cst_f = fix([128, 6, 128], F32)
    cst_b = Buf()
    ident_bf = fix([128, 128], BF16)
    ones_bf = fix([128, 128], BF16)
    cstbf_b = Buf()
    xbf = fix([128, 2, D], BF16)
    xbf_b = [Buf(), Buf()]
    lnst = fix([128, 4, 16], F32)
    lnst_b = [Buf() for _ in range(4)]
    epsc = fix([128, 4], F32)
    onec = epsc[:, 2:3]
    epsc_b = Buf()
    negm = fix([128, 4, 128], BF16)
    negm_b = Buf()
    tri_bf = fix([128, 128], BF16)
    negtri_bf = fix([128, 128], BF16)
    arena0 = sb_off[0]

    psum = nc.alloc_psum_tensor("psum", [128, 8, 512], F32).ap()
    ps_b = [Buf(ps=True) for _ in range(8)]
    ps_rr = [0]

    def bank():
        i = ps_rr[0]
        ps_rr[0] = (i + 1) % 4
        return psum[:, i, :], ps_b[i]

    pair_rr = [0]

    def bankpair():
        i = pair_rr[0]
        pair_rr[0] = (i + 1) % 2
        return psum[:, 4 + 2 * i:6 + 2 * i, :], ps_b[4 + 2 * i]

    wslot = [0]

    def load_w(src_ap, view_shape):
        i = wslot[0]
        wslot[0] = (i + 1) % NSLOT
        n = int(np.prod(view_shape[1:]))
        assert n <= 4096
        dst = wring[i][:, 0:n]
        if len(view_shape) == 3:
            dst = dst.rearrange("p (a b) -> p a b", a=view_shape[1])
        elif len(view_shape) == 4:
            dst = dst.rearrange("p (a b c) -> p a b c", a=view_shape[1], b=view_shape[2])
        P.dma("pool", dst, src_ap, writes=[wring_b[i]])
        return dst, wring_b[i]

    def mm(out, lhsT, rhs, start, stop, reads, writes, inc=None):
        if inc is None:
            inc = stop
        P.op("pe", ("matmul", dict(out=out, lhsT=lhsT, rhs=rhs, start=start, stop=stop)),
             reads=reads, writes=writes, inc=inc)

    evac_rr = [0]

    def evac_engine():
        evac_rr[0] ^= 1
        return "act" if evac_rr[0] else "dve"

    def copy_op(eng, out, in_, reads, writes, scale=None):
        if eng == "act":
            if scale is None:
                P.op("act", ("activation", dict(out=out, in_=in_, func=AF.Copy)), reads=reads, writes=writes)
            else:
                P.op("act", ("activation", dict(out=out, in_=in_, func=AF.Copy, scale=float(scale))),
                     reads=reads, writes=writes)
        else:
            if scale is None:
                P.op("dve", ("tensor_copy", dict(out=out, in_=in_)), reads=reads, writes=writes)
            else:
                P.op("dve", ("tensor_scalar", dict(out=out, in0=in_, scalar1=float(scale), scalar2=None,
                                                      op0=ALU.mult)), reads=reads, writes=writes)

    P.dma("sp", cst_f, consts.ap().rearrange("c p f -> p c f"), writes=[cst_b])
    P.op("dve", ("tensor_copy", dict(out=ident_bf, in_=cst_f[:, 0, :])), reads=[cst_b], writes=[cstbf_b])
    P.op("dve", ("tensor_copy", dict(out=ones_bf, in_=cst_f[:, 2, :])), reads=[cst_b], writes=[cstbf_b])

    def make_xT(tt, nrows=128):
        sl = tt % 2
        P.op("act", ("activation", dict(out=xbf[:, sl, :], in_=xres[:, tt, :], func=AF.Copy)),
             reads=[xres_b[tt]], writes=[xbf_b[sl]])
        pb, pbuf = bank()
        pbv = pb.bitcast(BF16)
        for kc in range(8):
            P.op("pe", ("transpose", dict(out=pbv[:, kc * 128:(kc + 1) * 128],
                                                    in_=xbf[:, sl, kc * 128:(kc + 1) * 128], identity=ident_bf)),
                 reads=[xbf_b[sl], cstbf_b], writes=[pbuf], inc=(kc == 7))
        P.op("dve", ("tensor_copy", dict(out=xT[:, :, tt * 128:(tt + 1) * 128],
                                            in_=pbv.rearrange("p (k t) -> p k t", k=8))),
             reads=[pbuf], writes=[xT_b[tt]])

    def ln_s1(tt, prescale):
        sl = tt % 4
        st = lnst[:, sl, :]
        stb = lnst_b[sl]
        if prescale:
            P.op("dve", ("tensor_scalar", dict(out=xres[:, tt, :], in0=xres[:, tt, :], scalar1=ALPHA, scalar2=None, op0=ALU.mult)),
                 reads=[xres_b[tt]], writes=[xres_b[tt]])
        P.op("dve", ("bn_stats", dict(out=st[:, 0:6], in_=xres[:, tt, 0:512])), reads=[xres_b[tt]], writes=[stb])
        P.op("dve", ("bn_stats", dict(out=st[:, 6:12], in_=xres[:, tt, 512:1024])), reads=[xres_b[tt]], writes=[stb])
        P.op("dve", ("bn_aggr", dict(out=st[:, 12:14], in_=st[:, 0:12])), reads=[stb], writes=[stb])
        P.op("act", ("activation", dict(out=st[:, 14:15], in_=st[:, 13:14], func=AF.Ln, bias=epsc[:, 0:1])),
             reads=[stb, epsc_b], writes=[stb])
        P.op("act", ("activation", dict(out=st[:, 14:15], in_=st[:, 14:15], func=AF.Exp, scale=-0.5)), reads=[stb], writes=[stb])
        P.op("dve", ("scalar_tensor_tensor", dict(out=st[:, 15:16], in0=st[:, 12:13], scalar=-1.0, in1=st[:, 14:15],
                                                     op0=ALU.mult, op1=ALU.mult)), reads=[stb], writes=[stb])

    def ln_s2(tt):
        sl = tt % 4
        st = lnst[:, sl, :]
        stb = lnst_b[sl]
        xt = xres[:, tt, :]
        P.op("act", ("activation", dict(out=xt, in_=xt, func=AF.Identity, scale=st[:, 14:15], bias=st[:, 15:16])),
             reads=[stb, xres_b[tt]], writes=[xres_b[tt]])
        P.op("dve", ("tensor_tensor", dict(out=xt, in0=xt, in1=lngb[:, 0, :], op=ALU.mult)),
             reads=[xres_b[tt], lngb_b], writes=[xres_b[tt]])
        P.op("dve", ("tensor_tensor", dict(out=xt, in0=xt, in1=lngb[:, 1, :], op=ALU.add)),
             reads=[xres_b[tt], lngb_b], writes=[xres_b[tt]])

    def ln_phase(tiles, prescale, s3):
        n = len(tiles)
        for i in range(n + 2):
            if i < n:
                ln_s1(tiles[i], prescale)
            if 0 <= i - 1 < n:
                ln_s2(tiles[i - 1])
            if 0 <= i - 2 < n:
                s3(tiles[i - 2])

    P.op("dve", ("tensor_scalar", dict(out=negm, in0=bass.AP(cst_f.tensor, cst_f[:, 1, :].offset, [list(cst_f[:, 1, :].ap[0]), [0, 4], [1, 128]]),
                                       scalar1=30000.0, scalar2=-30000.0, op0=ALU.mult, op1=ALU.add)), reads=[cst_b], writes=[negm_b])
    P.op("dve", ("tensor_copy", dict(out=tri_bf, in_=cst_f[:, 1, :])), reads=[cst_b], writes=[negm_b])
    P.op("dve", ("tensor_scalar", dict(out=negtri_bf, in0=cst_f[:, 1, :], scalar1=-1.0, scalar2=None, op0=ALU.mult)), reads=[cst_b], writes=[negm_b])
    P.op("dve", ("memset", dict(ap=epsc[:, 0:1], constant=LN_EPS)), writes=[epsc_b])
    P.op("dve", ("memset", dict(ap=epsc[:, 1:2], constant=RMS_EPS)), writes=[epsc_b])
    P.op("dve", ("memset", dict(ap=epsc[:, 2:3], constant=1.0)), writes=[epsc_b])

    def load_ln(layer, which):
        off = (layer * 4 + which * 2) * D
        P.dma("sp", lngb, bass.AP(lnp, off, [[0, 128], [D, 2], [1, D]]), writes=[lngb_b])

    HG = 2
    HGW = DFF // HG
    hT, a_end = salloc([128, HGW // 128, NT], BF16, arena0)
    hT_b = Buf()
    rtmp, a_end = salloc([128, 2, 512], F32, a_end)
    rtmp_b = [Buf(), Buf()]
    mlp_end = a_end

    def mlp(layer, nt):
        ntt = nt // 128
        tgs = [(t0, min(512, nt - t0)) for t0 in range(0, nt, 512)]
        rr = 0
        for g in range(HG):
            for wt in range(HGW // 512):
                c0 = g * HGW + wt * 512
                w, wb = load_w(w_ff_up.ap()[layer, :, c0:c0 + 512].rearrange("(k p) c -> p k c", p=128), [128, 8, 512])
                for sub in range(4):
                    for (t0, tn) in tgs:
                        pb, pbuf = bank()
                        for kc in range(8):
                            mm(pb[:, 0:tn], w[:, kc, sub * 128:(sub + 1) * 128], xT[:, kc, t0:t0 + tn],
                               kc == 0, kc == 7, reads=[wb] + xT_b[t0 // 128:(t0 + tn + 127) // 128], writes=[pbuf])
                        sl = rr % 2
                        rr += 1
                        P.op("act", ("activation", dict(out=rtmp[:, sl, 0:tn], in_=pb[:, 0:tn], func=AF.Relu)),
                             reads=[pbuf], writes=[rtmp_b[sl]])
                        fi = wt * 4 + sub
                        P.op("dve", ("tensor_tensor", dict(
                            out=hT[:, fi, t0:t0 + tn], in0=rtmp[:, sl, 0:tn], in1=rtmp[:, sl, 0:tn], op=ALU.mult)),
                            reads=[rtmp_b[sl]], writes=[hT_b])
            for ch in range(2):
                ws = []
                for kh in range(HGW // 1024):
                    r0 = g * HGW + kh * 1024
                    ws.append(load_w(w_ff_down.ap()[layer, r0:r0 + 1024, ch * 512:(ch + 1) * 512]
                                     .rearrange("(k p) c -> p k c", p=128), [128, 8, 512]))
                for tt in range(ntt):
                    pb, pbuf = bank()
                    nk = (HGW // 1024) * 8
                    for ki in range(nk):
                        w, wb = ws[ki // 8]
                        mm(pb, hT[:, ki, tt * 128:(tt + 1) * 128], w[:, ki % 8, :], ki == 0, ki == nk - 1,
                           reads=[wb, hT_b], writes=[pbuf])
                    xs_ = xres[:, tt, ch * 512:(ch + 1) * 512]
                    sc = ALPHA if g == 0 else 1.0
                    P.op("dve", ("scalar_tensor_tensor", dict(
                        out=xs_, in0=xs_, scalar=sc, in1=pb, op0=ALU.mult, op1=ALU.add)),
                        reads=[pbuf, xres_b[tt]], writes=[xres_b[tt]])

    KTW = max(WIN + TB, NSS * (WIN + 128))
    NVT = max(4 + TB // 128, NSS * 5)
    a_ = arena0
    OT, a_ = salloc([128, 8, NT], BF16, a_)
    OT_b = [Buf() for _ in range(8)]
    QT, a_ = salloc([128, 2, NT], BF16, a_)
    QT_b = [Buf(), Buf()]
    KT, a_ = salloc([128, 2, KTW], BF16, a_)
    KT_b = [Buf(), Buf()]
    VV, a_ = salloc([128, 2, NVT, 128], BF16, a_)
    VV_b = [Buf(), Buf()]
    bias2, a_ = salloc([128, 2, 2, 640], F32, a_)
    bias2_b = [Buf(), Buf()]
    stmp, a_ = salloc([128, 4, 640], F32, a_)
    stmp_b = [Buf() for _ in range(4)]
    PT, a_ = salloc([128, 4, 640], BF16, a_)
    PT_b = [Buf() for _ in range(4)]
    ktok, a_ = salloc([128, 2, 4, 128], BF16, a_)
    ktok_b = [Buf(), Buf()]
    kvout, a_ = salloc([128, 2, 4, 2, 128], F32, a_)
    kvout_b = [Buf(), Buf()]
    rsum, a_ = salloc([128, 4, 128], F32, a_)
    rsum_b = [Buf() for _ in range(4)]
    maskc, a_ = salloc([128, 640], F32, a_)
    maskc_b = Buf()
    attn_end = a_
    kscr_hb = [[Buf() for _ in range(8)] for _ in range(2)]
    vscr_hb = [[Buf() for _ in range(8)] for _ in range(2)]
    cnt_att = [0]
    att_par = [0]

    def attention_layer(li, kind, segs, ntt):
        nt = ntt * 128
        P.dma("sp", maskc, amask.ap(), writes=[maskc_b], arena=True)
        for hp in range(8):
            bf = hp % 2
            i_ = wslot[0]
            wslot[0] = (i_ + 1) % NSLOT
            w = wring[i_][:, 0:3072].rearrange("p (a b c) -> p a b c", a=8, b=3)
            wb = wring_b[i_]
            for which in range(3):
                P.dma("pool", w[:, :, which, :],
                      bass.AP(w_qkv, li * D * 3 * D + which * D + hp * 128, [[3 * D, 128], [128 * 3 * D, 8], [1, 128]]),
                      writes=[wb])
            P.dma("sp", bias2[:, bf, :, :], biasT.ap()[li, 2 * hp:2 * hp + 2, :, :].rearrange("h p f -> p h f"),
                  writes=[bias2_b[bf]], arena=True)
            for hh in range(2):
                P.op("dve", ("tensor_tensor", dict(out=bias2[:, bf, hh, :], in0=bias2[:, bf, hh, :], in1=maskc,
                                                             op=ALU.add)), reads=[bias2_b[bf], maskc_b], writes=[bias2_b[bf]])
            for sg in segs:
                if sg.Lt == 0 or "tail" in cfg.skip:
                    continue
                if sg.kind == "s":
                    ksrc, vsrc = cache_k.ap()[li, sg.s], cache_v.ap()[li, sg.s]
                    kb_, vb_ = [], []
                else:
                    ksrc, vsrc = kscr.ap()[li], vscr.ap()[li]
                    kb_, vb_ = [kscr_hb[li][hp]], [vscr_hb[li][hp]]
                koff = sg.idx * (WIN + 128) if sg.kind == "s" else 0
                vt0 = sg.idx * 5 if sg.kind == "s" else 0
                c = cnt_att[0] % 2
                cnt_att[0] += 1
                P.dma("pool", ktok[:, c, :, :], ksrc[:, hp * 128:(hp + 1) * 128].rearrange("(k p) f -> p k f", p=128),
                      reads=kb_, writes=[ktok_b[c]], arena=True)
                pb, pbuf = bank()
                pbv = pb.bitcast(BF16)
                for k4 in range(4):
                    P.op("pe", ("transpose", dict(out=pbv[:, k4 * 128:(k4 + 1) * 128], in_=ktok[:, c, k4, :],
                                                                  identity=ident_bf)),
                         reads=[ktok_b[c], cstbf_b], writes=[pbuf], inc=(k4 == 3))
                copy_op(evac_engine(), KT[:, bf, koff:koff + WIN], pbv[:, 0:WIN], [pbuf], [KT_b[bf]])
                P.dma("pool", VV[:, bf, vt0:vt0 + 4, :], vsrc[:, hp * 128:(hp + 1) * 128].rearrange("(k p) f -> p k f", p=128),
                      reads=vb_, writes=[VV_b[bf]], arena=True)
            for t0 in range(0, nt if "qk" not in cfg.skip else 0, 512):
                tn = min(512, nt - t0)
                xb_ = xT_b[t0 // 128:(t0 + tn) // 128]
                for which in range(2):
                    pb, pbuf = bank()
                    for kc in range(8):
                        mm(pb[:, 0:tn], w[:, kc, which, :], xT[:, kc, t0:t0 + tn], kc == 0, kc == 7, reads=[wb] + xb_, writes=[pbuf])
                    if which == 0:
                        copy_op(evac_engine(), QT[:, bf, t0:t0 + tn], pb[:, 0:tn], [pbuf], [QT_b[bf]], scale=HD ** -0.5)
                    else:
                        if kind == "p":
                            sg = segs[0]
                            dst = KT[:, bf, sg.Lt + t0: sg.Lt + t0 + tn]
                            src = pb[:, 0:tn]
                        else:
                            n_sg = tn // 128
                            s0 = t0 // 128
                            dst = KT[:, bf, s0 * (WIN + 128): (s0 + n_sg) * (WIN + 128)].rearrange(
                                "p (s c) -> p s c", c=WIN + 128)[:, :, WIN:WIN + 128]
                            src = pb[:, 0:tn].rearrange("p (s c) -> p s c", c=128)
                        copy_op(evac_engine(), dst, src, [pbuf], [KT_b[bf]])
            for sg in (segs if "v" not in cfg.skip else []):
                n_out = min(4, sg.ntiles)
                for ti in range(sg.ntiles):
                    tt = sg.tile0 + ti
                    vt = (sg.idx * 5 + 4) if sg.kind == "s" else (sg.Lt // 128 + ti)
                    is_out = ti >= sg.ntiles - n_out
                    oi = ti - (sg.ntiles - n_out) if sg.kind == "p" else sg.idx
                    pb, pbuf = bank()
                    for which in ((1, 2) if is_out else (2,)):
                        for kc in range(8):
                            mm(pb[:, (which - 1) * 128:which * 128], xT[:, kc, tt * 128:(tt + 1) * 128], w[:, kc, which, :],
                               kc == 0, kc == 7, reads=[wb, xT_b[tt]], writes=[pbuf])
                    copy_op("act", VV[:, bf, vt, :], pb[:, 128:256], [pbuf], [VV_b[bf]])
                    if is_out and "kvcopy" not in cfg.skip:
                        P.op("dve", ("tensor_copy", dict(out=kvout[:, bf, oi, :, :],
                                                                          in_=pb[:, 0:256].rearrange("p (a b) -> p a b", a=2))),
                             reads=[pbuf], writes=[kvout_b[bf]])
                if sg.kind == "p" and "vdma" not in cfg.skip:
                    if sg.final:
                        kd, vd = k_prompt.ap()[li, sg.s], v_prompt.ap()[li, sg.s]
                        kw_, vw_ = [], []
                    else:
                        kd, vd = kscr.ap()[li], vscr.ap()[li]
                        kw_, vw_ = [kscr_hb[li][hp]], [vscr_hb[li][hp]]
                    if sg.ntiles >= 4:
                        P.dma("sp", kd[:, hp * 128:(hp + 1) * 128].rearrange("(k p) f -> p k f", p=128), kvout[:, bf, :, 0, :],
                              reads=[kvout_b[bf]], writes=kw_)
                        P.dma("sp", vd[:, hp * 128:(hp + 1) * 128].rearrange("(k p) f -> p k f", p=128), kvout[:, bf, :, 1, :],
                              reads=[kvout_b[bf]], writes=vw_)
                    else:
                        n_ = sg.ntiles
                        P.dma("sp", kd[0:n_ * 128, hp * 128:(hp + 1) * 128].rearrange("(k p) f -> p k f", p=128),
                              kvout[:, bf, 0:n_, 0, :], reads=[kvout_b[bf]], writes=kw_)
                        P.dma("sp", vd[0:n_ * 128, hp * 128:(hp + 1) * 128].rearrange("(k p) f -> p k f", p=128),
                              kvout[:, bf, 0:n_, 1, :], reads=[kvout_b[bf]], writes=vw_)
            if kind == "s" and "v" not in cfg.skip and "vdma" not in cfg.skip:
                for sg in segs:
                    P.dma("sp", k_sample.ap()[li, sg.s, :, hp * 128:(hp + 1) * 128], kvout[0:DSEQ, bf, sg.idx, 0, :],
                          reads=[kvout_b[bf]])
                    P.dma("sp", v_sample.ap()[li, sg.s, :, hp * 128:(hp + 1) * 128], kvout[0:DSEQ, bf, sg.idx, 1, :],
                          reads=[kvout_b[bf]])
            if True:
                st_all = {}

                def att_S(sg, ti, st_all=st_all, bf=bf):
                    koff = sg.idx * (WIN + 128) if sg.kind == "s" else 0
                    tt = sg.tile0 + ti
                    qc0 = tt * 128
                    js = [j for j in range(5) if sg.Lt + 128 * ti - 512 + 128 * j >= 0]
                    heads = []
                    for hh in range(2):
                        hr = slice(hh * 64, (hh + 1) * 64)
                        sc, scb = psum[:, 4 + 2 * hh:6 + 2 * hh, :], ps_b[4 + 2 * hh]
                        scf = sc.rearrange("p a b -> p (a b)")
                        c = (att_par[0] % 2) * 2 + hh
                        nks = {}
                        for j in js:
                            kp = sg.Lt + 128 * ti - 512 + 128 * j
                            nk = 128 if j < 4 else sg.nvalid
                            nks[j] = nk
                            mm(scf[0:nk, j * 128:(j + 1) * 128], KT[hr, bf, koff + kp:koff + kp + nk], QT[hr, bf, qc0:qc0 + 128],
                               True, True, reads=[KT_b[bf], QT_b[bf]], writes=[scb], inc=(j == js[-1]))
                        heads.append((hr, scf, scb, c, nks))
                    st_all[(sg.idx, ti)] = dict(js=js, heads=heads, qc0=qc0)
                    att_par[0] += 1

                def att_AE(sg, ti, st_all=st_all, bf=bf):
                    st = st_all[(sg.idx, ti)]
                    js = st["js"]
                    j0 = js[0]
                    for hh in range(2):
                        hr, scf, scb, c, nks = st["heads"][hh]
                        groups = []
                        if sg.nvalid == 128:
                            groups.append((j0, 5, 128))
                        else:
                            if j0 < 4:
                                groups.append((j0, 4, 128))
                            groups.append((4, 5, sg.nvalid))
                        for (ja, jb, nk) in groups:
                            cs = slice(ja * 128, jb * 128)
                            P.op("dve", ("tensor_tensor", dict(out=stmp[0:nk, c, cs], in0=scf[0:nk, cs], in1=bias2[0:nk, bf, hh, cs], op=ALU.add)),
                                 reads=[scb, bias2_b[bf]], writes=[stmp_b[c]])
                            P.op("act", ("activation", dict(out=PT[0:nk, c, cs], in_=stmp[0:nk, c, cs], func=AF.Exp)),
                                 reads=[stmp_b[c]], writes=[PT_b[c]])

                def att_PV(sg, ti, st_all=st_all, bf=bf):
                    vt0 = sg.idx * 5 if sg.kind == "s" else 0
                    st = st_all[(sg.idx, ti)]
                    js = st["js"]
                    obs = []
                    for hh in range(2):
                        hr, scf, scb, c, nks = st["heads"][hh]
                        ob, obuf = bank()
                        obs.append((ob, obuf))
                        for idx_, j in enumerate(js):
                            kp = sg.Lt + 128 * ti - 512 + 128 * j
                            nk = nks[j]
                            vt = vt0 + kp // 128
                            mm(ob[:, 0:128], VV[0:nk, bf, vt, :], PT[0:nk, c, j * 128:(j + 1) * 128],
                               idx_ == 0, idx_ == len(js) - 1, reads=[VV_b[bf], PT_b[c]], writes=[obuf], inc=False)
                        for idx_, j in enumerate(js):
                            nk = nks[j]
                            mm(ob[:, 128:256], ones_bf[0:nk, :], PT[0:nk, c, j * 128:(j + 1) * 128],
                               idx_ == 0, idx_ == len(js) - 1, reads=[cstbf_b, PT_b[c]], writes=[obuf], inc=(idx_ == len(js) - 1))
                    st["obs"] = obs

                def att_NORM(sg, ti, st_all=st_all, hp=hp):
                    st = st_all[(sg.idx, ti)]
                    qc0 = st["qc0"]
                    for hh in range(2):
                        hr, scf, scb, c, nks = st["heads"][hh]
                        ob, obuf = st["obs"][hh]
                        P.op("act", ("activation", dict(out=rsum[hr, c, :], in_=ob[hr, 128:256], func=AF.Ln)), reads=[obuf], writes=[rsum_b[c]])
                        P.op("act", ("activation", dict(out=rsum[hr, c, :], in_=rsum[hr, c, :], func=AF.Exp, scale=-1.0)),
                             reads=[rsum_b[c]], writes=[rsum_b[c]])
                        P.op("dve", ("tensor_tensor", dict(out=OT[hr, hp, qc0:qc0 + 128], in0=ob[hr, 0:128], in1=rsum[hr, c, :], op=ALU.mult)),
                             reads=[obuf, rsum_b[c]], writes=[OT_b[hp]])
                    del st_all[(sg.idx, ti)]

                chunks = [(sg, ti) for sg in (segs if "band" not in cfg.skip else []) for ti in range(sg.ntiles)]
                if chunks:
                    att_S(*chunks[0])
                    att_AE(*chunks[0])
                for i_, ch_ in enumerate(chunks):
                    if i_ + 1 < len(chunks):
                        att_S(*chunks[i_ + 1])
                    att_PV(*ch_)
                    if i_ + 1 < len(chunks):
                        att_AE(*chunks[i_ + 1])
                    att_NORM(*ch_)
        for ch in range(2 if "oproj" not in cfg.skip else 0):
            w, wb = load_w(w_attn_out.ap()[li, :, ch * 512:(ch + 1) * 512].rearrange("(k p) c -> p k c", p=128), [128, 8, 512])
            for tt in range(ntt):
                pb, pbuf = bank()
                for kc in range(8):
                    mm(pb, OT[:, kc, tt * 128:(tt + 1) * 128], w[:, kc, :], kc == 0, kc == 7, reads=[wb, OT_b[kc]], writes=[pbuf])
                xs_ = xres[:, tt, ch * 512:(ch + 1) * 512]
                P.op("dve", ("scalar_tensor_tensor", dict(out=xs_, in0=xs_, scalar=ALPHA, in1=pb,
                                                                           op0=ALU.mult, op1=ALU.add)),
                     reads=[pbuf, xres_b[tt]], writes=[xres_b[tt]])

    def bc_mid(a, n):
        return bass.AP(a.tensor, a.offset, [list(a.ap[0]), [0, n]] + [list(x) for x in a.ap[1:]])

    def bc_last(a, n):
        return bass.AP(a.tensor, a.offset, [list(x) for x in a.ap] + [[0, n]])

    PWT = max(3 + TB, NSS * 131)
    a_ = arena0
    sv = {}
    for nm in ("dt", "dA", "cum", "ncum", "dw", "ecl", "ecum"):
        sv[nm], a_ = salloc([128, NTT, 32], F32, a_)
    sv_b = {nm: [Buf() for _ in range(NTT)] for nm in sv}
    mdw, a_ = salloc([128, NTT, 3, 32], F32, a_)
    X3, a_ = salloc([128, 2, 3, 256], BF16, a_)
    X3_b = [Buf(), Buf()]
    dAh, a_ = salloc([128, NTT, 32], BF16, a_)
    dAl, a_ = salloc([128, NTT, 32], BF16, a_)
    stmp32, a_ = salloc([128, 2, 32], F32, a_)
    stmp32_b = Buf()
    cw, a_ = salloc([128, 32, 4], F32, a_)
    cb, a_ = salloc([128, 32], F32, a_)
    vecb, a_ = salloc([128, 3, 32], F32, a_)
    abc, a_ = salloc([128, 32], F32, a_)
    par_b = Buf()
    zs, a_ = salloc([128, NTT, 256], F32, a_)
    zs_b = [Buf() for _ in range(NTT)]
    NSEGM = max(1, NSS)
    pre, a_ = salloc([128, 4, PWT], BF16, a_)
    pre_b = [Buf() for _ in range(4)]
    dg, a_ = salloc([128, 16, 128], BF16, a_)
    dg_b = Buf()
    hal, a_ = salloc([128, 4, NSEGM, 3], F32, a_)
    hal_b = [Buf() for _ in range(4)]
    hout3, a_ = salloc([128, 4, NSEGM, 3], F32, a_)
    hout3_b = [Buf() for _ in range(4)]
    xcT, a_ = salloc([128, 4, NT], BF16, a_)
    xcT_b = [Buf() for _ in range(4)]
    tokm, a_ = salloc([128, NTT, 384], BF16, a_)
    tokm_b = [Buf() for _ in range(NTT)]
    ynT, a_ = salloc([128, 4, NT], BF16, a_)
    ynT_b = [Buf() for _ in range(NTT)]
    nwg, a_ = salloc([128, 256], F32, a_)
    nwg_b = Buf()
    cbm, a_ = salloc([128, 2, 128], F32, a_)
    cbm_b = [Buf(), Buf()]
    Rg, a_ = salloc([128, 2, 512], F32, a_)
    Rg_b = [Buf(), Buf()]
    eh, a_ = salloc([128, 2, 512], F32, a_)
    eh_b = [Buf(), Buf()]
    MT, a_ = salloc([128, 2, 512], BF16, a_)
    MT_b = [Buf(), Buf()]
    Xdt, a_ = salloc([128, 2, 256], BF16, a_)
    Xdt_b = [Buf(), Buf()]
    Xw, a_ = salloc([128, 2, 256], BF16, a_)
    Xw_b = [Buf(), Buf()]
    xsD, a_ = salloc([128, 2, 256], BF16, a_)
    xsD_b = [Buf(), Buf()]
    Ysb, a_ = salloc([128, 2, 256], F32, a_)
    Ysb_b = [Buf(), Buf()]
    junk, a_ = salloc([128, 2, 256], F32, a_)
    junk_b = [Buf(), Buf()]
    yn, a_ = salloc([128, 2, 256], BF16, a_)
    yn_b = [Buf(), Buf()]
    gst, a_ = salloc([128, 2, 4], F32, a_)
    gst_b = [Buf(), Buf()]
    HsA, a_ = salloc([128, max(1, NSS), 256], F32, a_)
    HsA_b = [Buf() for _ in range(max(1, NSS))]
    HbA, a_ = salloc([128, max(1, NSS), 256], BF16, a_)
    HbA_b = [Buf() for _ in range(max(1, NSS))]
    hld, a_ = salloc([128, max(1, NSS), 2, 128], F32, a_)
    hld_b = [Buf() for _ in range(max(1, NSS))]
    hout, a_ = salloc([128, 2, 128], F32, a_)
    hout_b = Buf()
    ssm_end = a_
    hscr_gb = [[Buf() for _ in range(8)] for _ in range(2)]
    cscr_gb = [[Buf() for _ in range(8)] for _ in range(2)]
    tri_f = cst_f[:, 1, :]
    ones_f = cst_f[:, 2, :]
    ident_f = cst_f[:, 0, :]

    def ssm_layer(li, kind, segs, ntt):
        nt = ntt * 128
        P.dma("sp", cw, conv_wT.ap()[li], writes=[par_b], arena=True)
        P.dma("sp", cb, conv_bT.ap()[li], writes=[par_b], arena=True)
        P.dma("sp", vecb, bass.AP(ssm_vec, li * 96, [[0, 128], [1, 96]]), writes=[par_b], arena=True)
        P.op("act", ("activation", dict(out=abc, in_=vecb[:, 1, :], func=AF.Exp)), reads=[par_b], writes=[par_b])
        P.op("dve", ("tensor_scalar", dict(out=abc, in0=abc, scalar1=-1.0, scalar2=None, op0=ALU.mult)), reads=[par_b], writes=[par_b])
        wdt, wdt_b = load_w(w_ssm_in.ap()[li, :, 6144:6176].rearrange("(k p) c -> p k c", p=128), [128, 8, 32])
        tile_nv = {}
        for sg in segs:
            for ti in range(sg.ntiles):
                tile_nv[sg.tile0 + ti] = sg.nvalid
        for tt in range(ntt):
            nv = tile_nv[tt]
            pb, pbuf = bank()
            for kc in range(8):
                mm(pb[:, 0:32], xT[:, kc, tt * 128:(tt + 1) * 128], wdt[:, kc, :], kc == 0, kc == 7, reads=[wdt_b, xT_b[tt]], writes=[pbuf])
            dt_, dA_, cum_, ncum_, dw_, ecl_ = (sv[k][:, tt, :] for k in ("dt", "dA", "cum", "ncum", "dw", "ecl"))
            P.op("dve", ("tensor_tensor", dict(out=dt_, in0=pb[:, 0:32], in1=vecb[:, 0, :], op=ALU.add)),
                 reads=[pbuf, par_b], writes=[sv_b["dt"][tt]])
            P.op("act", ("activation", dict(out=dt_, in_=dt_, func=AF.Exp)), reads=[sv_b["dt"][tt]], writes=[sv_b["dt"][tt]])
            P.op("act", ("activation", dict(out=dt_, in_=dt_, func=AF.Ln, bias=onec[:, 0:1])), reads=[sv_b["dt"][tt], epsc_b], writes=[sv_b["dt"][tt]])
            P.op("dve", ("tensor_tensor", dict(out=dA_, in0=dt_, in1=abc, op=ALU.mult)), reads=[sv_b["dt"][tt], par_b], writes=[sv_b["dA"][tt]])
            P.op("dve", ("tensor_copy", dict(out=dAh[:, tt, :], in_=dA_)), reads=[sv_b["dA"][tt]], writes=[sv_b["dA"][tt]])
            P.op("dve", ("tensor_tensor", dict(out=stmp32[:, 0, :], in0=dA_, in1=dAh[:, tt, :], op=ALU.subtract)),
                 reads=[sv_b["dA"][tt]], writes=[stmp32_b])
            P.op("dve", ("tensor_copy", dict(out=dAl[:, tt, :], in_=stmp32[:, 0, :])), reads=[stmp32_b], writes=[sv_b["dA"][tt]])
            pb2, pbuf2 = bank()
            mm(pb2[:, 0:32], tri_f, dA_, True, True, reads=[cst_b, sv_b["dA"][tt]], writes=[pbuf2], inc=False)
            mm(pb2[:, 32:64], ones_f[0:nv, :], sv["dA"][0:nv, tt, :], True, True, reads=[cst_b, sv_b["dA"][tt]], writes=[pbuf2])
            P.op("act", ("activation", dict(out=cum_, in_=pb2[:, 0:32], func=AF.Copy)), reads=[pbuf2], writes=[sv_b["cum"][tt]])
            P.op("act", ("activation", dict(out=ncum_, in_=pb2[:, 0:32], func=AF.Copy, scale=-1.0)), reads=[pbuf2], writes=[sv_b["ncum"][tt]])
            P.op("act", ("activation", dict(out=ecl_, in_=pb2[:, 32:64], func=AF.Exp)), reads=[pbuf2], writes=[sv_b["ecl"][tt]])
            P.op("act", ("activation", dict(out=sv["ecum"][:, tt, :], in_=pb2[:, 0:32], func=AF.Exp)), reads=[pbuf2], writes=[sv_b["ecum"][tt]])
            P.op("dve", ("tensor_tensor", dict(out=dw_, in0=pb2[:, 32:64], in1=cum_, op=ALU.subtract)),
                 reads=[pbuf2, sv_b["cum"][tt]], writes=[sv_b["dw"][tt]])
            P.op("dve", ("tensor_scalar", dict(out=dw_, in0=dw_, scalar1=0.0, scalar2=None, op0=ALU.min)), reads=[sv_b["dw"][tt]], writes=[sv_b["dw"][tt]])
            P.op("act", ("activation", dict(out=dw_, in_=dw_, func=AF.Exp)), reads=[sv_b["dw"][tt]], writes=[sv_b["dw"][tt]])
            P.op("dve", ("tensor_tensor", dict(out=dw_, in0=dw_, in1=dt_, op=ALU.mult)), reads=[sv_b["dw"][tt], sv_b["dt"][tt]], writes=[sv_b["dw"][tt]])
            P.op("dve", ("tensor_copy", dict(out=mdw[:, tt, 0, :], in_=dt_)), reads=[sv_b["dt"][tt]], writes=[sv_b["dw"][tt]])
            P.op("dve", ("tensor_copy", dict(out=mdw[:, tt, 1, :], in_=dw_)), reads=[sv_b["dw"][tt]], writes=[sv_b["dw"][tt]])
            P.op("dve", ("tensor_copy", dict(out=mdw[:, tt, 2, :], in_=vecb[:, 2, :])), reads=[par_b], writes=[sv_b["dw"][tt]])

        for g in range(NG):
            i_ = wslot[0]
            wslot[0] = (i_ + 1) % NSLOT
            wa = wring[i_][:, 0:4096].rearrange("p (a b c) -> p a b c", a=8, b=2)
            wa_b = wring_b[i_]
            for which, c0 in ((0, g * 256), (1, DIN + g * 256)):
                P.dma("pool", wa[:, :, which, :], bass.AP(w_ssm_in, li * D * WIN_COLS + c0, [[WIN_COLS, 128], [128 * WIN_COLS, 8], [1, 256]]),
                      writes=[wa_b])
            i_ = wslot[0]
            wslot[0] = (i_ + 1) % NSLOT
            wbc = wring[i_][:, 0:2048].rearrange("p (a b c) -> p a b c", a=8, b=2)
            wbc_b = wring_b[i_]
            for which, c0 in ((0, 2 * DIN + g * 128), (1, 2 * DIN + 1024 + g * 128)):
                P.dma("pool", wbc[:, :, which, :], bass.AP(w_ssm_in, li * D * WIN_COLS + c0, [[WIN_COLS, 128], [128 * WIN_COLS, 8], [1, 128]]),
                      writes=[wbc_b])
            P.dma("sp", nwg, bass.AP(ssm_norm_w, li * DIN + g * 256, [[0, 128], [1, 256]]), writes=[nwg_b], arena=True)
            gfts = [2 * g, 2 * g + 1, 16 + g, 24 + g]
            ch0s = [g * 256, g * 256 + 128, DIN + g * 128, DIN + 1024 + g * 128]

            for sg in segs:
                if sg.Lt == 0:
                    continue
                if sg.kind == "s":
                    h_src, h_rb = state_ssm.ap()[li, sg.s], []
                else:
                    h_src, h_rb = hscr.ap()[li], [hscr_gb[li][g]]
                P.dma("sp", hld[:, sg.idx, :, :], h_src[g * 256:(g + 1) * 256, :].rearrange("(k p) n -> p k n", p=128),
                      reads=h_rb, writes=[hld_b[sg.idx]], arena=True)
            for tt in range(ntt):
                pb, pbuf = bank()
                for kc in range(8):
                    mm(pb[:, 0:256], xT[:, kc, tt * 128:(tt + 1) * 128], wa[:, kc, 0, :], kc == 0, kc == 7, reads=[wa_b, xT_b[tt]], writes=[pbuf])
                P.op("act", ("activation", dict(out=zs[:, tt, :], in_=pb[:, 0:256], func=AF.Silu)), reads=[pbuf], writes=[zs_b[tt]])
            for fi in range(4):
                for k in range(4):
                    P.op("dve", ("tensor_scalar", dict(out=dg[:, fi * 4 + k, :], in0=ident_bf, scalar1=cw[:, gfts[fi], k:k + 1], scalar2=None,
                                                       op0=ALU.mult)), reads=[cstbf_b, par_b], writes=[dg_b])
            for sg in segs:
                off = sg.idx * 131 if sg.kind == "s" else 0
                for fi in range(4):
                    if sg.Lt == 0:
                        P.op("dve", ("memset", dict(ap=pre[:, fi, off:off + 3], constant=0.0)), writes=[pre_b[fi]])
                    else:
                        if sg.kind == "s":
                            src_t, base, rb = state_conv, (li * NSS + sg.s) * 3 * CCH, []
                        else:
                            src_t, base, rb = cscr, li * 3 * CCH, [cscr_gb[li][g]]
                        P.dma("sp", hal[:, fi, sg.idx, :], bass.AP(src_t, base + ch0s[fi], [[1, 128], [CCH, 3]]),
                              reads=rb, writes=[hal_b[fi]], arena=True, slow=True)
                        P.op("dve", ("tensor_copy", dict(out=pre[:, fi, off:off + 3], in_=hal[:, fi, sg.idx, :])),
                             reads=[hal_b[fi]], writes=[pre_b[fi]])
            for fi in range(4):
                for t0 in range(0, nt, 512):
                    tn = min(512, nt - t0)
                    pb, pbuf = bank()
                    for kc in range(8):
                        lw = wa[:, kc, 1, fi * 128:(fi + 1) * 128] if fi < 2 else wbc[:, kc, fi - 2, :]
                        mm(pb[:, 0:tn], lw, xT[:, kc, t0:t0 + tn], kc == 0, kc == 7,
                           reads=[wa_b if fi < 2 else wbc_b] + xT_b[t0 // 128:(t0 + tn) // 128], writes=[pbuf])
                    if kind == "p":
                        dst, src = pre[:, fi, 3 + t0:3 + t0 + tn], pb[:, 0:tn]
                        if t0 + tn == nt:
                            P.op("dve", ("tensor_copy", dict(out=hout3[:, fi, 0, :], in_=pb[:, tn - 3:tn])), reads=[pbuf], writes=[hout3_b[fi]])
                    else:
                        n_sg, s0 = tn // 128, t0 // 128
                        dst = pre[:, fi, s0 * 131:(s0 + n_sg) * 131].rearrange("p (s c) -> p s c", c=131)[:, :, 3:131]
                        src = pb[:, 0:tn].rearrange("p (s c) -> p s c", c=128)
                        P.op("dve", ("tensor_copy", dict(out=hout3[:, fi, s0:s0 + n_sg, :], in_=src[:, :, DSEQ - 3:DSEQ])),
                             reads=[pbuf], writes=[hout3_b[fi]])
                    copy_op("act", dst, src, [pbuf], [pre_b[fi]])
            for sg in segs:
                if sg.kind == "s":
                    dst_t, base, wbf = conv_sample, (li * NSS + sg.s) * 3 * CCH, []
                elif sg.final:
                    dst_t, base, wbf = conv_prompt, (li * NPS + sg.s) * 3 * CCH, []
                else:
                    dst_t, base, wbf = cscr, li * 3 * CCH, [cscr_gb[li][g]]
                for fi in range(4):
                    P.dma("sp", bass.AP(dst_t, base + ch0s[fi], [[1, 128], [CCH, 3]]), hout3[:, fi, sg.idx, :],
                          reads=[hout3_b[fi]], writes=wbf, slow=True)
            for fi in range(4):
                gf = gfts[fi]
                for t0 in range(0, nt, 512):
                    tn = min(512, nt - t0)
                    pb, pbuf = bank()
                    for k in range(4):
                        if kind == "p":
                            rhs_ = pre[:, fi, t0 + k:t0 + k + tn]
                            out_ = pb[:, 0:tn]
                        else:
                            n_sg, s0 = tn // 128, t0 // 128
                            rhs_ = pre[:, fi, s0 * 131:(s0 + n_sg) * 131].rearrange("p (s c) -> p s c", c=131)[:, :, k:k + 128]
                            out_ = pb[:, 0:tn].rearrange("p (s c) -> p s c", c=128)
                        mm(out_, dg[:, fi * 4 + k, :], rhs_, k == 0, k == 3, reads=[dg_b, pre_b[fi]], writes=[pbuf])
                    P.op("act", ("activation", dict(out=xcT[:, fi, t0:t0 + tn], in_=pb[:, 0:tn], func=AF.Silu, bias=cb[:, gf:gf + 1])),
                         reads=[pbuf, par_b], writes=[xcT_b[fi]])
            for tt in range(ntt):
                pb, pbuf = bank()
                pbv = pb.bitcast(BF16)
                for fi in range(3):
                    P.op("pe", ("transpose", dict(out=pbv[:, fi * 128:(fi + 1) * 128], in_=xcT[:, fi, tt * 128:(tt + 1) * 128], identity=ident_bf)),
                         reads=[xcT_b[fi], cstbf_b], writes=[pbuf], inc=(fi == 2))
                copy_op(evac_engine(), tokm[:, tt, :], pbv[:, 0:384], [pbuf], [tokm_b[tt]])
            for sg in segs:
                Hs, Hs_b, Hb, Hb_b = HsA[:, sg.idx, :], HsA_b[sg.idx], HbA[:, sg.idx, :], HbA_b[sg.idx]
                if sg.kind == "s":
                    h_src, h_rb = state_ssm.ap()[li, sg.s], []
                else:
                    h_src, h_rb = hscr.ap()[li], [hscr_gb[li][g]]
                if sg.Lt == 0:
                    P.op("dve", ("memset", dict(ap=Hs, constant=0.0)), writes=[Hs_b])
                else:
                    pb, pbuf = bank()
                    for k2 in range(2):
                        P.op("pe", ("transpose", dict(out=pb[:, k2 * 128:(k2 + 1) * 128], in_=hld[:, sg.idx, k2, :], identity=ident_f)),
                             reads=[hld_b[sg.idx], cst_b], writes=[pbuf], inc=(k2 == 1))
                    P.op("dve", ("tensor_copy", dict(out=Hs, in_=pb[:, 0:256])), reads=[pbuf], writes=[Hs_b])
                P.op("act", ("activation", dict(out=Hb, in_=Hs, func=AF.Copy)), reads=[Hs_b], writes=[Hb_b])
            h4 = slice(4 * g, 4 * g + 4)

            def ssd_iter(c1, ca, cb_, cc, h4=h4):
                t1 = ta = tb = tc = None
                if c1 is not None:
                    sg1, t1, k1 = c1
                if cb_ is not None:
                    sgb, tb, kb = cb_
                if cc is not None:
                    sgc, tc, kc_ = cc
                if ca is not None:
                    sga, ta, ka = ca
                    Hs, Hs_b, Hb, Hb_b = HsA[:, sga.idx, :], HsA_b[sga.idx], HbA[:, sga.idx, :], HbA_b[sga.idx]
                    tt = sga.tile0 + ta
                    nv = sga.nvalid
                    p_ = ka % 2
                    cs = slice(tt * 128, (tt + 1) * 128)
                    xs3 = tokm[:, tt, 0:256].rearrange("p (h d) -> p h d", h=4)
                    eh3 = eh[:, p_, :].rearrange("p (h t) -> p h t", h=4)
                    pb, pbuf = psum[:, 0 + p_, :], ps_b[0 + p_]
                    pc, pcbuf = psum[:, 2 + p_, :], ps_b[2 + p_]
                    py, pybuf = psum[:, 4 + p_, :], ps_b[4 + p_]
                if tb is not None:
                    ttb = sgb.tile0 + tb
                    q_ = kb % 2
                    pyb, pybbuf = psum[:, 4 + q_, :], ps_b[4 + q_]
                if tc is not None:
                    ttc = sgc.tile0 + tc
                    c_ = kc_ % 2
                    csc = slice(ttc * 128, (ttc + 1) * 128)
                    pt, ptbuf = psum[:, 6, :], ps_b[6]
                    ptv = pt.bitcast(BF16)
                    for k2 in range(2):
                        P.op("pe", ("transpose", dict(out=ptv[:, k2 * 128:(k2 + 1) * 128], in_=yn[:, c_, k2 * 128:(k2 + 1) * 128], identity=ident_bf)),
                             reads=[yn_b[c_], cstbf_b], writes=[ptbuf], inc=(k2 == 1))
                if t1 is not None:
                    tt1 = sg1.tile0 + t1
                    r_ = k1 % 2
                    cs1 = slice(tt1 * 128, (tt1 + 1) * 128)
                    mm(psum[:, 0 + r_, 0:128], xcT[:, 2, cs1], xcT[:, 3, cs1], True, True, reads=[xcT_b[2], xcT_b[3]], writes=[ps_b[0 + r_]])
                    pcb, pcbbuf = psum[:, 2 + r_, :], ps_b[2 + r_]
                    for h in range(4):
                        hc = 4 * g + h
                        for pi, dX in enumerate((dAh, dAl)):
                            P.op("pe", ("matmul", dict(out=pcb[:, h * 128:(h + 1) * 128], lhsT=bc_last(dX[:, tt1, hc:hc + 1], 128).rearrange("p a b -> p (a b)") if False else bass.AP(dX.tensor, dX[:, tt1, hc:hc + 1].offset, [list(dX[:, tt1, hc:hc + 1].ap[0]), [0, 128]]),
                                                       rhs=tri_bf, start=(pi == 0 and h == 0), stop=False, skip_group_check=True)),
                                 reads=[sv_b["dA"][tt1], negm_b], writes=[pcbbuf], inc=False)
                    pcb3 = pcb.rearrange("p (h t) -> p h t", h=4)
                    for dX in (dAh, dAl):
                        P.op("pe", ("matmul", dict(out=pcb3, lhsT=negtri_bf, rhs=bc_last(dX[:, tt1, h4], 128), start=False, stop=False,
                                                   skip_group_check=True)),
                             reads=[sv_b["dA"][tt1], negm_b], writes=[pcbbuf], inc=False)
                    P.op("pe", ("matmul", dict(out=pcb, lhsT=ident_bf, rhs=negm.rearrange("p h t -> p (h t)"), start=False, stop=True,
                                               skip_group_check=True)),
                         reads=[cstbf_b, negm_b], writes=[pcbbuf])
                if tb is not None:
                    for h in range(4):
                        P.op("act", ("activation", dict(out=junk[:, q_, h * 64:(h + 1) * 64], in_=pyb[:, 256 + h * 64:256 + (h + 1) * 64],
                                                        func=AF.Identity, scale=sv["ecum"][:, ttb, 4 * g + h:4 * g + h + 1])),
                             reads=[pybbuf, sv_b["ecum"][ttb]], writes=[junk_b[q_]])
                if ta is not None:
                    P.op("act", ("activation", dict(out=eh[:, p_, :], in_=pc, func=AF.Exp)), reads=[pcbuf], writes=[eh_b[p_]])
                    if ta > 0:
                        P.op("act", ("activation", dict(out=Hb, in_=Hs, func=AF.Copy)), reads=[Hs_b], writes=[Hb_b])
                    xs_ap = tokm[:, tt, 0:256]
                    in0_ = bass.AP(xs_ap.tensor, xs_ap.offset, [list(xs_ap.ap[0]), [0, 3], [64, 4], [1, 64]])
                    m_ap = mdw[:, tt, :, 4 * g:4 * g + 4]
                    in1_ = bass.AP(m_ap.tensor, m_ap.offset, [list(m_ap.ap[0]), [32, 3], [1, 4], [0, 64]])
                    P.op("dve", ("tensor_tensor", dict(out=X3[:, p_, :, :].rearrange("p j (h d) -> p j h d", h=4), in0=in0_, in1=in1_, op=ALU.mult)),
                         reads=[tokm_b[tt], sv_b["dw"][tt]], writes=[X3_b[p_]])
                if tc is not None:
                    copy_op("act", ynT[:, 2 * (g % 2):2 * (g % 2) + 2, csc], ptv[:, 0:256].rearrange("p (k t) -> p k t", k=2), [ptbuf], [ynT_b[ttc]])
                if tb is not None:
                    P.op("dve", ("tensor_tensor", dict(out=Ysb[:, q_, :], in0=pyb[:, 0:256], in1=junk[:, q_, :], op=ALU.add)),
                         reads=[pybbuf, junk_b[q_]], writes=[Ysb_b[q_]])
                    P.op("dve", ("tensor_tensor", dict(out=Ysb[:, q_, :], in0=Ysb[:, q_, :], in1=zs[:, ttb, :], op=ALU.mult)),
                         reads=[Ysb_b[q_], zs_b[ttb]], writes=[Ysb_b[q_]])
                    P.op("act", ("activation", dict(out=junk[:, q_, :], in_=Ysb[:, q_, :], func=AF.Square, accum_out=gst[:, q_, 0:1])),
                         reads=[Ysb_b[q_]], writes=[junk_b[q_], gst_b[q_]])
                    P.op("act", ("activation", dict(out=gst[:, q_, 1:2], in_=gst[:, q_, 0:1], func=AF.Ln, scale=1.0 / 256.0, bias=epsc[:, 1:2])),
                         reads=[gst_b[q_], epsc_b], writes=[gst_b[q_]])
                    P.op("act", ("activation", dict(out=gst[:, q_, 1:2], in_=gst[:, q_, 1:2], func=AF.Exp, scale=-0.5)),
                         reads=[gst_b[q_]], writes=[gst_b[q_]])
                if ta is not None:
                    P.op("dve", ("tensor_tensor", dict(out=MT[:, p_, :].rearrange("p (h t) -> p h t", h=4), in0=eh3,
                                                       in1=bc_mid(pb[:, 0:128], 4), op=ALU.mult)),
                         reads=[eh_b[p_], pbuf], writes=[MT_b[p_]])
                    for h in range(4):
                        hs_ = slice(h * 64, (h + 1) * 64)
                        mm(py[:, hs_], MT[:, p_, h * 128:(h + 1) * 128], X3[:, p_, 0, hs_], True, False,
                           reads=[MT_b[p_], X3_b[p_]], writes=[pybuf], inc=False)
                        mm(py[:, hs_], ident_bf, X3[:, p_, 2, hs_], False, True,
                           reads=[cstbf_b, X3_b[p_]], writes=[pybuf], inc=False)
                    mm(py[:, 256:512], xcT[:, 3, cs], Hb, True, True, reads=[xcT_b[3], Hb_b], writes=[pybuf])
                    ph, phbuf = psum[:, 7, :], ps_b[7]
                    mm(ph[:, 0:256], tokm[0:nv, tt, 256:384], X3[0:nv, p_, 1, :], True, True, reads=[tokm_b[tt], X3_b[p_]], writes=[phbuf])
                if tb is not None:
                    P.op("dve", ("scalar_tensor_tensor", dict(out=yn[:, q_, :], in0=Ysb[:, q_, :], scalar=gst[:, q_, 1:2], in1=nwg,
                                                              op0=ALU.mult, op1=ALU.mult)),
                         reads=[Ysb_b[q_], gst_b[q_], nwg_b], writes=[yn_b[q_]])
                if ta is not None:
                    Hs3 = Hs.rearrange("p (h d) -> p h d", h=4)
                    P.op("dve", ("tensor_tensor", dict(out=Hs3, in0=Hs3, in1=bc_last(sv["ecl"][:, tt, h4], 64), op=ALU.mult)),
                         reads=[Hs_b, sv_b["ecl"][tt]], writes=[Hs_b])
                    P.op("dve", ("tensor_tensor", dict(out=Hs, in0=Hs, in1=ph[:, 0:256], op=ALU.add)), reads=[Hs_b, phbuf], writes=[Hs_b])

            chunks = []
            for sg in segs:
                for ti in range(sg.ntiles):
                    chunks.append((sg, ti, len(chunks)))
            n_ = len(chunks)
            rng = lambda v: chunks[v] if 0 <= v < n_ else None
            for k in range(n_ + 3):
                ssd_iter(rng(k), rng(k - 1), rng(k - 2), rng(k - 3))
            for sg in segs:
                Hs, Hs_b = HsA[:, sg.idx, :], HsA_b[sg.idx]
                if sg.kind == "s":
                    h_dst, h_wb = ssm_sample.ap()[li, sg.s], []
                elif sg.final:
                    h_dst, h_wb = ssm_prompt.ap()[li, sg.s], []
                else:
                    h_dst, h_wb = hscr.ap()[li], [hscr_gb[li][g]]
                pb, pbuf = bank()
                for k2 in range(2):
                    P.op("pe", ("transpose", dict(out=pb[:, k2 * 128:(k2 + 1) * 128], in_=Hs[:, k2 * 128:(k2 + 1) * 128], identity=ident_f)),
                         reads=[Hs_b, cst_b], writes=[pbuf], inc=(k2 == 1))
                P.op("dve", ("tensor_copy", dict(out=hout, in_=pb[:, 0:256].rearrange("p (k n) -> p k n", k=2))), reads=[pbuf], writes=[hout_b])
                P.dma("sp", h_dst[g * 256:(g + 1) * 256, :].rearrange("(k p) n -> p k n", p=128), hout, reads=[hout_b], writes=h_wb)
            if g % 2 == 1:
                wo, wo_b = load_w(w_ssm_out.ap()[li, (g - 1) * 256:(g + 1) * 256, :].rearrange("(k p) c -> p k c", p=128), [128, 4, 1024])
                for tt in range(ntt):
                    for ch in range(2):
                        pb, pbuf = bank()
                        for k4 in range(4):
                            mm(pb, ynT[:, k4, tt * 128:(tt + 1) * 128], wo[:, k4, ch * 512:(ch + 1) * 512], k4 == 0, k4 == 3,
                               reads=[wo_b, ynT_b[tt]], writes=[pbuf])
                        xs_ = xres[:, tt, ch * 512:(ch + 1) * 512]
                        P.op("dve", ("scalar_tensor_tensor", dict(out=xs_, in0=xs_, scalar=(ALPHA if g == 1 else 1.0), in1=pb,
                                                                  op0=ALU.mult, op1=ALU.add)), reads=[pbuf, xres_b[tt]], writes=[xres_b[tt]])

    blocks = []
    nb = SEQ // TB
    for s_ in range(NPS):
        for b in range(nb):
            blocks.append(("p", [Seg("p", s_, b * TB, TB // 128, 0, 128, 0 if b == 0 else WIN, b == nb - 1, 0)]))
    blocks.append(("s", [Seg("s", s_, 0, 1, s_, DSEQ, WIN, True, s_) for s_ in range(NSS)]))

    for kind, segs in blocks:
        ntt = sum(sg.ntiles for sg in segs)
        nt = ntt * 128
        for sg in segs:
            for ti in range(sg.ntiles):
                tt = sg.tile0 + ti
                if kind == "p":
                    P.dma("sp", xres[:, tt, :], x_prompt.ap()[sg.s, sg.t0 + ti * 128: sg.t0 + (ti + 1) * 128, :],
                          writes=[xres_b[tt]])
                else:
                    P.op("dve", ("memset", dict(ap=xres[:, tt, :], constant=0.0)), writes=[xres_b[tt]])
                    P.dma("sp", xres[0:DSEQ, tt, :], x_sample.ap()[sg.s, :, :], writes=[xres_b[tt]])
                make_xT(tt)
        for layer in range(cfg.layers):
            P.barrier()
            mixed = False
            if layer % 2 == 0 and cfg.stage in ("attn", "full"):
                attention_layer(layer // 2, kind, segs, ntt)
                mixed = True
            if layer % 2 == 1 and cfg.stage in ("ssm", "full"):
                ssm_layer(layer // 2, kind, segs, ntt)
                mixed = True
            P.barrier()
            load_ln(layer, 0)
            ln_phase(list(range(ntt)), not mixed, make_xT)
            P.barrier()
            mlp(layer, nt)
            load_ln(layer, 1)
            last = layer == cfg.layers - 1
            tile_seg = {}
            for sg in segs:
                for ti in range(sg.ntiles):
                    tile_seg[sg.tile0 + ti] = (sg, ti)

            def store_y(tt, tile_seg=tile_seg, kind=kind):
                sg, ti = tile_seg[tt]
                if kind == "p":
                    P.dma("sp", y_prompt.ap()[sg.s, sg.t0 + ti * 128: sg.t0 + (ti + 1) * 128, :], xres[:, tt, :],
                          reads=[xres_b[tt]])
                else:
                    P.dma("sp", y_sample.ap()[sg.s, :, :], xres[0:DSEQ, tt, :], reads=[xres_b[tt]])

            ln_phase(list(range(ntt)), False, store_y if last else make_xT)
            P.barrier()

    P.finish()
    P.emit()
    return nc


def _consts():
    c = np.zeros((6, 128, 128), np.float32)
    i = np.arange(128)
    c[0] = np.eye(128, dtype=np.float32)
    c[1] = (i[:, None] <= i[None, :]).astype(np.float32)
    c[2] = 1.0
    same = (i[:, None] // 64) == (i[None, :] // 64)
    c[3] = c[1] * same
    c[4] = same.astype(np.float32)
    return c


def _bias_index():
    k = np.arange(128)[:, None]
    col = np.arange(640)[None, :]
    j, r = col // 128, col % 128
    q = 128 * (4 - j) + r
    idx = np.minimum(q - k, 256) + 256
    valid = np.where(k < 64, q < 576, q >= 64)
    mask = np.where(valid, 0.0, -1e30).astype(np.float32)
    return idx, mask


def prepare(inputs, cfg):
    f = lambda a: np.ascontiguousarray(np.asarray(a, dtype=np.float32))
    NPS, NSS = cfg.NPS, cfg.NSS
    rel = f(inputs["rel_bias"])
    idx, amask = _bias_index()
    biasT = np.ascontiguousarray(rel[:, :, idx])
    lnp = np.ascontiguousarray(np.stack([f(inputs["ln_mix_g"]), f(inputs["ln_mix_b"]),
                                         f(inputs["ln_ff_g"]), f(inputs["ln_ff_b"])], axis=1))
    conv_wT = np.ascontiguousarray(f(inputs["conv_w"]).reshape(2, 4, 32, 128).transpose(0, 3, 2, 1))
    conv_bT = np.ascontiguousarray(f(inputs["conv_b"]).reshape(2, 32, 128).transpose(0, 2, 1))
    ssm_vec = np.ascontiguousarray(np.stack([f(inputs["dt_bias"]), f(inputs["a_log"]), f(inputs["d_skip"])], axis=1))
    shared = {
        "w_qkv": f(inputs["w_qkv"]), "w_attn_out": f(inputs["w_attn_out"]), "w_ssm_in": f(inputs["w_ssm_in"]),
        "w_ssm_out": f(inputs["w_ssm_out"]), "w_ff_up": f(inputs["w_ff_up"]), "w_ff_down": f(inputs["w_ff_down"]),
        "biasT": biasT, "lnp": lnp, "conv_wT": conv_wT, "conv_bT": conv_bT, "ssm_vec": ssm_vec,
        "ssm_norm_w": f(inputs["ssm_norm_w"]), "consts": _consts(), "amask": amask,
    }
    xp, xs = f(inputs["x_prompt"]), f(inputs["x_sample"])
    ck, cv = f(inputs["cache_k"]), f(inputs["cache_v"])
    ss, sc = f(inputs["state_ssm"]), f(inputs["state_conv"])
    maps = []
    for c in range(NCORES):
        m = dict(shared)
        m["x_prompt"] = np.ascontiguousarray(xp[c * NPS:(c + 1) * NPS])
        m["x_sample"] = np.ascontiguousarray(xs[c * NSS:(c + 1) * NSS])
        m["cache_k"] = np.ascontiguousarray(ck[:, c * NSS:(c + 1) * NSS].reshape(2, NSS, WIN, D))
        m["cache_v"] = np.ascontiguousarray(cv[:, c * NSS:(c + 1) * NSS].reshape(2, NSS, WIN, D))
        m["state_ssm"] = np.ascontiguousarray(ss[:, c * NSS:(c + 1) * NSS].reshape(2, NSS, DIN, DST))
        m["state_conv"] = np.ascontiguousarray(sc[:, c * NSS:(c + 1) * NSS])
        maps.append(m)
    return maps


def assemble(results, cfg):
    NPS, NSS, SEQ, DSEQ, KEEP = cfg.NPS, cfg.NSS, cfg.SEQ, cfg.DSEQ, cfg.KEEP
    cat0 = lambda k: np.concatenate([r[k] for r in results], axis=0)
    cat1 = lambda k: np.concatenate([r[k] for r in results], axis=1)
    return (
        cat0("y_prompt"), cat0("y_sample"),
        cat1("k_prompt").reshape(2, NCORES * NPS, KEEP, NH, HD),
        cat1("v_prompt").reshape(2, NCORES * NPS, KEEP, NH, HD),
        cat1("ssm_prompt").reshape(2, NCORES * NPS, 32, 64, DST),
        cat1("conv_prompt"),
        cat1("k_sample").reshape(2, NCORES * NSS, DSEQ, NH, HD),
        cat1("v_sample").reshape(2, NCORES * NSS, DSEQ, NH, HD),
        cat1("ssm_sample").reshape(2, NCORES * NSS, 32, 64, DST),
        cat1("conv_sample"),
    )


def run(inputs, cfg, trace=False):
    nc = build(cfg)
    maps = prepare(inputs, cfg)
    res = run_bass_kernel_spmd(nc, maps, core_ids=list(range(NCORES)), trace=trace)
    out = assemble(res.results, cfg)
    if trace:
        return out, res
    return out


def kernel(**inputs):
    cfg = Cfg(NPS=4, SEQ=2048, TB=1024, NSS=4)
    return run(inputs, cfg)
```

```python
import numpy as np
import concourse.bass as bass
import concourse.mybir as mybir
from concourse.bass_utils import run_bass_kernel_spmd

F32 = mybir.dt.float32
BF16 = mybir.dt.bfloat16
AF = mybir.ActivationFunctionType
ALU = mybir.AluOpType

D = 1024
NH = 16
HD = 64
WIN = 512
DFF = 4096
DIN = 2048
NG = 8
DST = 128
CCH = 4096
WIN_COLS = 6176
ALPHA = float(8 ** 0.25)
LN_EPS = 1e-5
RMS_EPS = 1e-5
NCORES = 8


class Buf:
    __slots__ = ("w", "r", "ps")

    def __init__(self, ps=False):
        self.w = None
        self.r = {}
        self.ps = ps


class Prog:
    def __init__(self, nc):
        self.nc = nc
        self.E = {"pe": nc.tensor, "act": nc.scalar, "dve": nc.vector, "pool": nc.gpsimd, "sp": nc.sync}
        self.ops = {e: [] for e in self.E}
        self.cnt = {e: 0 for e in self.E}
        self.sem = {e: nc.alloc_semaphore(name="sem_" + e) for e in self.E}
        self.known = {e: {} for e in self.E}
        self.nds = 40
        self.dsem = [nc.alloc_semaphore(name="dsem%d" % i) for i in range(self.nds)]
        self.dcnt = [0] * self.nds
        self.dpool = {"sp": list(range(0, 28)), "pool": list(range(28, 32)), "ring": list(range(32, 40))}
        self.dnext = {"sp": 0, "pool": 0, "ring": 0}
        self.phase = []

    def _semh(self, k):
        return self.sem[k] if isinstance(k, str) else self.dsem[k[1]]

    def _deps(self, reads, writes):
        deps = []
        for b in reads:
            deps.append(b.w)
        for b in writes:
            deps.append(b.w)
            deps.extend(b.r.items())
        return deps

    def _waits(self, eng, deps):
        need = {}
        for t in deps:
            if t is None:
                continue
            k, v = t
            if k == eng and eng == "pe":
                continue
            if need.get(k, 0) < v:
                need[k] = v
        out = []
        kn = self.known[eng]
        for k, v in need.items():
            if kn.get(k, 0) >= v:
                continue
            kn[k] = v
            out.append((self._semh(k), v))
        return out

    def _commit(self, tick, reads, writes):
        k, v = tick
        for b in reads:
            if b.r.get(k, 0) < v:
                b.r[k] = v
        for b in writes:
            b.w = tick
            b.r = {}

    def op(self, eng, fn, reads=(), writes=(), inc=True):
        psr = [b for b in reads if b.ps]
        if psr:
            writes = list(writes) + psr
        waits = self._waits(eng, self._deps(reads, writes))
        if inc:
            self.cnt[eng] += 1
            tick = (eng, self.cnt[eng])
        else:
            tick = (eng, self.cnt[eng] + 1)
        sem = self.sem[eng]

        def run(e, waits=waits, fn=fn, sem=sem, inc=inc):
            for s, v in waits:
                e.wait_ge(s, v)
            if isinstance(fn, tuple):
                r = getattr(e, fn[0])(**fn[1])
            else:
                r = fn(e)
            if inc:
                r.then_inc(sem, 1)

        self.ops[eng].append(run)
        self._commit(tick, reads, writes)

    def dma(self, eng, out, in_, reads=(), writes=(), arena=False, slow=False, ring=False):
        pk = "ring" if ring else eng
        pl = self.dpool[pk]
        i = pl[self.dnext[pk]]
        self.dnext[pk] = (self.dnext[pk] + 1) % len(pl)
        deps = self._deps(reads, writes)
        if arena:
            deps.extend(self.phase)
        if self.dcnt[i] > 0:
            deps.append((("d", i), self.dcnt[i]))
        waits = self._waits(eng, deps)
        self.dcnt[i] += 16
        tick = (("d", i), self.dcnt[i])
        sem = self.dsem[i]

        def run(e, waits=waits, sem=sem, out=out, in_=in_, slow=slow):
            for s, v in waits:
                e.wait_ge(s, v)
            if slow:
                e.dma_start(out=out, in_=in_, allow_slow_non_contiguous=True).then_inc(sem, 16)
            else:
                e.dma_start(out=out, in_=in_).then_inc(sem, 16)

        self.ops[eng].append(run)
        self._commit(tick, reads, writes)

    def barrier(self, engs=("pe", "act", "dve")):
        comp = ("pe", "act", "dve")
        self.phase = [(c, self.cnt[c]) for c in comp if self.cnt[c] > 0]
        for e in engs:
            deps = [(c, self.cnt[c]) for c in comp if c != e and self.cnt[c] > 0]
            deps += [(("d", i), self.dcnt[i]) for i in range(self.nds) if self.dcnt[i] > 0 and i not in self.dpool["ring"]]
            waits = self._waits(e, deps)
            if waits:
                def run(eh, waits=waits):
                    for s, v in waits:
                        eh.wait_ge(s, v)
                self.ops[e].append(run)

    def finish(self):
        deps = [(("d", i), self.dcnt[i]) for i in range(self.nds) if self.dcnt[i] > 0]
        deps += [(c, self.cnt[c]) for c in ("pe", "act", "dve", "pool") if self.cnt[c] > 0]
        waits = self._waits("sp", deps)

        def run(eh, waits=waits):
            for s, v in waits:
                eh.wait_ge(s, v)
        self.ops["sp"].append(run)

    def emit(self):
        nc = self.nc
        with nc.Block() as block:
            @block.sync
            def _(e):
                for f in self.ops["sp"]:
                    f(e)

            @block.tensor
            def _(e):
                for f in self.ops["pe"]:
                    f(e)

            @block.scalar
            def _(e):
                for f in self.ops["act"]:
                    f(e)

            @block.vector
            def _(e):
                for f in self.ops["dve"]:
                    f(e)

            @block.gpsimd
            def _(e):
                for f in self.ops["pool"]:
                    f(e)


class Cfg:
    def __init__(self, NPS, SEQ, TB, NSS, DSEQ=64, layers=4, debug=False):
        self.NPS, self.SEQ, self.TB, self.NSS, self.DSEQ = NPS, SEQ, TB, NSS, DSEQ
        self.layers = layers
        self.stage = "full"
        self.skip = set()
        assert SEQ % TB == 0 and TB % 128 == 0
        assert TB >= WIN or SEQ == TB
        assert DSEQ == 64
        self.NT = max(TB, NSS * 128)
        self.NTT = self.NT // 128
        self.KEEP = min(WIN, SEQ)


class Seg:
    def __init__(self, kind, s, t0, ntiles, tile0, nvalid, Lt, final, idx):
        self.kind, self.s, self.t0, self.ntiles, self.tile0 = kind, s, t0, ntiles, tile0
        self.nvalid, self.Lt, self.final, self.idx = nvalid, Lt, final, idx


def bc_ap(t, off, n, parts=128):
    return bass.AP(t, off, [[0, parts], [1, n]])


def build(cfg):
    nc = bass.Bass("TRN2", target_bir_lowering=False)
    P = Prog(nc)
    NPS, SEQ, TB, NSS, DSEQ = cfg.NPS, cfg.SEQ, cfg.TB, cfg.NSS, cfg.DSEQ
    NT, NTT, KEEP = cfg.NT, cfg.NTT, cfg.KEEP

    def din(name, shape):
        return nc.dram_tensor(name, list(shape), F32, kind="ExternalInput")

    def dout(name, shape):
        return nc.dram_tensor(name, list(shape), F32, kind="ExternalOutput")

    def dscr(name, shape):
        return nc.dram_tensor(name, list(shape), F32, kind="Internal")

    x_prompt = din("x_prompt", [NPS, SEQ, D])
    x_sample = din("x_sample", [NSS, DSEQ, D])
    cache_k = din("cache_k", [2, NSS, WIN, D])
    cache_v = din("cache_v", [2, NSS, WIN, D])
    state_ssm = din("state_ssm", [2, NSS, DIN, DST])
    state_conv = din("state_conv", [2, NSS, 3, CCH])
    w_qkv = din("w_qkv", [2, D, 3 * D])
    w_attn_out = din("w_attn_out", [2, D, D])
    w_ssm_in = din("w_ssm_in", [2, D, WIN_COLS])
    w_ssm_out = din("w_ssm_out", [2, DIN, D])
    w_ff_up = din("w_ff_up", [4, D, DFF])
    w_ff_down = din("w_ff_down", [4, DFF, D])
    biasT = din("biasT", [2, NH, 128, 640])
    lnp = din("lnp", [4, 4, D])
    conv_wT = din("conv_wT", [2, 128, 32, 4])
    conv_bT = din("conv_bT", [2, 128, 32])
    ssm_vec = din("ssm_vec", [2, 3, 32])
    ssm_norm_w = din("ssm_norm_w", [2, DIN])
    consts = din("consts", [6, 128, 128])
    amask = din("amask", [128, 640])

    y_prompt = dout("y_prompt", [NPS, SEQ, D])
    y_sample = dout("y_sample", [NSS, DSEQ, D])
    k_prompt = dout("k_prompt", [2, NPS, KEEP, D])
    v_prompt = dout("v_prompt", [2, NPS, KEEP, D])
    ssm_prompt = dout("ssm_prompt", [2, NPS, DIN, DST])
    conv_prompt = dout("conv_prompt", [2, NPS, 3, CCH])
    k_sample = dout("k_sample", [2, NSS, DSEQ, D])
    v_sample = dout("v_sample", [2, NSS, DSEQ, D])
    ssm_sample = dout("ssm_sample", [2, NSS, DIN, DST])
    conv_sample = dout("conv_sample", [2, NSS, 3, CCH])

    kscr = dscr("kscr", [2, WIN, D])
    vscr = dscr("vscr", [2, WIN, D])
    hscr = dscr("hscr", [2, DIN, DST])
    cscr = dscr("cscr", [2, 3, CCH])
    kscr_b = [Buf(), Buf()]
    vscr_b = [Buf(), Buf()]
    hscr_b = [Buf(), Buf()]
    cscr_b = [Buf(), Buf()]

    sb_off = [(int(nc.sbuf_base) + 63) // 64 * 64]
    sb_top = int(nc.sbuf_top)
    uid = [0]

    def salloc(shape, dtype, off=None):
        nbytes = int(np.prod(shape[1:])) * (4 if dtype == F32 else 2)
        nbytes = (nbytes + 31) // 32 * 32
        if off is None:
            off = sb_off[0]
            sb_off[0] += nbytes
        uid[0] += 1
        assert off + nbytes <= sb_top, ("SBUF overflow", off, nbytes, sb_top)
        t = nc.alloc_sbuf_tensor_at("t%d" % uid[0], list(shape), dtype, offset=off)
        return t.ap(), off + nbytes

    def fix(shape, dtype):
        return salloc(shape, dtype)[0]

    xres = fix([128, NTT, D], F32)
    xres_b = [Buf() for _ in range(NTT)]
    xT = fix([128, 8, NT], BF16)
    xT_b = [Buf() for _ in range(NTT)]
    NSLOT = 4
    wring = [fix([128, 4096], BF16) for _ in range(NSLOT)]
    wring_b = [Buf() for _ in range(NSLOT)]
    lngb = fix([128, 2, D], F32)
    lngb_b = Buf()
    cst_f = fix([128, 6, 128], F32)
    cst_b = Buf()
    ident_bf = fix([128, 128], BF16)
    ones_bf = fix([128, 128], BF16)
    cstbf_b = Buf()
    xbf = fix([128, 2, D], BF16)
    xbf_b = [Buf(), Buf()]
    lnst = fix([128, 4, 16], F32)
    lnst_b = [Buf() for _ in range(4)]
    epsc = fix([128, 4], F32)
    onec = epsc[:, 2:3]
    epsc_b = Buf()
    negm = fix([128, 4, 128], BF16)
    negm_b = Buf()
    tri_bf = fix([128, 128], BF16)
    negtri_bf = fix([128, 128], BF16)
    arena0 = sb_off[0]

    psum = nc.alloc_psum_tensor("psum", [128, 8, 512], F32).ap()
    ps_b = [Buf(ps=True) for _ in range(8)]
    ps_rr = [0]

    def bank():
        i = ps_rr[0]
        ps_rr[0] = (i + 1) % 4
        return psum[:, i, :], ps_b[i]

    pair_rr = [0]

    def bankpair():
        i = pair_rr[0]
        pair_rr[0] = (i + 1) % 2
        return psum[:, 4 + 2 * i:6 + 2 * i, :], ps_b[4 + 2 * i]

    wslot = [0]

    def load_w(src_ap, view_shape):
        i = wslot[0]
        wslot[0] = (i + 1) % NSLOT
        n = int(np.prod(view_shape[1:]))
        assert n <= 4096
        dst = wring[i][:, 0:n]
        if len(view_shape) == 3:
            dst = dst.rearrange("p (a b) -> p a b", a=view_shape[1])
        elif len(view_shape) == 4:
            dst = dst.rearrange("p (a b c) -> p a b c", a=view_shape[1], b=view_shape[2])
        P.dma("pool", dst, src_ap, writes=[wring_b[i]], ring=True)
        return dst, wring_b[i]

    def mm(out, lhsT, rhs, start, stop, reads, writes, inc=None):
        if inc is None:
            inc = stop
        P.op("pe", ("matmul", dict(out=out, lhsT=lhsT, rhs=rhs, start=start, stop=stop)),
             reads=reads, writes=writes, inc=inc)

    evac_rr = [0]

    def evac_engine():
        evac_rr[0] ^= 1
        return "act" if evac_rr[0] else "dve"

    def copy_op(eng, out, in_, reads, writes, scale=None):
        if eng == "act":
            if scale is None:
                P.op("act", ("activation", dict(out=out, in_=in_, func=AF.Copy)), reads=reads, writes=writes)
            else:
                P.op("act", ("activation", dict(out=out, in_=in_, func=AF.Copy, scale=float(scale))),
                     reads=reads, writes=writes)
        else:
            if scale is None:
                P.op("dve", ("tensor_copy", dict(out=out, in_=in_)), reads=reads, writes=writes)
            else:
                P.op("dve", ("tensor_scalar", dict(out=out, in0=in_, scalar1=float(scale), scalar2=None,
                                                      op0=ALU.mult)), reads=reads, writes=writes)

    P.dma("sp", cst_f, consts.ap().rearrange("c p f -> p c f"), writes=[cst_b])
    P.op("dve", ("tensor_copy", dict(out=ident_bf, in_=cst_f[:, 0, :])), reads=[cst_b], writes=[cstbf_b])
    P.op("dve", ("tensor_copy", dict(out=ones_bf, in_=cst_f[:, 2, :])), reads=[cst_b], writes=[cstbf_b])

    def make_xT(tt, nrows=128):
        sl = tt % 2
        P.op("act", ("activation", dict(out=xbf[:, sl, :], in_=xres[:, tt, :], func=AF.Copy)),
             reads=[xres_b[tt]], writes=[xbf_b[sl]])
        pb, pbuf = bank()
        pbv = pb.bitcast(BF16)
        for kc in range(8):
            P.op("pe", ("transpose", dict(out=pbv[:, kc * 128:(kc + 1) * 128],
                                                    in_=xbf[:, sl, kc * 128:(kc + 1) * 128], identity=ident_bf)),
                 reads=[xbf_b[sl], cstbf_b], writes=[pbuf], inc=(kc == 7))
        P.op("dve", ("tensor_copy", dict(out=xT[:, :, tt * 128:(tt + 1) * 128],
                                            in_=pbv.rearrange("p (k t) -> p k t", k=8))),
             reads=[pbuf], writes=[xT_b[tt]])

    def ln_s1(tt, prescale):
        sl = tt % 4
        st = lnst[:, sl, :]
        stb = lnst_b[sl]
        if prescale:
            P.op("dve", ("tensor_scalar", dict(out=xres[:, tt, :], in0=xres[:, tt, :], scalar1=ALPHA, scalar2=None, op0=ALU.mult)),
                 reads=[xres_b[tt]], writes=[xres_b[tt]])
        P.op("dve", ("bn_stats", dict(out=st[:, 0:6], in_=xres[:, tt, 0:512])), reads=[xres_b[tt]], writes=[stb])
        P.op("dve", ("bn_stats", dict(out=st[:, 6:12], in_=xres[:, tt, 512:1024])), reads=[xres_b[tt]], writes=[stb])
        P.op("dve", ("bn_aggr", dict(out=st[:, 12:14], in_=st[:, 0:12])), reads=[stb], writes=[stb])
        P.op("act", ("activation", dict(out=st[:, 14:15], in_=st[:, 13:14], func=AF.Ln, bias=epsc[:, 0:1])),
             reads=[stb, epsc_b], writes=[stb])
        P.op("act", ("activation", dict(out=st[:, 14:15], in_=st[:, 14:15], func=AF.Exp, scale=-0.5)), reads=[stb], writes=[stb])
        P.op("dve", ("scalar_tensor_tensor", dict(out=st[:, 15:16], in0=st[:, 12:13], scalar=-1.0, in1=st[:, 14:15],
                                                     op0=ALU.mult, op1=ALU.mult)), reads=[stb], writes=[stb])

    def ln_s2(tt):
        sl = tt % 4
        st = lnst[:, sl, :]
        stb = lnst_b[sl]
        xt = xres[:, tt, :]
        P.op("act", ("activation", dict(out=xt, in_=xt, func=AF.Identity, scale=st[:, 14:15], bias=st[:, 15:16])),
             reads=[stb, xres_b[tt]], writes=[xres_b[tt]])
        P.op("dve", ("tensor_tensor", dict(out=xt, in0=xt, in1=lngb[:, 0, :], op=ALU.mult)),
             reads=[xres_b[tt], lngb_b], writes=[xres_b[tt]])
        P.op("dve", ("tensor_tensor", dict(out=xt, in0=xt, in1=lngb[:, 1, :], op=ALU.add)),
             reads=[xres_b[tt], lngb_b], writes=[xres_b[tt]])

    def ln_phase(tiles, prescale, s3):
        n = len(tiles)
        for i in range(n + 2):
            if i < n:
                ln_s1(tiles[i], prescale)
            if 0 <= i - 1 < n:
                ln_s2(tiles[i - 1])
            if 0 <= i - 2 < n:
                s3(tiles[i - 2])

    P.op("dve", ("tensor_scalar", dict(out=negm, in0=bass.AP(cst_f.tensor, cst_f[:, 1, :].offset, [list(cst_f[:, 1, :].ap[0]), [0, 4], [1, 128]]),
                                       scalar1=30000.0, scalar2=-30000.0, op0=ALU.mult, op1=ALU.add)), reads=[cst_b], writes=[negm_b])
    P.op("dve", ("tensor_copy", dict(out=tri_bf, in_=cst_f[:, 1, :])), reads=[cst_b], writes=[negm_b])
    P.op("dve", ("tensor_scalar", dict(out=negtri_bf, in0=cst_f[:, 1, :], scalar1=-1.0, scalar2=None, op0=ALU.mult)), reads=[cst_b], writes=[negm_b])
    P.op("dve", ("memset", dict(ap=epsc[:, 0:1], constant=LN_EPS)), writes=[epsc_b])
    P.op("dve", ("memset", dict(ap=epsc[:, 1:2], constant=RMS_EPS)), writes=[epsc_b])
    P.op("dve", ("memset", dict(ap=epsc[:, 2:3], constant=1.0)), writes=[epsc_b])

    def load_ln(layer, which):
        off = (layer * 4 + which * 2) * D
        P.dma("sp", lngb, bass.AP(lnp, off, [[0, 128], [D, 2], [1, D]]), writes=[lngb_b])

    HG = 2
    HGW = DFF // HG
    hT, a_end = salloc([128, HGW // 128, NT], BF16, arena0)
    hT_b = Buf()
    rtmp, a_end = salloc([128, 2, 512], F32, a_end)
    rtmp_b = [Buf(), Buf()]
    mlp_end = a_end

    def mlp(layer, nt):
        ntt = nt // 128
        tgs = [(t0, min(512, nt - t0)) for t0 in range(0, nt, 512)]
        rr = 0
        for g in range(HG):
            for wt in range(HGW // 512):
                c0 = g * HGW + wt * 512
                w, wb = load_w(w_ff_up.ap()[layer, :, c0:c0 + 512].rearrange("(k p) c -> p k c", p=128), [128, 8, 512])
                for sub in range(4):
                    for (t0, tn) in tgs:
                        pb, pbuf = bank()
                        for kc in range(8):
                            mm(pb[:, 0:tn], w[:, kc, sub * 128:(sub + 1) * 128], xT[:, kc, t0:t0 + tn],
                               kc == 0, kc == 7, reads=[wb] + xT_b[t0 // 128:(t0 + tn + 127) // 128], writes=[pbuf])
                        sl = rr % 2
                        rr += 1
                        P.op("act", ("activation", dict(out=rtmp[:, sl, 0:tn], in_=pb[:, 0:tn], func=AF.Relu)),
                             reads=[pbuf], writes=[rtmp_b[sl]])
                        fi = wt * 4 + sub
                        P.op("dve", ("tensor_tensor", dict(
                            out=hT[:, fi, t0:t0 + tn], in0=rtmp[:, sl, 0:tn], in1=rtmp[:, sl, 0:tn], op=ALU.mult)),
                            reads=[rtmp_b[sl]], writes=[hT_b])
            for ch in range(2):
                ws = []
                for kh in range(HGW // 1024):
                    r0 = g * HGW + kh * 1024
                    ws.append(load_w(w_ff_down.ap()[layer, r0:r0 + 1024, ch * 512:(ch + 1) * 512]
                                     .rearrange("(k p) c -> p k c", p=128), [128, 8, 512]))
                for tt in range(ntt):
                    pb, pbuf = bank()
                    nk = (HGW // 1024) * 8
                    for ki in range(nk):
                        w, wb = ws[ki // 8]
                        mm(pb, hT[:, ki, tt * 128:(tt + 1) * 128], w[:, ki % 8, :], ki == 0, ki == nk - 1,
                           reads=[wb, hT_b], writes=[pbuf])
                    xs_ = xres[:, tt, ch * 512:(ch + 1) * 512]
                    sc = ALPHA if g == 0 else 1.0
                    P.op("dve", ("scalar_tensor_tensor", dict(
                        out=xs_, in0=xs_, scalar=sc, in1=pb, op0=ALU.mult, op1=ALU.add)),
                        reads=[pbuf, xres_b[tt]], writes=[xres_b[tt]])

    KTW = max(WIN + TB, NSS * (WIN + 128))
    NVT = max(4 + TB // 128, NSS * 5)
    a_ = arena0
    OT, a_ = salloc([128, 8, NT], BF16, a_)
    OT_b = [Buf() for _ in range(8)]
    QT, a_ = salloc([128, 2, NT], BF16, a_)
    QT_b = [Buf(), Buf()]
    KT, a_ = salloc([128, 2, KTW], BF16, a_)
    KT_b = [Buf(), Buf()]
    VV, a_ = salloc([128, 2, NVT, 128], BF16, a_)
    VV_b = [Buf(), Buf()]
    bias2, a_ = salloc([128, 2, 2, 640], F32, a_)
    bias2_b = [Buf(), Buf()]
    stmp, a_ = salloc([128, 4, 640], F32, a_)
    stmp_b = [Buf() for _ in range(4)]
    PT, a_ = salloc([128, 4, 640], BF16, a_)
    PT_b = [Buf() for _ in range(4)]
    ktok, a_ = salloc([128, 2, 4, 128], BF16, a_)
    ktok_b = [Buf(), Buf()]
    kvout, a_ = salloc([128, 2, 4, 2, 128], F32, a_)
    kvout_b = [Buf(), Buf()]
    rsum, a_ = salloc([128, 4, 128], F32, a_)
    rsum_b = [Buf() for _ in range(4)]
    maskc, a_ = salloc([128, 640], F32, a_)
    maskc_b = Buf()
    attn_end = a_
    kscr_hb = [[Buf() for _ in range(8)] for _ in range(2)]
    vscr_hb = [[Buf() for _ in range(8)] for _ in range(2)]
    cnt_att = [0]
    att_par = [0]

    def attention_layer(li, kind, segs, ntt):
        nt = ntt * 128
        P.dma("sp", maskc, amask.ap(), writes=[maskc_b], arena=True)
        for hp in range(8):
            bf = hp % 2
            i_ = wslot[0]
            wslot[0] = (i_ + 1) % NSLOT
            w = wring[i_][:, 0:3072].rearrange("p (a b c) -> p a b c", a=8, b=3)
            wb = wring_b[i_]
            for which in range(3):
                P.dma("pool", w[:, :, which, :],
                      bass.AP(w_qkv, li * D * 3 * D + which * D + hp * 128, [[3 * D, 128], [128 * 3 * D, 8], [1, 128]]),
                      writes=[wb], ring=True)
            P.dma("sp", bias2[:, bf, :, :], biasT.ap()[li, 2 * hp:2 * hp + 2, :, :].rearrange("h p f -> p h f"),
                  writes=[bias2_b[bf]], arena=True)
            for hh in range(2):
                P.op("dve", ("tensor_tensor", dict(out=bias2[:, bf, hh, :], in0=bias2[:, bf, hh, :], in1=maskc,
                                                             op=ALU.add)), reads=[bias2_b[bf], maskc_b], writes=[bias2_b[bf]])
            for sg in segs:
                if sg.Lt == 0 or "tail" in cfg.skip:
                    continue
                if sg.kind == "s":
                    ksrc, vsrc = cache_k.ap()[li, sg.s], cache_v.ap()[li, sg.s]
                    kb_, vb_ = [], []
                else:
                    ksrc, vsrc = kscr.ap()[li], vscr.ap()[li]
                    kb_, vb_ = [kscr_hb[li][hp]], [vscr_hb[li][hp]]
                koff = sg.idx * (WIN + 128) if sg.kind == "s" else 0
                vt0 = sg.idx * 5 if sg.kind == "s" else 0
                c = cnt_att[0] % 2
                cnt_att[0] += 1
                P.dma("pool", ktok[:, c, :, :], ksrc[:, hp * 128:(hp + 1) * 128].rearrange("(k p) f -> p k f", p=128),
                      reads=kb_, writes=[ktok_b[c]], arena=True)
                pb, pbuf = bank()
                pbv = pb.bitcast(BF16)
                for k4 in range(4):
                    P.op("pe", ("transpose", dict(out=pbv[:, k4 * 128:(k4 + 1) * 128], in_=ktok[:, c, k4, :],
                                                                  identity=ident_bf)),
                         reads=[ktok_b[c], cstbf_b], writes=[pbuf], inc=(k4 == 3))
                copy_op(evac_engine(), KT[:, bf, koff:koff + WIN], pbv[:, 0:WIN], [pbuf], [KT_b[bf]])
                P.dma("pool", VV[:, bf, vt0:vt0 + 4, :], vsrc[:, hp * 128:(hp + 1) * 128].rearrange("(k p) f -> p k f", p=128),
                      reads=vb_, writes=[VV_b[bf]], arena=True)
            for t0 in range(0, nt if "qk" not in cfg.skip else 0, 512):
                tn = min(512, nt - t0)
                xb_ = xT_b[t0 // 128:(t0 + tn) // 128]
                for which in range(2):
                    pb, pbuf = bank()
                    for kc in range(8):
                        mm(pb[:, 0:tn], w[:, kc, which, :], xT[:, kc, t0:t0 + tn], kc == 0, kc == 7, reads=[wb] + xb_, writes=[pbuf])
                    if which == 0:
                        copy_op(evac_engine(), QT[:, bf, t0:t0 + tn], pb[:, 0:tn], [pbuf], [QT_b[bf]], scale=HD ** -0.5)
                    else:
                        if kind == "p":
                            sg = segs[0]
                            dst = KT[:, bf, sg.Lt + t0: sg.Lt + t0 + tn]
                            src = pb[:, 0:tn]
                        else:
                            n_sg = tn // 128
                            s0 = t0 // 128
                            dst = KT[:, bf, s0 * (WIN + 128): (s0 + n_sg) * (WIN + 128)].rearrange(
                                "p (s c) -> p s c", c=WIN + 128)[:, :, WIN:WIN + 128]
                            src = pb[:, 0:tn].rearrange("p (s c) -> p s c", c=128)
                        copy_op(evac_engine(), dst, src, [pbuf], [KT_b[bf]])
            for sg in (segs if "v" not in cfg.skip else []):
                n_out = min(4, sg.ntiles)
                for ti in range(sg.ntiles):
                    tt = sg.tile0 + ti
                    vt = (sg.idx * 5 + 4) if sg.kind == "s" else (sg.Lt // 128 + ti)
                    is_out = ti >= sg.ntiles - n_out
                    oi = ti - (sg.ntiles - n_out) if sg.kind == "p" else sg.idx
                    pb, pbuf = bank()
                    for which in ((1, 2) if is_out else (2,)):
                        for kc in range(8):
                            mm(pb[:, (which - 1) * 128:which * 128], xT[:, kc, tt * 128:(tt + 1) * 128], w[:, kc, which, :],
                               kc == 0, kc == 7, reads=[wb, xT_b[tt]], writes=[pbuf])
                    copy_op("act", VV[:, bf, vt, :], pb[:, 128:256], [pbuf], [VV_b[bf]])
                    if is_out and "kvcopy" not in cfg.skip:
                        P.op("dve", ("tensor_copy", dict(out=kvout[:, bf, oi, :, :],
                                                                          in_=pb[:, 0:256].rearrange("p (a b) -> p a b", a=2))),
                             reads=[pbuf], writes=[kvout_b[bf]])
                if sg.kind == "p" and "vdma" not in cfg.skip:
                    if sg.final:
                        kd, vd = k_prompt.ap()[li, sg.s], v_prompt.ap()[li, sg.s]
                        kw_, vw_ = [], []
                    else:
                        kd, vd = kscr.ap()[li], vscr.ap()[li]
                        kw_, vw_ = [kscr_hb[li][hp]], [vscr_hb[li][hp]]
                    if sg.ntiles >= 4:
                        P.dma("sp", kd[:, hp * 128:(hp + 1) * 128].rearrange("(k p) f -> p k f", p=128), kvout[:, bf, :, 0, :],
                              reads=[kvout_b[bf]], writes=kw_)
                        P.dma("sp", vd[:, hp * 128:(hp + 1) * 128].rearrange("(k p) f -> p k f", p=128), kvout[:, bf, :, 1, :],
                              reads=[kvout_b[bf]], writes=vw_)
                    else:
                        n_ = sg.ntiles
                        P.dma("sp", kd[0:n_ * 128, hp * 128:(hp + 1) * 128].rearrange("(k p) f -> p k f", p=128),
                              kvout[:, bf, 0:n_, 0, :], reads=[kvout_b[bf]], writes=kw_)
                        P.dma("sp", vd[0:n_ * 128, hp * 128:(hp + 1) * 128].rearrange("(k p) f -> p k f", p=128),
                              kvout[:, bf, 0:n_, 1, :], reads=[kvout_b[bf]], writes=vw_)
            if kind == "s" and "v" not in cfg.skip and "vdma" not in cfg.skip:
                for sg in segs:
                    P.dma("sp", k_sample.ap()[li, sg.s, :, hp * 128:(hp + 1) * 128], kvout[0:DSEQ, bf, sg.idx, 0, :],
                          reads=[kvout_b[bf]])
                    P.dma("sp", v_sample.ap()[li, sg.s, :, hp * 128:(hp + 1) * 128], kvout[0:DSEQ, bf, sg.idx, 1, :],
                          reads=[kvout_b[bf]])
            if True:
                st_all = {}

                def att_S(sg, ti, st_all=st_all, bf=bf):
                    koff = sg.idx * (WIN + 128) if sg.kind == "s" else 0
                    tt = sg.tile0 + ti
                    qc0 = tt * 128
                    js = [j for j in range(5) if sg.Lt + 128 * ti - 512 + 128 * j >= 0]
                    heads = []
                    for hh in range(2):
                        hr = slice(hh * 64, (hh + 1) * 64)
                        sc, scb = psum[:, 4 + 2 * hh:6 + 2 * hh, :], ps_b[4 + 2 * hh]
                        scf = sc.rearrange("p a b -> p (a b)")
                        c = (att_par[0] % 2) * 2 + hh
                        nks = {}
                        for j in js:
                            kp = sg.Lt + 128 * ti - 512 + 128 * j
                            nk = 128 if j < 4 else sg.nvalid
                            nks[j] = nk
                            mm(scf[0:nk, j * 128:(j + 1) * 128], KT[hr, bf, koff + kp:koff + kp + nk], QT[hr, bf, qc0:qc0 + 128],
                               True, True, reads=[KT_b[bf], QT_b[bf]], writes=[scb], inc=(j == js[-1]))
                        heads.append((hr, scf, scb, c, nks))
                    st_all[(sg.idx, ti)] = dict(js=js, heads=heads, qc0=qc0)
                    att_par[0] += 1

                def att_AE(sg, ti, st_all=st_all, bf=bf):
                    st = st_all[(sg.idx, ti)]
                    js = st["js"]
                    j0 = js[0]
                    for hh in range(2):
                        hr, scf, scb, c, nks = st["heads"][hh]
                        groups = []
                        if sg.nvalid == 128:
                            groups.append((j0, 5, 128))
                        else:
                            if j0 < 4:
                                groups.append((j0, 4, 128))
                            groups.append((4, 5, sg.nvalid))
                        for (ja, jb, nk) in groups:
                            cs = slice(ja * 128, jb * 128)
                            P.op("dve", ("tensor_tensor", dict(out=stmp[0:nk, c, cs], in0=scf[0:nk, cs], in1=bias2[0:nk, bf, hh, cs], op=ALU.add)),
                                 reads=[scb, bias2_b[bf]], writes=[stmp_b[c]])
                            P.op("act", ("activation", dict(out=PT[0:nk, c, cs], in_=stmp[0:nk, c, cs], func=AF.Exp)),
                                 reads=[stmp_b[c]], writes=[PT_b[c]])

                def att_PV(sg, ti, st_all=st_all, bf=bf):
                    vt0 = sg.idx * 5 if sg.kind == "s" else 0
                    st = st_all[(sg.idx, ti)]
                    js = st["js"]
                    obs = []
                    for hh in range(2):
                        hr, scf, scb, c, nks = st["heads"][hh]
                        ob, obuf = bank()
                        obs.append((ob, obuf))
                        for idx_, j in enumerate(js):
                            kp = sg.Lt + 128 * ti - 512 + 128 * j
                            nk = nks[j]
                            vt = vt0 + kp // 128
                            mm(ob[:, 0:128], VV[0:nk, bf, vt, :], PT[0:nk, c, j * 128:(j + 1) * 128],
                               idx_ == 0, idx_ == len(js) - 1, reads=[VV_b[bf], PT_b[c]], writes=[obuf], inc=False)
                        for idx_, j in enumerate(js):
                            nk = nks[j]
                            mm(ob[:, 128:256], ones_bf[0:nk, :], PT[0:nk, c, j * 128:(j + 1) * 128],
                               idx_ == 0, idx_ == len(js) - 1, reads=[cstbf_b, PT_b[c]], writes=[obuf], inc=(idx_ == len(js) - 1))
                    st["obs"] = obs

                def att_NORM(sg, ti, st_all=st_all, hp=hp):
                    st = st_all[(sg.idx, ti)]
                    qc0 = st["qc0"]
                    for hh in range(2):
                        hr, scf, scb, c, nks = st["heads"][hh]
                        ob, obuf = st["obs"][hh]
                        P.op("act", ("activation", dict(out=rsum[hr, c, :], in_=ob[hr, 128:256], func=AF.Ln)), reads=[obuf], writes=[rsum_b[c]])
                        P.op("act", ("activation", dict(out=rsum[hr, c, :], in_=rsum[hr, c, :], func=AF.Exp, scale=-1.0)),
                             reads=[rsum_b[c]], writes=[rsum_b[c]])
                        P.op("dve", ("tensor_tensor", dict(out=OT[hr, hp, qc0:qc0 + 128], in0=ob[hr, 0:128], in1=rsum[hr, c, :], op=ALU.mult)),
                             reads=[obuf, rsum_b[c]], writes=[OT_b[hp]])
                    del st_all[(sg.idx, ti)]

                chunks = [(sg, ti) for sg in (segs if "band" not in cfg.skip else []) for ti in range(sg.ntiles)]
                if chunks:
                    att_S(*chunks[0])
                    att_AE(*chunks[0])
                for i_, ch_ in enumerate(chunks):
                    if i_ + 1 < len(chunks):
                        att_S(*chunks[i_ + 1])
                    att_PV(*ch_)
                    if i_ + 1 < len(chunks):
                        att_AE(*chunks[i_ + 1])
                    att_NORM(*ch_)
        for ch in range(2 if "oproj" not in cfg.skip else 0):
            w, wb = load_w(w_attn_out.ap()[li, :, ch * 512:(ch + 1) * 512].rearrange("(k p) c -> p k c", p=128), [128, 8, 512])
            for tt in range(ntt):
                pb, pbuf = bank()
                for kc in range(8):
                    mm(pb, OT[:, kc, tt * 128:(tt + 1) * 128], w[:, kc, :], kc == 0, kc == 7, reads=[wb, OT_b[kc]], writes=[pbuf])
                xs_ = xres[:, tt, ch * 512:(ch + 1) * 512]
                P.op("dve", ("scalar_tensor_tensor", dict(out=xs_, in0=xs_, scalar=ALPHA, in1=pb,
                                                                           op0=ALU.mult, op1=ALU.add)),
                     reads=[pbuf, xres_b[tt]], writes=[xres_b[tt]])

    def bc_mid(a, n):
        return bass.AP(a.tensor, a.offset, [list(a.ap[0]), [0, n]] + [list(x) for x in a.ap[1:]])

    def bc_last(a, n):
        return bass.AP(a.tensor, a.offset, [list(x) for x in a.ap] + [[0, n]])

    PWT = max(3 + TB, NSS * 131)
    a_ = arena0
    sv = {}
    for nm in ("dt", "dA", "cum", "ncum", "dw", "ecl", "ecum"):
        sv[nm], a_ = salloc([128, NTT, 32], F32, a_)
    sv_b = {nm: [Buf() for _ in range(NTT)] for nm in sv}
    mdw, a_ = salloc([128, NTT, 3, 32], F32, a_)
    X3, a_ = salloc([128, 2, 3, 256], BF16, a_)
    X3_b = [Buf(), Buf()]
    dAh, a_ = salloc([128, NTT, 32], BF16, a_)
    dAl, a_ = salloc([128, NTT, 32], BF16, a_)
    stmp32, a_ = salloc([128, 2, 32], F32, a_)
    stmp32_b = Buf()
    cw, a_ = salloc([128, 32, 4], F32, a_)
    cb, a_ = salloc([128, 32], F32, a_)
    vecb, a_ = salloc([128, 3, 32], F32, a_)
    abc, a_ = salloc([128, 32], F32, a_)
    par_b = Buf()
    zs, a_ = salloc([128, NTT, 256], F32, a_)
    zs_b = [Buf() for _ in range(NTT)]
    NSEGM = max(1, NSS)
    pre, a_ = salloc([128, 4, PWT], BF16, a_)
    pre_b = [Buf() for _ in range(4)]
    dg, a_ = salloc([128, 16, 128], BF16, a_)
    dg_b = Buf()
    hal, a_ = salloc([128, 4, NSEGM, 3], F32, a_)
    hal_b = [Buf() for _ in range(4)]
    hout3, a_ = salloc([128, 4, NSEGM, 3], F32, a_)
    hout3_b = [Buf() for _ in range(4)]
    xcT, a_ = salloc([128, 4, NT], BF16, a_)
    xcT_b = [Buf() for _ in range(4)]
    tokm, a_ = salloc([128, NTT, 384], BF16, a_)
    tokm_b = [Buf() for _ in range(NTT)]
    ynT, a_ = salloc([128, 4, NT], BF16, a_)
    ynT_b = [Buf() for _ in range(NTT)]
    nwg, a_ = salloc([128, 256], F32, a_)
    nwg_b = Buf()
    cbm, a_ = salloc([128, 2, 128], F32, a_)
    cbm_b = [Buf(), Buf()]
    Rg, a_ = salloc([128, 2, 512], F32, a_)
    Rg_b = [Buf(), Buf()]
    eh, a_ = salloc([128, 2, 512], F32, a_)
    eh_b = [Buf(), Buf()]
    MT, a_ = salloc([128, 2, 512], BF16, a_)
    MT_b = [Buf(), Buf()]
    Xdt, a_ = salloc([128, 2, 256], BF16, a_)
    Xdt_b = [Buf(), Buf()]
    Xw, a_ = salloc([128, 2, 256], BF16, a_)
    Xw_b = [Buf(), Buf()]
    xsD, a_ = salloc([128, 2, 256], BF16, a_)
    xsD_b = [Buf(), Buf()]
    Ysb, a_ = salloc([128, 2, 256], F32, a_)
    Ysb_b = [Buf(), Buf()]
    junk, a_ = salloc([128, 2, 256], F32, a_)
    junk_b = [Buf(), Buf()]
    yn, a_ = salloc([128, 2, 256], BF16, a_)
    yn_b = [Buf(), Buf()]
    gst, a_ = salloc([128, 2, 4], F32, a_)
    gst_b = [Buf(), Buf()]
    HsA, a_ = salloc([128, max(1, NSS), 256], F32, a_)
    HsA_b = [Buf() for _ in range(max(1, NSS))]
    HbA, a_ = salloc([128, max(1, NSS), 256], BF16, a_)
    HbA_b = [Buf() for _ in range(max(1, NSS))]
    hld, a_ = salloc([128, max(1, NSS), 2, 128], F32, a_)
    hld_b = [Buf() for _ in range(max(1, NSS))]
    hout, a_ = salloc([128, 2, 128], F32, a_)
    hout_b = Buf()
    ssm_end = a_
    hscr_gb = [[Buf() for _ in range(8)] for _ in range(2)]
    cscr_gb = [[Buf() for _ in range(8)] for _ in range(2)]
    tri_f = cst_f[:, 1, :]
    ones_f = cst_f[:, 2, :]
    ident_f = cst_f[:, 0, :]

    def ssm_layer(li, kind, segs, ntt):
        nt = ntt * 128
        P.dma("sp", cw, conv_wT.ap()[li], writes=[par_b], arena=True)
        P.dma("sp", cb, conv_bT.ap()[li], writes=[par_b], arena=True)
        P.dma("sp", vecb, bass.AP(ssm_vec, li * 96, [[0, 128], [1, 96]]), writes=[par_b], arena=True)
        P.op("act", ("activation", dict(out=abc, in_=vecb[:, 1, :], func=AF.Exp)), reads=[par_b], writes=[par_b])
        P.op("dve", ("tensor_scalar", dict(out=abc, in0=abc, scalar1=-1.0, scalar2=None, op0=ALU.mult)), reads=[par_b], writes=[par_b])
        wdt, wdt_b = load_w(w_ssm_in.ap()[li, :, 6144:6176].rearrange("(k p) c -> p k c", p=128), [128, 8, 32])
        tile_nv = {}
        for sg in segs:
            for ti in range(sg.ntiles):
                tile_nv[sg.tile0 + ti] = sg.nvalid
        for tt in range(ntt):
            nv = tile_nv[tt]
            pb, pbuf = bank()
            for kc in range(8):
                mm(pb[:, 0:32], xT[:, kc, tt * 128:(tt + 1) * 128], wdt[:, kc, :], kc == 0, kc == 7, reads=[wdt_b, xT_b[tt]], writes=[pbuf])
            dt_, dA_, cum_, ncum_, dw_, ecl_ = (sv[k][:, tt, :] for k in ("dt", "dA", "cum", "ncum", "dw", "ecl"))
            P.op("dve", ("tensor_tensor", dict(out=dt_, in0=pb[:, 0:32], in1=vecb[:, 0, :], op=ALU.add)),
                 reads=[pbuf, par_b], writes=[sv_b["dt"][tt]])
            P.op("act", ("activation", dict(out=dt_, in_=dt_, func=AF.Exp)), reads=[sv_b["dt"][tt]], writes=[sv_b["dt"][tt]])
            P.op("act", ("activation", dict(out=dt_, in_=dt_, func=AF.Ln, bias=onec[:, 0:1])), reads=[sv_b["dt"][tt], epsc_b], writes=[sv_b["dt"][tt]])
            P.op("dve", ("tensor_tensor", dict(out=dA_, in0=dt_, in1=abc, op=ALU.mult)), reads=[sv_b["dt"][tt], par_b], writes=[sv_b["dA"][tt]])
            P.op("dve", ("tensor_copy", dict(out=dAh[:, tt, :], in_=dA_)), reads=[sv_b["dA"][tt]], writes=[sv_b["dA"][tt]])
            P.op("dve", ("tensor_tensor", dict(out=stmp32[:, 0, :], in0=dA_, in1=dAh[:, tt, :], op=ALU.subtract)),
                 reads=[sv_b["dA"][tt]], writes=[stmp32_b])
            P.op("dve", ("tensor_copy", dict(out=dAl[:, tt, :], in_=stmp32[:, 0, :])), reads=[stmp32_b], writes=[sv_b["dA"][tt]])
            pb2, pbuf2 = bank()
            mm(pb2[:, 0:32], tri_f, dA_, True, True, reads=[cst_b, sv_b["dA"][tt]], writes=[pbuf2], inc=False)
            mm(pb2[:, 32:64], ones_f[0:nv, :], sv["dA"][0:nv, tt, :], True, True, reads=[cst_b, sv_b["dA"][tt]], writes=[pbuf2])
            P.op("act", ("activation", dict(out=cum_, in_=pb2[:, 0:32], func=AF.Copy)), reads=[pbuf2], writes=[sv_b["cum"][tt]])
            P.op("act", ("activation", dict(out=ncum_, in_=pb2[:, 0:32], func=AF.Copy, scale=-1.0)), reads=[pbuf2], writes=[sv_b["ncum"][tt]])
            P.op("act", ("activation", dict(out=ecl_, in_=pb2[:, 32:64], func=AF.Exp)), reads=[pbuf2], writes=[sv_b["ecl"][tt]])
            P.op("act", ("activation", dict(out=sv["ecum"][:, tt, :], in_=pb2[:, 0:32], func=AF.Exp)), reads=[pbuf2], writes=[sv_b["ecum"][tt]])
            P.op("dve", ("tensor_tensor", dict(out=dw_, in0=pb2[:, 32:64], in1=cum_, op=ALU.subtract)),
                 reads=[pbuf2, sv_b["cum"][tt]], writes=[sv_b["dw"][tt]])
            P.op("dve", ("tensor_scalar", dict(out=dw_, in0=dw_, scalar1=0.0, scalar2=None, op0=ALU.min)), reads=[sv_b["dw"][tt]], writes=[sv_b["dw"][tt]])
            P.op("act", ("activation", dict(out=dw_, in_=dw_, func=AF.Exp)), reads=[sv_b["dw"][tt]], writes=[sv_b["dw"][tt]])
            P.op("dve", ("tensor_tensor", dict(out=dw_, in0=dw_, in1=dt_, op=ALU.mult)), reads=[sv_b["dw"][tt], sv_b["dt"][tt]], writes=[sv_b["dw"][tt]])
            P.op("dve", ("tensor_copy", dict(out=mdw[:, tt, 0, :], in_=dt_)), reads=[sv_b["dt"][tt]], writes=[sv_b["dw"][tt]])
            P.op("dve", ("tensor_copy", dict(out=mdw[:, tt, 1, :], in_=dw_)), reads=[sv_b["dw"][tt]], writes=[sv_b["dw"][tt]])
            P.op("dve", ("tensor_copy", dict(out=mdw[:, tt, 2, :], in_=vecb[:, 2, :])), reads=[par_b], writes=[sv_b["dw"][tt]])

        for g in range(NG):
            i_ = wslot[0]
            wslot[0] = (i_ + 1) % NSLOT
            wa = wring[i_][:, 0:4096].rearrange("p (a b c) -> p a b c", a=8, b=2)
            wa_b = wring_b[i_]
            for which, c0 in ((0, g * 256), (1, DIN + g * 256)):
                P.dma("pool", wa[:, :, which, :], bass.AP(w_ssm_in, li * D * WIN_COLS + c0, [[WIN_COLS, 128], [128 * WIN_COLS, 8], [1, 256]]),
                      writes=[wa_b], ring=True)
            i_ = wslot[0]
            wslot[0] = (i_ + 1) % NSLOT
            wbc = wring[i_][:, 0:2048].rearrange("p (a b c) -> p a b c", a=8, b=2)
            wbc_b = wring_b[i_]
            for which, c0 in ((0, 2 * DIN + g * 128), (1, 2 * DIN + 1024 + g * 128)):
                P.dma("pool", wbc[:, :, which, :], bass.AP(w_ssm_in, li * D * WIN_COLS + c0, [[WIN_COLS, 128], [128 * WIN_COLS, 8], [1, 128]]),
                      writes=[wbc_b], ring=True)
            P.dma("sp", nwg, bass.AP(ssm_norm_w, li * DIN + g * 256, [[0, 128], [1, 256]]), writes=[nwg_b], arena=True)
            gfts = [2 * g, 2 * g + 1, 16 + g, 24 + g]
            ch0s = [g * 256, g * 256 + 128, DIN + g * 128, DIN + 1024 + g * 128]

            for sg in segs:
                if sg.Lt == 0:
                    continue
                if sg.kind == "s":
                    h_src, h_rb = state_ssm.ap()[li, sg.s], []
                else:
                    h_src, h_rb = hscr.ap()[li], [hscr_gb[li][g]]
                P.dma("sp", hld[:, sg.idx, :, :], h_src[g * 256:(g + 1) * 256, :].rearrange("(k p) n -> p k n", p=128),
                      reads=h_rb, writes=[hld_b[sg.idx]], arena=True)
            for tt in range(ntt):
                pb, pbuf = bank()
                for kc in range(8):
                    mm(pb[:, 0:256], xT[:, kc, tt * 128:(tt + 1) * 128], wa[:, kc, 0, :], kc == 0, kc == 7, reads=[wa_b, xT_b[tt]], writes=[pbuf])
                P.op("act", ("activation", dict(out=zs[:, tt, :], in_=pb[:, 0:256], func=AF.Silu)), reads=[pbuf], writes=[zs_b[tt]])
            for fi in range(4):
                for k in range(4):
                    P.op("dve", ("tensor_scalar", dict(out=dg[:, fi * 4 + k, :], in0=ident_bf, scalar1=cw[:, gfts[fi], k:k + 1], scalar2=None,
                                                       op0=ALU.mult)), reads=[cstbf_b, par_b], writes=[dg_b])
            for sg in segs:
                off = sg.idx * 131 if sg.kind == "s" else 0
                for fi in range(4):
                    if sg.Lt == 0:
                        P.op("dve", ("memset", dict(ap=pre[:, fi, off:off + 3], constant=0.0)), writes=[pre_b[fi]])
                    else:
                        if sg.kind == "s":
                            src_t, base, rb = state_conv, (li * NSS + sg.s) * 3 * CCH, []
                        else:
                            src_t, base, rb = cscr, li * 3 * CCH, [cscr_gb[li][g]]
                        P.dma("sp", hal[:, fi, sg.idx, :], bass.AP(src_t, base + ch0s[fi], [[1, 128], [CCH, 3]]),
                              reads=rb, writes=[hal_b[fi]], arena=True, slow=True)
                        P.op("dve", ("tensor_copy", dict(out=pre[:, fi, off:off + 3], in_=hal[:, fi, sg.idx, :])),
                             reads=[hal_b[fi]], writes=[pre_b[fi]])
            for fi in range(4):
                for t0 in range(0, nt, 512):
                    tn = min(512, nt - t0)
                    pb, pbuf = bank()
                    for kc in range(8):
                        lw = wa[:, kc, 1, fi * 128:(fi + 1) * 128] if fi < 2 else wbc[:, kc, fi - 2, :]
                        mm(pb[:, 0:tn], lw, xT[:, kc, t0:t0 + tn], kc == 0, kc == 7,
                           reads=[wa_b if fi < 2 else wbc_b] + xT_b[t0 // 128:(t0 + tn) // 128], writes=[pbuf])
                    if kind == "p":
                        dst, src = pre[:, fi, 3 + t0:3 + t0 + tn], pb[:, 0:tn]
                        if t0 + tn == nt:
                            P.op("dve", ("tensor_copy", dict(out=hout3[:, fi, 0, :], in_=pb[:, tn - 3:tn])), reads=[pbuf], writes=[hout3_b[fi]])
                    else:
                        n_sg, s0 = tn // 128, t0 // 128
                        dst = pre[:, fi, s0 * 131:(s0 + n_sg) * 131].rearrange("p (s c) -> p s c", c=131)[:, :, 3:131]
                        src = pb[:, 0:tn].rearrange("p (s c) -> p s c", c=128)
                        P.op("dve", ("tensor_copy", dict(out=hout3[:, fi, s0:s0 + n_sg, :], in_=src[:, :, DSEQ - 3:DSEQ])),
                             reads=[pbuf], writes=[hout3_b[fi]])
                    copy_op("act", dst, src, [pbuf], [pre_b[fi]])
            for sg in segs:
                if sg.kind == "s":
                    dst_t, base, wbf = conv_sample, (li * NSS + sg.s) * 3 * CCH, []
                elif sg.final:
                    dst_t, base, wbf = conv_prompt, (li * NPS + sg.s) * 3 * CCH, []
                else:
                    dst_t, base, wbf = cscr, li * 3 * CCH, [cscr_gb[li][g]]
                for fi in range(4):
                    P.dma("sp", bass.AP(dst_t, base + ch0s[fi], [[1, 128], [CCH, 3]]), hout3[:, fi, sg.idx, :],
                          reads=[hout3_b[fi]], writes=wbf, slow=True)
            for fi in range(4):
                gf = gfts[fi]
                for t0 in range(0, nt, 512):
                    tn = min(512, nt - t0)
                    pb, pbuf = bank()
                    for k in range(4):
                        if kind == "p":
                            rhs_ = pre[:, fi, t0 + k:t0 + k + tn]
                            out_ = pb[:, 0:tn]
                        else:
                            n_sg, s0 = tn // 128, t0 // 128
                            rhs_ = pre[:, fi, s0 * 131:(s0 + n_sg) * 131].rearrange("p (s c) -> p s c", c=131)[:, :, k:k + 128]
                            out_ = pb[:, 0:tn].rearrange("p (s c) -> p s c", c=128)
                        mm(out_, dg[:, fi * 4 + k, :], rhs_, k == 0, k == 3, reads=[dg_b, pre_b[fi]], writes=[pbuf])
                    P.op("act", ("activation", dict(out=xcT[:, fi, t0:t0 + tn], in_=pb[:, 0:tn], func=AF.Silu, bias=cb[:, gf:gf + 1])),
                         reads=[pbuf, par_b], writes=[xcT_b[fi]])
            for tt in range(ntt):
                pb, pbuf = bank()
                pbv = pb.bitcast(BF16)
                for fi in range(3):
                    P.op("pe", ("transpose", dict(out=pbv[:, fi * 128:(fi + 1) * 128], in_=xcT[:, fi, tt * 128:(tt + 1) * 128], identity=ident_bf)),
                         reads=[xcT_b[fi], cstbf_b], writes=[pbuf], inc=(fi == 2))
                copy_op(evac_engine(), tokm[:, tt, :], pbv[:, 0:384], [pbuf], [tokm_b[tt]])
            for sg in segs:
                Hs, Hs_b, Hb, Hb_b = HsA[:, sg.idx, :], HsA_b[sg.idx], HbA[:, sg.idx, :], HbA_b[sg.idx]
                if sg.kind == "s":
                    h_src, h_rb = state_ssm.ap()[li, sg.s], []
                else:
                    h_src, h_rb = hscr.ap()[li], [hscr_gb[li][g]]
                if sg.Lt == 0:
                    P.op("dve", ("memset", dict(ap=Hs, constant=0.0)), writes=[Hs_b])
                else:
                    pb, pbuf = bank()
                    for k2 in range(2):
                        P.op("pe", ("transpose", dict(out=pb[:, k2 * 128:(k2 + 1) * 128], in_=hld[:, sg.idx, k2, :], identity=ident_f)),
                             reads=[hld_b[sg.idx], cst_b], writes=[pbuf], inc=(k2 == 1))
                    P.op("dve", ("tensor_copy", dict(out=Hs, in_=pb[:, 0:256])), reads=[pbuf], writes=[Hs_b])
                P.op("act", ("activation", dict(out=Hb, in_=Hs, func=AF.Copy)), reads=[Hs_b], writes=[Hb_b])
            h4 = slice(4 * g, 4 * g + 4)

            def ssd_iter(c1, ca, cb_, cc, h4=h4):
                t1 = ta = tb = tc = None
                if c1 is not None:
                    sg1, t1, k1 = c1
                if cb_ is not None:
                    sgb, tb, kb = cb_
                if cc is not None:
                    sgc, tc, kc_ = cc
                if ca is not None:
                    sga, ta, ka = ca
                    Hs, Hs_b, Hb, Hb_b = HsA[:, sga.idx, :], HsA_b[sga.idx], HbA[:, sga.idx, :], HbA_b[sga.idx]
                    tt = sga.tile0 + ta
                    nv = sga.nvalid
                    p_ = ka % 2
                    cs = slice(tt * 128, (tt + 1) * 128)
                    xs3 = tokm[:, tt, 0:256].rearrange("p (h d) -> p h d", h=4)
                    eh3 = eh[:, p_, :].rearrange("p (h t) -> p h t", h=4)
                    pb, pbuf = psum[:, 0 + p_, :], ps_b[0 + p_]
                    pc, pcbuf = psum[:, 2 + p_, :], ps_b[2 + p_]
                    py, pybuf = psum[:, 4 + p_, :], ps_b[4 + p_]
                if tb is not None:
                    ttb = sgb.tile0 + tb
                    q_ = kb % 2
                    pyb, pybbuf = psum[:, 4 + q_, :], ps_b[4 + q_]
                if tc is not None:
                    ttc = sgc.tile0 + tc
                    c_ = kc_ % 2
                    csc = slice(ttc * 128, (ttc + 1) * 128)
                    pt, ptbuf = psum[:, 6, :], ps_b[6]
                    ptv = pt.bitcast(BF16)
                    for k2 in range(2):
                        P.op("pe", ("transpose", dict(out=ptv[:, k2 * 128:(k2 + 1) * 128], in_=yn[:, c_, k2 * 128:(k2 + 1) * 128], identity=ident_bf)),
                             reads=[yn_b[c_], cstbf_b], writes=[ptbuf], inc=(k2 == 1))
                if t1 is not None:
                    tt1 = sg1.tile0 + t1
                    r_ = k1 % 2
                    cs1 = slice(tt1 * 128, (tt1 + 1) * 128)
                    mm(psum[:, 0 + r_, 0:128], xcT[:, 2, cs1], xcT[:, 3, cs1], True, True, reads=[xcT_b[2], xcT_b[3]], writes=[ps_b[0 + r_]])
                    pcb, pcbbuf = psum[:, 2 + r_, :], ps_b[2 + r_]
                    for h in range(4):
                        hc = 4 * g + h
                        for pi, dX in enumerate((dAh, dAl)):
                            P.op("pe", ("matmul", dict(out=pcb[:, h * 128:(h + 1) * 128], lhsT=bc_last(dX[:, tt1, hc:hc + 1], 128).rearrange("p a b -> p (a b)") if False else bass.AP(dX.tensor, dX[:, tt1, hc:hc + 1].offset, [list(dX[:, tt1, hc:hc + 1].ap[0]), [0, 128]]),
                                                       rhs=tri_bf, start=(pi == 0 and h == 0), stop=False, skip_group_check=True)),
                                 reads=[sv_b["dA"][tt1], negm_b], writes=[pcbbuf], inc=False)
                    pcb3 = pcb.rearrange("p (h t) -> p h t", h=4)
                    for dX in (dAh, dAl):
                        P.op("pe", ("matmul", dict(out=pcb3, lhsT=negtri_bf, rhs=bc_last(dX[:, tt1, h4], 128), start=False, stop=False,
                                                   skip_group_check=True)),
                             reads=[sv_b["dA"][tt1], negm_b], writes=[pcbbuf], inc=False)
                    P.op("pe", ("matmul", dict(out=pcb, lhsT=ident_bf, rhs=negm.rearrange("p h t -> p (h t)"), start=False, stop=True,
                                               skip_group_check=True)),
                         reads=[cstbf_b, negm_b], writes=[pcbbuf])
                if tb is not None:
                    for h in range(4):
                        P.op("act", ("activation", dict(out=junk[:, q_, h * 64:(h + 1) * 64], in_=pyb[:, 256 + h * 64:256 + (h + 1) * 64],
                                                        func=AF.Identity, scale=sv["ecum"][:, ttb, 4 * g + h:4 * g + h + 1])),
                             reads=[pybbuf, sv_b["ecum"][ttb]], writes=[junk_b[q_]])
                if ta is not None:
                    P.op("act", ("activation", dict(out=eh[:, p_, :], in_=pc, func=AF.Exp)), reads=[pcbuf], writes=[eh_b[p_]])
                    if ta > 0:
                        P.op("act", ("activation", dict(out=Hb, in_=Hs, func=AF.Copy)), reads=[Hs_b], writes=[Hb_b])
                    xs_ap = tokm[:, tt, 0:256]
                    in0_ = bass.AP(xs_ap.tensor, xs_ap.offset, [list(xs_ap.ap[0]), [0, 3], [64, 4], [1, 64]])
                    m_ap = mdw[:, tt, :, 4 * g:4 * g + 4]
                    in1_ = bass.AP(m_ap.tensor, m_ap.offset, [list(m_ap.ap[0]), [32, 3], [1, 4], [0, 64]])
                    P.op("dve", ("tensor_tensor", dict(out=X3[:, p_, :, :].rearrange("p j (h d) -> p j h d", h=4), in0=in0_, in1=in1_, op=ALU.mult)),
                         reads=[tokm_b[tt], sv_b["dw"][tt]], writes=[X3_b[p_]])
                if tc is not None:
                    copy_op("act", ynT[:, 2 * (g % 2):2 * (g % 2) + 2, csc], ptv[:, 0:256].rearrange("p (k t) -> p k t", k=2), [ptbuf], [ynT_b[ttc]])
                if tb is not None:
                    P.op("dve", ("tensor_tensor", dict(out=Ysb[:, q_, :], in0=pyb[:, 0:256], in1=junk[:, q_, :], op=ALU.add)),
                         reads=[pybbuf, junk_b[q_]], writes=[Ysb_b[q_]])
                    P.op("dve", ("tensor_tensor", dict(out=Ysb[:, q_, :], in0=Ysb[:, q_, :], in1=zs[:, ttb, :], op=ALU.mult)),
                         reads=[Ysb_b[q_], zs_b[ttb]], writes=[Ysb_b[q_]])
                    P.op("act", ("activation", dict(out=junk[:, q_, :], in_=Ysb[:, q_, :], func=AF.Square, accum_out=gst[:, q_, 0:1])),
                         reads=[Ysb_b[q_]], writes=[junk_b[q_], gst_b[q_]])
                    P.op("act", ("activation", dict(out=gst[:, q_, 1:2], in_=gst[:, q_, 0:1], func=AF.Ln, scale=1.0 / 256.0, bias=epsc[:, 1:2])),
                         reads=[gst_b[q_], epsc_b], writes=[gst_b[q_]])
                    P.op("act", ("activation", dict(out=gst[:, q_, 1:2], in_=gst[:, q_, 1:2], func=AF.Exp, scale=-0.5)),
                         reads=[gst_b[q_]], writes=[gst_b[q_]])
                if ta is not None:
                    P.op("dve", ("tensor_tensor", dict(out=MT[:, p_, :].rearrange("p (h t) -> p h t", h=4), in0=eh3,
                                                       in1=bc_mid(pb[:, 0:128], 4), op=ALU.mult)),
                         reads=[eh_b[p_], pbuf], writes=[MT_b[p_]])
                    for h in range(4):
                        hs_ = slice(h * 64, (h + 1) * 64)
                        mm(py[:, hs_], MT[:, p_, h * 128:(h + 1) * 128], X3[:, p_, 0, hs_], True, False,
                           reads=[MT_b[p_], X3_b[p_]], writes=[pybuf], inc=False)
                        mm(py[:, hs_], ident_bf, X3[:, p_, 2, hs_], False, True,
                           reads=[cstbf_b, X3_b[p_]], writes=[pybuf], inc=False)
                    mm(py[:, 256:512], xcT[:, 3, cs], Hb, True, True, reads=[xcT_b[3], Hb_b], writes=[pybuf])
                    ph, phbuf = psum[:, 7, :], ps_b[7]
                    mm(ph[:, 0:256], tokm[0:nv, tt, 256:384], X3[0:nv, p_, 1, :], True, True, reads=[tokm_b[tt], X3_b[p_]], writes=[phbuf])
                if tb is not None:
                    P.op("dve", ("scalar_tensor_tensor", dict(out=yn[:, q_, :], in0=Ysb[:, q_, :], scalar=gst[:, q_, 1:2], in1=nwg,
                                                              op0=ALU.mult, op1=ALU.mult)),
                         reads=[Ysb_b[q_], gst_b[q_], nwg_b], writes=[yn_b[q_]])
                if ta is not None:
                    Hs3 = Hs.rearrange("p (h d) -> p h d", h=4)
                    P.op("dve", ("tensor_tensor", dict(out=Hs3, in0=Hs3, in1=bc_last(sv["ecl"][:, tt, h4], 64), op=ALU.mult)),
                         reads=[Hs_b, sv_b["ecl"][tt]], writes=[Hs_b])
                    P.op("dve", ("tensor_tensor", dict(out=Hs, in0=Hs, in1=ph[:, 0:256], op=ALU.add)), reads=[Hs_b, phbuf], writes=[Hs_b])

            chunks = []
            for sg in segs:
                for ti in range(sg.ntiles):
                    chunks.append((sg, ti, len(chunks)))
            n_ = len(chunks)
            rng = lambda v: chunks[v] if 0 <= v < n_ else None
            for k in range(n_ + 3):
                ssd_iter(rng(k), rng(k - 1), rng(k - 2), rng(k - 3))
            for sg in segs:
                Hs, Hs_b = HsA[:, sg.idx, :], HsA_b[sg.idx]
                if sg.kind == "s":
                    h_dst, h_wb = ssm_sample.ap()[li, sg.s], []
                elif sg.final:
                    h_dst, h_wb = ssm_prompt.ap()[li, sg.s], []
                else:
                    h_dst, h_wb = hscr.ap()[li], [hscr_gb[li][g]]
                pb, pbuf = bank()
                for k2 in range(2):
                    P.op("pe", ("transpose", dict(out=pb[:, k2 * 128:(k2 + 1) * 128], in_=Hs[:, k2 * 128:(k2 + 1) * 128], identity=ident_f)),
                         reads=[Hs_b, cst_b], writes=[pbuf], inc=(k2 == 1))
                P.op("dve", ("tensor_copy", dict(out=hout, in_=pb[:, 0:256].rearrange("p (k n) -> p k n", k=2))), reads=[pbuf], writes=[hout_b])
                P.dma("sp", h_dst[g * 256:(g + 1) * 256, :].rearrange("(k p) n -> p k n", p=128), hout, reads=[hout_b], writes=h_wb)
            if g % 2 == 1:
                wo, wo_b = load_w(w_ssm_out.ap()[li, (g - 1) * 256:(g + 1) * 256, :].rearrange("(k p) c -> p k c", p=128), [128, 4, 1024])
                for tt in range(ntt):
                    for ch in range(2):
                        pb, pbuf = bank()
                        for k4 in range(4):
                            mm(pb, ynT[:, k4, tt * 128:(tt + 1) * 128], wo[:, k4, ch * 512:(ch + 1) * 512], k4 == 0, k4 == 3,
                               reads=[wo_b, ynT_b[tt]], writes=[pbuf])
                        xs_ = xres[:, tt, ch * 512:(ch + 1) * 512]
                        P.op("dve", ("scalar_tensor_tensor", dict(out=xs_, in0=xs_, scalar=(ALPHA if g == 1 else 1.0), in1=pb,
                                                                  op0=ALU.mult, op1=ALU.add)), reads=[pbuf, xres_b[tt]], writes=[xres_b[tt]])

    blocks = []
    nb = SEQ // TB
    for s_ in range(NPS):
        for b in range(nb):
            blocks.append(("p", [Seg("p", s_, b * TB, TB // 128, 0, 128, 0 if b == 0 else WIN, b == nb - 1, 0)]))
    blocks.append(("s", [Seg("s", s_, 0, 1, s_, DSEQ, WIN, True, s_) for s_ in range(NSS)]))

    for kind, segs in blocks:
        ntt = sum(sg.ntiles for sg in segs)
        nt = ntt * 128
        for sg in segs:
            for ti in range(sg.ntiles):
                tt = sg.tile0 + ti
                if kind == "p":
                    P.dma("sp", xres[:, tt, :], x_prompt.ap()[sg.s, sg.t0 + ti * 128: sg.t0 + (ti + 1) * 128, :],
                          writes=[xres_b[tt]])
                else:
                    P.op("dve", ("memset", dict(ap=xres[:, tt, :], constant=0.0)), writes=[xres_b[tt]])
                    P.dma("sp", xres[0:DSEQ, tt, :], x_sample.ap()[sg.s, :, :], writes=[xres_b[tt]])
                make_xT(tt)
        for layer in range(cfg.layers):
            P.barrier()
            mixed = False
            if layer % 2 == 0 and cfg.stage in ("attn", "full"):
                attention_layer(layer // 2, kind, segs, ntt)
                mixed = True
            if layer % 2 == 1 and cfg.stage in ("ssm", "full"):
                ssm_layer(layer // 2, kind, segs, ntt)
                mixed = True
            P.barrier()
            load_ln(layer, 0)
            ln_phase(list(range(ntt)), not mixed, make_xT)
            P.barrier()
            mlp(layer, nt)
            load_ln(layer, 1)
            last = layer == cfg.layers - 1
            tile_seg = {}
            for sg in segs:
                for ti in range(sg.ntiles):
                    tile_seg[sg.tile0 + ti] = (sg, ti)

            def store_y(tt, tile_seg=tile_seg, kind=kind):
                sg, ti = tile_seg[tt]
                if kind == "p":
                    P.dma("sp", y_prompt.ap()[sg.s, sg.t0 + ti * 128: sg.t0 + (ti + 1) * 128, :], xres[:, tt, :],
                          reads=[xres_b[tt]])
                else:
                    P.dma("sp", y_sample.ap()[sg.s, :, :], xres[0:DSEQ, tt, :], reads=[xres_b[tt]])

            ln_phase(list(range(ntt)), False, store_y if last else make_xT)
            P.barrier()

    P.finish()
    P.emit()
    return nc


def _consts():
    c = np.zeros((6, 128, 128), np.float32)
    i = np.arange(128)
    c[0] = np.eye(128, dtype=np.float32)
    c[1] = (i[:, None] <= i[None, :]).astype(np.float32)
    c[2] = 1.0
    same = (i[:, None] // 64) == (i[None, :] // 64)
    c[3] = c[1] * same
    c[4] = same.astype(np.float32)
    return c


def _bias_index():
    k = np.arange(128)[:, None]
    col = np.arange(640)[None, :]
    j, r = col // 128, col % 128
    q = 128 * (4 - j) + r
    idx = np.minimum(q - k, 256) + 256
    valid = np.where(k < 64, q < 576, q >= 64)
    mask = np.where(valid, 0.0, -1e30).astype(np.float32)
    return idx, mask


def prepare(inputs, cfg):
    f = lambda a: np.ascontiguousarray(np.asarray(a, dtype=np.float32))
    NPS, NSS = cfg.NPS, cfg.NSS
    rel = f(inputs["rel_bias"])
    idx, amask = _bias_index()
    biasT = np.ascontiguousarray(rel[:, :, idx])
    lnp = np.ascontiguousarray(np.stack([f(inputs["ln_mix_g"]), f(inputs["ln_mix_b"]),
                                         f(inputs["ln_ff_g"]), f(inputs["ln_ff_b"])], axis=1))
    conv_wT = np.ascontiguousarray(f(inputs["conv_w"]).reshape(2, 4, 32, 128).transpose(0, 3, 2, 1))
    conv_bT = np.ascontiguousarray(f(inputs["conv_b"]).reshape(2, 32, 128).transpose(0, 2, 1))
    ssm_vec = np.ascontiguousarray(np.stack([f(inputs["dt_bias"]), f(inputs["a_log"]), f(inputs["d_skip"])], axis=1))
    shared = {
        "w_qkv": f(inputs["w_qkv"]), "w_attn_out": f(inputs["w_attn_out"]), "w_ssm_in": f(inputs["w_ssm_in"]),
        "w_ssm_out": f(inputs["w_ssm_out"]), "w_ff_up": f(inputs["w_ff_up"]), "w_ff_down": f(inputs["w_ff_down"]),
        "biasT": biasT, "lnp": lnp, "conv_wT": conv_wT, "conv_bT": conv_bT, "ssm_vec": ssm_vec,
        "ssm_norm_w": f(inputs["ssm_norm_w"]), "consts": _consts(), "amask": amask,
    }
    xp, xs = f(inputs["x_prompt"]), f(inputs["x_sample"])
    ck, cv = f(inputs["cache_k"]), f(inputs["cache_v"])
    ss, sc = f(inputs["state_ssm"]), f(inputs["state_conv"])
    maps = []
    for c in range(NCORES):
        m = dict(shared)
        m["x_prompt"] = np.ascontiguousarray(xp[c * NPS:(c + 1) * NPS])
        m["x_sample"] = np.ascontiguousarray(xs[c * NSS:(c + 1) * NSS])
        m["cache_k"] = np.ascontiguousarray(ck[:, c * NSS:(c + 1) * NSS].reshape(2, NSS, WIN, D))
        m["cache_v"] = np.ascontiguousarray(cv[:, c * NSS:(c + 1) * NSS].reshape(2, NSS, WIN, D))
        m["state_ssm"] = np.ascontiguousarray(ss[:, c * NSS:(c + 1) * NSS].reshape(2, NSS, DIN, DST))
        m["state_conv"] = np.ascontiguousarray(sc[:, c * NSS:(c + 1) * NSS])
        maps.append(m)
    return maps


def assemble(results, cfg):
    NPS, NSS, SEQ, DSEQ, KEEP = cfg.NPS, cfg.NSS, cfg.SEQ, cfg.DSEQ, cfg.KEEP
    cat0 = lambda k: np.concatenate([r[k] for r in results], axis=0)
    cat1 = lambda k: np.concatenate([r[k] for r in results], axis=1)
    return (
        cat0("y_prompt"), cat0("y_sample"),
        cat1("k_prompt").reshape(2, NCORES * NPS, KEEP, NH, HD),
        cat1("v_prompt").reshape(2, NCORES * NPS, KEEP, NH, HD),
        cat1("ssm_prompt").reshape(2, NCORES * NPS, 32, 64, DST),
        cat1("conv_prompt"),
        cat1("k_sample").reshape(2, NCORES * NSS, DSEQ, NH, HD),
        cat1("v_sample").reshape(2, NCORES * NSS, DSEQ, NH, HD),
        cat1("ssm_sample").reshape(2, NCORES * NSS, 32, 64, DST),
        cat1("conv_sample"),
    )


def run(inputs, cfg, trace=False):
    nc = build(cfg)
    maps = prepare(inputs, cfg)
    res = run_bass_kernel_spmd(nc, maps, core_ids=list(range(NCORES)), trace=trace)
    out = assemble(res.results, cfg)
    if trace:
        return out, res
    return out


def kernel(**inputs):
    cfg = Cfg(NPS=4, SEQ=2048, TB=1024, NSS=4)
    return run(inputs, cfg)
```

```python
import numpy as np
import concourse.bass as bass
import concourse.mybir as mybir
from concourse.bass_utils import run_bass_kernel_spmd

F32 = mybir.dt.float32
BF16 = mybir.dt.bfloat16
AF = mybir.ActivationFunctionType
ALU = mybir.AluOpType

D = 1024
NH = 16
HD = 64
WIN = 512
DFF = 4096
DIN = 2048
NG = 8
DST = 128
CCH = 4096
WIN_COLS = 6176
ALPHA = float(8 ** 0.25)
LN_EPS = 1e-5
RMS_EPS = 1e-5
NCORES = 8


class Buf:
    __slots__ = ("w", "r", "ps")

    def __init__(self, ps=False):
        self.w = None
        self.r = {}
        self.ps = ps


class Prog:
    def __init__(self, nc):
        self.nc = nc
        self.E = {"pe": nc.tensor, "act": nc.scalar, "dve": nc.vector, "pool": nc.gpsimd, "sp": nc.sync}
        self.ops = {e: [] for e in self.E}
        self.cnt = {e: 0 for e in self.E}
        self.sem = {e: nc.alloc_semaphore(name="sem_" + e) for e in self.E}
        self.known = {e: {} for e in self.E}
        self.nds = 40
        self.dsem = [nc.alloc_semaphore(name="dsem%d" % i) for i in range(self.nds)]
        self.dcnt = [0] * self.nds
        self.dpool = {"sp": list(range(0, 28)), "pool": list(range(28, 32)), "ring": list(range(32, 40))}
        self.dnext = {"sp": 0, "pool": 0, "ring": 0}
        self.phase = []

    def _semh(self, k):
        return self.sem[k] if isinstance(k, str) else self.dsem[k[1]]

    def _deps(self, reads, writes):
        deps = []
        for b in reads:
            deps.append(b.w)
        for b in writes:
            deps.append(b.w)
            deps.extend(b.r.items())
        return deps

    def _waits(self, eng, deps):
        need = {}
        for t in deps:
            if t is None:
                continue
            k, v = t
            if k == eng and eng == "pe":
                continue
            if need.get(k, 0) < v:
                need[k] = v
        out = []
        kn = self.known[eng]
        for k, v in need.items():
            if kn.get(k, 0) >= v:
                continue
            kn[k] = v
            out.append((self._semh(k), v))
        return out

    def _commit(self, tick, reads, writes):
        k, v = tick
        for b in reads:
            if b.r.get(k, 0) < v:
                b.r[k] = v
        for b in writes:
            b.w = tick
            b.r = {}

    def op(self, eng, fn, reads=(), writes=(), inc=True):
        psr = [b for b in reads if b.ps]
        if psr:
            writes = list(writes) + psr
        waits = self._waits(eng, self._deps(reads, writes))
        if inc:
            self.cnt[eng] += 1
            tick = (eng, self.cnt[eng])
        else:
            tick = (eng, self.cnt[eng] + 1)
        sem = self.sem[eng]

        def run(e, waits=waits, fn=fn, sem=sem, inc=inc):
            for s, v in waits:
                e.wait_ge(s, v)
            if isinstance(fn, tuple):
                r = getattr(e, fn[0])(**fn[1])
            else:
                r = fn(e)
            if inc:
                r.then_inc(sem, 1)

        self.ops[eng].append(run)
        self._commit(tick, reads, writes)

    def dma(self, eng, out, in_, reads=(), writes=(), arena=False, slow=False, ring=False):
        pk = "ring" if ring else eng
        pl = self.dpool[pk]
        i = pl[self.dnext[pk]]
        self.dnext[pk] = (self.dnext[pk] + 1) % len(pl)
        deps = self._deps(reads, writes)
        if arena:
            deps.extend(self.phase)
        if self.dcnt[i] > 0:
            deps.append((("d", i), self.dcnt[i]))
        waits = self._waits(eng, deps)
        self.dcnt[i] += 16
        tick = (("d", i), self.dcnt[i])
        sem = self.dsem[i]

        def run(e, waits=waits, sem=sem, out=out, in_=in_, slow=slow):
            for s, v in waits:
                e.wait_ge(s, v)
            if slow:
                e.dma_start(out=out, in_=in_, allow_slow_non_contiguous=True).then_inc(sem, 16)
            else:
                e.dma_start(out=out, in_=in_).then_inc(sem, 16)

        self.ops[eng].append(run)
        self._commit(tick, reads, writes)

    def barrier(self, engs=("pe", "act", "dve")):
        comp = ("pe", "act", "dve")
        self.phase = [(c, self.cnt[c]) for c in comp if self.cnt[c] > 0]
        for e in engs:
            deps = [(c, self.cnt[c]) for c in comp if c != e and self.cnt[c] > 0]
            deps += [(("d", i), self.dcnt[i]) for i in range(self.nds) if self.dcnt[i] > 0 and i not in self.dpool["ring"]]
            waits = self._waits(e, deps)
            if waits:
                def run(eh, waits=waits):
                    for s, v in waits:
                        eh.wait_ge(s, v)
                self.ops[e].append(run)

    def finish(self):
        deps = [(("d", i), self.dcnt[i]) for i in range(self.nds) if self.dcnt[i] > 0]
        deps += [(c, self.cnt[c]) for c in ("pe", "act", "dve", "pool") if self.cnt[c] > 0]
        waits = self._waits("sp", deps)

        def run(eh, waits=waits):
            for s, v in waits:
                eh.wait_ge(s, v)
        self.ops["sp"].append(run)

    def emit(self):
        nc = self.nc
        with nc.Block() as block:
            @block.sync
            def _(e):
                for f in self.ops["sp"]:
                    f(e)

            @block.tensor
            def _(e):
                for f in self.ops["pe"]:
                    f(e)

            @block.scalar
            def _(e):
                for f in self.ops["act"]:
                    f(e)

            @block.vector
            def _(e):
                for f in self.ops["dve"]:
                    f(e)

            @block.gpsimd
            def _(e):
                for f in self.ops["pool"]:
                    f(e)


class Cfg:
    def __init__(self, NPS, SEQ, TB, NSS, DSEQ=64, layers=4, debug=False):
        self.NPS, self.SEQ, self.TB, self.NSS, self.DSEQ = NPS, SEQ, TB, NSS, DSEQ
        self.layers = layers
        self.stage = "full"
        self.skip = set()
        assert SEQ % TB == 0 and TB % 128 == 0
        assert TB >= WIN or SEQ == TB
        assert DSEQ == 64
        self.NT = max(TB, NSS * 128)
        self.NTT = self.NT // 128
        self.KEEP = min(WIN, SEQ)


class Seg:
    def __init__(self, kind, s, t0, ntiles, tile0, nvalid, Lt, final, idx):
        self.kind, self.s, self.t0, self.ntiles, self.tile0 = kind, s, t0, ntiles, tile0
        self.nvalid, self.Lt, self.final, self.idx = nvalid, Lt, final, idx


def bc_ap(t, off, n, parts=128):
    return bass.AP(t, off, [[0, parts], [1, n]])


def build(cfg):
    nc = bass.Bass("TRN2", target_bir_lowering=False)
    P = Prog(nc)
    NPS, SEQ, TB, NSS, DSEQ = cfg.NPS, cfg.SEQ, cfg.TB, cfg.NSS, cfg.DSEQ
    NT, NTT, KEEP = cfg.NT, cfg.NTT, cfg.KEEP

    def din(name, shape):
        return nc.dram_tensor(name, list(shape), F32, kind="ExternalInput")

    def dout(name, shape):
        return nc.dram_tensor(name, list(shape), F32, kind="ExternalOutput")

    def dscr(name, shape):
        return nc.dram_tensor(name, list(shape), F32, kind="Internal")

    x_prompt = din("x_prompt", [NPS, SEQ, D])
    x_sample = din("x_sample", [NSS, DSEQ, D])
    cache_k = din("cache_k", [2, NSS, WIN, D])
    cache_v = din("cache_v", [2, NSS, WIN, D])
    state_ssm = din("state_ssm", [2, NSS, DIN, DST])
    state_conv = din("state_conv", [2, NSS, 3, CCH])
    w_qkv = din("w_qkv", [2, D, 3 * D])
    w_attn_out = din("w_attn_out", [2, D, D])
    w_ssm_in = din("w_ssm_in", [2, D, WIN_COLS])
    w_ssm_out = din("w_ssm_out", [2, DIN, D])
    w_ff_up = din("w_ff_up", [4, D, DFF])
    w_ff_down = din("w_ff_down", [4, DFF, D])
    biasT = din("biasT", [2, NH, 128, 640])
    lnp = din("lnp", [4, 4, D])
    conv_wT = din("conv_wT", [2, 128, 32, 4])
    conv_bT = din("conv_bT", [2, 128, 32])
    ssm_vec = din("ssm_vec", [2, 3, 32])
    ssm_norm_w = din("ssm_norm_w", [2, DIN])
    consts = din("consts", [6, 128, 128])
    amask = din("amask", [128, 640])

    y_prompt = dout("y_prompt", [NPS, SEQ, D])
    y_sample = dout("y_sample", [NSS, DSEQ, D])
    k_prompt = dout("k_prompt", [2, NPS, KEEP, D])
    v_prompt = dout("v_prompt", [2, NPS, KEEP, D])
    ssm_prompt = dout("ssm_prompt", [2, NPS, DIN, DST])
    conv_prompt = dout("conv_prompt", [2, NPS, 3, CCH])
    k_sample = dout("k_sample", [2, NSS, DSEQ, D])
    v_sample = dout("v_sample", [2, NSS, DSEQ, D])
    ssm_sample = dout("ssm_sample", [2, NSS, DIN, DST])
    conv_sample = dout("conv_sample", [2, NSS, 3, CCH])

    kscr = dscr("kscr", [2, WIN, D])
    vscr = dscr("vscr", [2, WIN, D])
    hscr = dscr("hscr", [2, DIN, DST])
    cscr = dscr("cscr", [2, 3, CCH])
    kscr_b = [Buf(), Buf()]
    vscr_b = [Buf(), Buf()]
    hscr_b = [Buf(), Buf()]
    cscr_b = [Buf(), Buf()]

    sb_off = [(int(nc.sbuf_base) + 63) // 64 * 64]
    sb_top = int(nc.sbuf_top)
    uid = [0]

    def salloc(shape, dtype, off=None):
        nbytes = int(np.prod(shape[1:])) * (4 if dtype == F32 else 2)
        nbytes = (nbytes + 31) // 32 * 32
        if off is None:
            off = sb_off[0]
            sb_off[0] += nbytes
        uid[0] += 1
        assert off + nbytes <= sb_top, ("SBUF overflow", off, nbytes, sb_top)
        t = nc.alloc_sbuf_tensor_at("t%d" % uid[0], list(shape), dtype, offset=off)
        return t.ap(), off + nbytes

    def fix(shape, dtype):
        return salloc(shape, dtype)[0]

    xres = fix([128, NTT, D], F32)
    xres_b = [Buf() for _ in range(NTT)]
    xT = fix([128, 8, NT], BF16)
    xT_b = [Buf() for _ in range(NTT)]
    NSLOT = 5
    wring = [fix([128, 4096], BF16) for _ in range(NSLOT)]
    wring_b = [Buf() for _ in range(NSLOT)]
    lngb = fix([128, 2, D], F32)
    lngb_b = Buf()
    cst_f = fix([128, 6, 128], F32)
    cst_b = Buf()
    ident_bf = fix([128, 128], BF16)
    ones_bf = fix([128, 128], BF16)
    cstbf_b = Buf()
    xbf = fix([128, 2, D], BF16)
    xbf_b = [Buf(), Buf()]
    lnst = fix([128, 4, 16], F32)
    lnst_b = [Buf() for _ in range(4)]
    epsc = fix([128, 4], F32)
    onec = epsc[:, 2:3]
    epsc_b = Buf()
    negm = fix([128, 4, 128], BF16)
    negm_b = Buf()
    tri_bf = fix([128, 128], BF16)
    negtri_bf = fix([128, 128], BF16)
    arena0 = sb_off[0]

    psum = nc.alloc_psum_tensor("psum", [128, 8, 512], F32).ap()
    ps_b = [Buf(ps=True) for _ in range(8)]
    ps_rr = [0]

    def bank():
        i = ps_rr[0]
        ps_rr[0] = (i + 1) % 4
        return psum[:, i, :], ps_b[i]

    pair_rr = [0]

    def bankpair():
        i = pair_rr[0]
        pair_rr[0] = (i + 1) % 2
        return psum[:, 4 + 2 * i:6 + 2 * i, :], ps_b[4 + 2 * i]

    wslot = [0]

    def load_w(src_ap, view_shape):
        i = wslot[0]
        wslot[0] = (i + 1) % NSLOT
        n = int(np.prod(view_shape[1:]))
        assert n <= 4096
        dst = wring[i][:, 0:n]
        if len(view_shape) == 3:
            dst = dst.rearrange("p (a b) -> p a b", a=view_shape[1])
        elif len(view_shape) == 4:
            dst = dst.rearrange("p (a b c) -> p a b c", a=view_shape[1], b=view_shape[2])
        P.dma("pool", dst, src_ap, writes=[wring_b[i]], ring=True)
        return dst, wring_b[i]

    def mm(out, lhsT, rhs, start, stop, reads, writes, inc=None):
        if inc is None:
            inc = stop
        P.op("pe", ("matmul", dict(out=out, lhsT=lhsT, rhs=rhs, start=start, stop=stop)),
             reads=reads, writes=writes, inc=inc)

    evac_rr = [0]

    def evac_engine():
        evac_rr[0] ^= 1
        return "act" if evac_rr[0] else "dve"

    def copy_op(eng, out, in_, reads, writes, scale=None):
        if eng == "act":
            if scale is None:
                P.op("act", ("activation", dict(out=out, in_=in_, func=AF.Copy)), reads=reads, writes=writes)
            else:
                P.op("act", ("activation", dict(out=out, in_=in_, func=AF.Copy, scale=float(scale))),
                     reads=reads, writes=writes)
        else:
            if scale is None:
                P.op("dve", ("tensor_copy", dict(out=out, in_=in_)), reads=reads, writes=writes)
            else:
                P.op("dve", ("tensor_scalar", dict(out=out, in0=in_, scalar1=float(scale), scalar2=None,
                                                      op0=ALU.mult)), reads=reads, writes=writes)

    P.dma("sp", cst_f, consts.ap().rearrange("c p f -> p c f"), writes=[cst_b])
    P.op("dve", ("tensor_copy", dict(out=ident_bf, in_=cst_f[:, 0, :])), reads=[cst_b], writes=[cstbf_b])
    P.op("dve", ("tensor_copy", dict(out=ones_bf, in_=cst_f[:, 2, :])), reads=[cst_b], writes=[cstbf_b])

    def make_xT(tt, nrows=128):
        sl = tt % 2
        P.op("act", ("activation", dict(out=xbf[:, sl, :], in_=xres[:, tt, :], func=AF.Copy)),
             reads=[xres_b[tt]], writes=[xbf_b[sl]])
        pb, pbuf = bank()
        pbv = pb.bitcast(BF16)
        for kc in range(8):
            P.op("pe", ("transpose", dict(out=pbv[:, kc * 128:(kc + 1) * 128],
                                                    in_=xbf[:, sl, kc * 128:(kc + 1) * 128], identity=ident_bf)),
                 reads=[xbf_b[sl], cstbf_b], writes=[pbuf], inc=(kc == 7))
        P.op("dve", ("tensor_copy", dict(out=xT[:, :, tt * 128:(tt + 1) * 128],
                                            in_=pbv.rearrange("p (k t) -> p k t", k=8))),
             reads=[pbuf], writes=[xT_b[tt]])

    def ln_s1(tt, prescale):
        sl = tt % 4
        st = lnst[:, sl, :]
        stb = lnst_b[sl]
        if prescale:
            P.op("dve", ("tensor_scalar", dict(out=xres[:, tt, :], in0=xres[:, tt, :], scalar1=ALPHA, scalar2=None, op0=ALU.mult)),
                 reads=[xres_b[tt]], writes=[xres_b[tt]])
        P.op("dve", ("bn_stats", dict(out=st[:, 0:6], in_=xres[:, tt, 0:512])), reads=[xres_b[tt]], writes=[stb])
        P.op("dve", ("bn_stats", dict(out=st[:, 6:12], in_=xres[:, tt, 512:1024])), reads=[xres_b[tt]], writes=[stb])
        P.op("dve", ("bn_aggr", dict(out=st[:, 12:14], in_=st[:, 0:12])), reads=[stb], writes=[stb])
        P.op("act", ("activation", dict(out=st[:, 14:15], in_=st[:, 13:14], func=AF.Ln, bias=epsc[:, 0:1])),
             reads=[stb, epsc_b], writes=[stb])
        P.op("act", ("activation", dict(out=st[:, 14:15], in_=st[:, 14:15], func=AF.Exp, scale=-0.5)), reads=[stb], writes=[stb])
        P.op("dve", ("scalar_tensor_tensor", dict(out=st[:, 15:16], in0=st[:, 12:13], scalar=-1.0, in1=st[:, 14:15],
                                                     op0=ALU.mult, op1=ALU.mult)), reads=[stb], writes=[stb])

    def ln_s2(tt):
        sl = tt % 4
        st = lnst[:, sl, :]
        stb = lnst_b[sl]
        xt = xres[:, tt, :]
        P.op("act", ("activation", dict(out=xt, in_=xt, func=AF.Identity, scale=st[:, 14:15], bias=st[:, 15:16])),
             reads=[stb, xres_b[tt]], writes=[xres_b[tt]])
        P.op("dve", ("tensor_tensor", dict(out=xt, in0=xt, in1=lngb[:, 0, :], op=ALU.mult)),
             reads=[xres_b[tt], lngb_b], writes=[xres_b[tt]])
        P.op("dve", ("tensor_tensor", dict(out=xt, in0=xt, in1=lngb[:, 1, :], op=ALU.add)),
             reads=[xres_b[tt], lngb_b], writes=[xres_b[tt]])

    def ln_phase(tiles, prescale, s3):
        n = len(tiles)
        for i in range(n + 2):
            if i < n:
                ln_s1(tiles[i], prescale)
            if 0 <= i - 1 < n:
                ln_s2(tiles[i - 1])
            if 0 <= i - 2 < n:
                s3(tiles[i - 2])

    P.op("dve", ("tensor_scalar", dict(out=negm, in0=bass.AP(cst_f.tensor, cst_f[:, 1, :].offset, [list(cst_f[:, 1, :].ap[0]), [0, 4], [1, 128]]),
                                       scalar1=30000.0, scalar2=-30000.0, op0=ALU.mult, op1=ALU.add)), reads=[cst_b], writes=[negm_b])
    P.op("dve", ("tensor_copy", dict(out=tri_bf, in_=cst_f[:, 1, :])), reads=[cst_b], writes=[negm_b])
    P.op("dve", ("tensor_scalar", dict(out=negtri_bf, in0=cst_f[:, 1, :], scalar1=-1.0, scalar2=None, op0=ALU.mult)), reads=[cst_b], writes=[negm_b])
    P.op("dve", ("memset", dict(ap=epsc[:, 0:1], constant=LN_EPS)), writes=[epsc_b])
    P.op("dve", ("memset", dict(ap=epsc[:, 1:2], constant=RMS_EPS)), writes=[epsc_b])
    P.op("dve", ("memset", dict(ap=epsc[:, 2:3], constant=1.0)), writes=[epsc_b])

    def load_ln(layer, which):
        off = (layer * 4 + which * 2) * D
        P.dma("sp", lngb, bass.AP(lnp, off, [[0, 128], [D, 2], [1, D]]), writes=[lngb_b])

    HG = 2
    HGW = DFF // HG
    hT, a_end = salloc([128, HGW // 128, NT], BF16, arena0)
    hT_b = Buf()
    rtmp, a_end = salloc([128, 2, 512], F32, a_end)
    rtmp_b = [Buf(), Buf()]
    mlp_end = a_end

    def mlp(layer, nt):
        ntt = nt // 128
        tgs = [(t0, min(512, nt - t0)) for t0 in range(0, nt, 512)]
        rr = 0
        for g in range(HG):
            for wt in range(HGW // 512):
                c0 = g * HGW + wt * 512
                w, wb = load_w(w_ff_up.ap()[layer, :, c0:c0 + 512].rearrange("(k p) c -> p k c", p=128), [128, 8, 512])
                for sub in range(4):
                    for (t0, tn) in tgs:
                        pb, pbuf = bank()
                        for kc in range(8):
                            mm(pb[:, 0:tn], w[:, kc, sub * 128:(sub + 1) * 128], xT[:, kc, t0:t0 + tn],
                               kc == 0, kc == 7, reads=[wb] + xT_b[t0 // 128:(t0 + tn + 127) // 128], writes=[pbuf])
                        sl = rr % 2
                        rr += 1
                        P.op("act", ("activation", dict(out=rtmp[:, sl, 0:tn], in_=pb[:, 0:tn], func=AF.Relu)),
                             reads=[pbuf], writes=[rtmp_b[sl]])
                        fi = wt * 4 + sub
                        P.op("dve", ("tensor_tensor", dict(
                            out=hT[:, fi, t0:t0 + tn], in0=rtmp[:, sl, 0:tn], in1=rtmp[:, sl, 0:tn], op=ALU.mult)),
                            reads=[rtmp_b[sl]], writes=[hT_b])
            for ch in range(2):
                ws = []
                for kh in range(HGW // 1024):
                    r0 = g * HGW + kh * 1024
                    ws.append(load_w(w_ff_down.ap()[layer, r0:r0 + 1024, ch * 512:(ch + 1) * 512]
                                     .rearrange("(k p) c -> p k c", p=128), [128, 8, 512]))
                for tt in range(ntt):
                    pb, pbuf = bank()
                    nk = (HGW // 1024) * 8
                    for ki in range(nk):
                        w, wb = ws[ki // 8]
                        mm(pb, hT[:, ki, tt * 128:(tt + 1) * 128], w[:, ki % 8, :], ki == 0, ki == nk - 1,
                           reads=[wb, hT_b], writes=[pbuf])
                    xs_ = xres[:, tt, ch * 512:(ch + 1) * 512]
                    sc = ALPHA if g == 0 else 1.0
                    P.op("dve", ("scalar_tensor_tensor", dict(
                        out=xs_, in0=xs_, scalar=sc, in1=pb, op0=ALU.mult, op1=ALU.add)),
                        reads=[pbuf, xres_b[tt]], writes=[xres_b[tt]])

    KTW = max(WIN + TB, NSS * (WIN + 128))
    NVT = max(4 + TB // 128, NSS * 5)
    a_ = arena0
    OT, a_ = salloc([128, 8, NT], BF16, a_)
    OT_b = [Buf() for _ in range(8)]
    QT, a_ = salloc([128, 2, NT], BF16, a_)
    QT_b = [Buf(), Buf()]
    KT, a_ = salloc([128, 2, KTW], BF16, a_)
    KT_b = [Buf(), Buf()]
    VV, a_ = salloc([128, 2, NVT, 128], BF16, a_)
    VV_b = [Buf(), Buf()]
    bias2, a_ = salloc([128, 2, 2, 640], F32, a_)
    bias2_b = [Buf(), Buf()]
    stmp, a_ = salloc([128, 4, 640], F32, a_)
    stmp_b = [Buf() for _ in range(4)]
    PT, a_ = salloc([128, 4, 640], BF16, a_)
    PT_b = [Buf() for _ in range(4)]
    ktok, a_ = salloc([128, 2, 4, 128], BF16, a_)
    ktok_b = [Buf(), Buf()]
    kvout, a_ = salloc([128, 2, 4, 2, 128], F32, a_)
    kvout_b = [Buf(), Buf()]
    rsum, a_ = salloc([128, 4, 128], F32, a_)
    rsum_b = [Buf() for _ in range(4)]
    maskc, a_ = salloc([128, 640], F32, a_)
    maskc_b = Buf()
    attn_end = a_
    kscr_hb = [[Buf() for _ in range(8)] for _ in range(2)]
    vscr_hb = [[Buf() for _ in range(8)] for _ in range(2)]
    cnt_att = [0]
    att_par = [0]

    def attention_layer(li, kind, segs, ntt):
        nt = ntt * 128
        P.dma("sp", maskc, amask.ap(), writes=[maskc_b], arena=True)
        for hp in range(8):
            bf = hp % 2
            i_ = wslot[0]
            wslot[0] = (i_ + 1) % NSLOT
            w = wring[i_][:, 0:3072].rearrange("p (a b c) -> p a b c", a=8, b=3)
            wb = wring_b[i_]
            for which in range(3):
                P.dma("pool", w[:, :, which, :],
                      bass.AP(w_qkv, li * D * 3 * D + which * D + hp * 128, [[3 * D, 128], [128 * 3 * D, 8], [1, 128]]),
                      writes=[wb], ring=True)
            P.dma("sp", bias2[:, bf, :, :], biasT.ap()[li, 2 * hp:2 * hp + 2, :, :].rearrange("h p f -> p h f"),
                  writes=[bias2_b[bf]], arena=True)
            for hh in range(2):
                P.op("dve", ("tensor_tensor", dict(out=bias2[:, bf, hh, :], in0=bias2[:, bf, hh, :], in1=maskc,
                                                             op=ALU.add)), reads=[bias2_b[bf], maskc_b], writes=[bias2_b[bf]])
            for sg in segs:
                if sg.Lt == 0 or "tail" in cfg.skip:
                    continue
                if sg.kind == "s":
                    ksrc, vsrc = cache_k.ap()[li, sg.s], cache_v.ap()[li, sg.s]
                    kb_, vb_ = [], []
                else:
                    ksrc, vsrc = kscr.ap()[li], vscr.ap()[li]
                    kb_, vb_ = [kscr_hb[li][hp]], [vscr_hb[li][hp]]
                koff = sg.idx * (WIN + 128) if sg.kind == "s" else 0
                vt0 = sg.idx * 5 if sg.kind == "s" else 0
                c = cnt_att[0] % 2
                cnt_att[0] += 1
                P.dma("pool", ktok[:, c, :, :], ksrc[:, hp * 128:(hp + 1) * 128].rearrange("(k p) f -> p k f", p=128),
                      reads=kb_, writes=[ktok_b[c]], arena=True)
                pb, pbuf = bank()
                pbv = pb.bitcast(BF16)
                for k4 in range(4):
                    P.op("pe", ("transpose", dict(out=pbv[:, k4 * 128:(k4 + 1) * 128], in_=ktok[:, c, k4, :],
                                                                  identity=ident_bf)),
                         reads=[ktok_b[c], cstbf_b], writes=[pbuf], inc=(k4 == 3))
                copy_op(evac_engine(), KT[:, bf, koff:koff + WIN], pbv[:, 0:WIN], [pbuf], [KT_b[bf]])
                P.dma("pool", VV[:, bf, vt0:vt0 + 4, :], vsrc[:, hp * 128:(hp + 1) * 128].rearrange("(k p) f -> p k f", p=128),
                      reads=vb_, writes=[VV_b[bf]], arena=True)
            for t0 in range(0, nt if "qk" not in cfg.skip else 0, 512):
                tn = min(512, nt - t0)
                xb_ = xT_b[t0 // 128:(t0 + tn) // 128]
                for which in range(2):
                    pb, pbuf = bank()
                    for kc in range(8):
                        mm(pb[:, 0:tn], w[:, kc, which, :], xT[:, kc, t0:t0 + tn], kc == 0, kc == 7, reads=[wb] + xb_, writes=[pbuf])
                    if which == 0:
                        copy_op(evac_engine(), QT[:, bf, t0:t0 + tn], pb[:, 0:tn], [pbuf], [QT_b[bf]], scale=HD ** -0.5)
                    else:
                        if kind == "p":
                            sg = segs[0]
                            dst = KT[:, bf, sg.Lt + t0: sg.Lt + t0 + tn]
                            src = pb[:, 0:tn]
                        else:
                            n_sg = tn // 128
                            s0 = t0 // 128
                            dst = KT[:, bf, s0 * (WIN + 128): (s0 + n_sg) * (WIN + 128)].rearrange(
                                "p (s c) -> p s c", c=WIN + 128)[:, :, WIN:WIN + 128]
                            src = pb[:, 0:tn].rearrange("p (s c) -> p s c", c=128)
                        copy_op(evac_engine(), dst, src, [pbuf], [KT_b[bf]])
            for sg in (segs if "v" not in cfg.skip else []):
                n_out = min(4, sg.ntiles)
                for ti in range(sg.ntiles):
                    tt = sg.tile0 + ti
                    vt = (sg.idx * 5 + 4) if sg.kind == "s" else (sg.Lt // 128 + ti)
                    is_out = ti >= sg.ntiles - n_out
                    oi = ti - (sg.ntiles - n_out) if sg.kind == "p" else sg.idx
                    pb, pbuf = bank()
                    for which in ((1, 2) if is_out else (2,)):
                        for kc in range(8):
                            mm(pb[:, (which - 1) * 128:which * 128], xT[:, kc, tt * 128:(tt + 1) * 128], w[:, kc, which, :],
                               kc == 0, kc == 7, reads=[wb, xT_b[tt]], writes=[pbuf])
                    copy_op("act", VV[:, bf, vt, :], pb[:, 128:256], [pbuf], [VV_b[bf]])
                    if is_out and "kvcopy" not in cfg.skip:
                        P.op("dve", ("tensor_copy", dict(out=kvout[:, bf, oi, :, :],
                                                                          in_=pb[:, 0:256].rearrange("p (a b) -> p a b", a=2))),
                             reads=[pbuf], writes=[kvout_b[bf]])
                if sg.kind == "p" and "vdma" not in cfg.skip:
                    if sg.final:
                        kd, vd = k_prompt.ap()[li, sg.s], v_prompt.ap()[li, sg.s]
                        kw_, vw_ = [], []
                    else:
                        kd, vd = kscr.ap()[li], vscr.ap()[li]
                        kw_, vw_ = [kscr_hb[li][hp]], [vscr_hb[li][hp]]
                    if sg.ntiles >= 4:
                        P.dma("sp", kd[:, hp * 128:(hp + 1) * 128].rearrange("(k p) f -> p k f", p=128), kvout[:, bf, :, 0, :],
                              reads=[kvout_b[bf]], writes=kw_)
                        P.dma("sp", vd[:, hp * 128:(hp + 1) * 128].rearrange("(k p) f -> p k f", p=128), kvout[:, bf, :, 1, :],
                              reads=[kvout_b[bf]], writes=vw_)
                    else:
                        n_ = sg.ntiles
                        P.dma("sp", kd[0:n_ * 128, hp * 128:(hp + 1) * 128].rearrange("(k p) f -> p k f", p=128),
                              kvout[:, bf, 0:n_, 0, :], reads=[kvout_b[bf]], writes=kw_)
                        P.dma("sp", vd[0:n_ * 128, hp * 128:(hp + 1) * 128].rearrange("(k p) f -> p k f", p=128),
                              kvout[:, bf, 0:n_, 1, :], reads=[kvout_b[bf]], writes=vw_)
            if kind == "s" and "v" not in cfg.skip and "vdma" not in cfg.skip:
                for sg in segs:
                    P.dma("sp", k_sample.ap()[li, sg.s, :, hp * 128:(hp + 1) * 128], kvout[0:DSEQ, bf, sg.idx, 0, :],
                          reads=[kvout_b[bf]])
                    P.dma("sp", v_sample.ap()[li, sg.s, :, hp * 128:(hp + 1) * 128], kvout[0:DSEQ, bf, sg.idx, 1, :],
                          reads=[kvout_b[bf]])
            if True:
                st_all = {}

                def att_S(sg, ti, st_all=st_all, bf=bf):
                    koff = sg.idx * (WIN + 128) if sg.kind == "s" else 0
                    tt = sg.tile0 + ti
                    qc0 = tt * 128
                    js = [j for j in range(5) if sg.Lt + 128 * ti - 512 + 128 * j >= 0]
                    heads = []
                    for hh in range(2):
                        hr = slice(hh * 64, (hh + 1) * 64)
                        sc, scb = psum[:, 4 + 2 * hh:6 + 2 * hh, :], ps_b[4 + 2 * hh]
                        scf = sc.rearrange("p a b -> p (a b)")
                        c = (att_par[0] % 2) * 2 + hh
                        nks = {}
                        for j in js:
                            kp = sg.Lt + 128 * ti - 512 + 128 * j
                            nk = 128 if j < 4 else sg.nvalid
                            nks[j] = nk
                            mm(scf[0:nk, j * 128:(j + 1) * 128], KT[hr, bf, koff + kp:koff + kp + nk], QT[hr, bf, qc0:qc0 + 128],
                               True, True, reads=[KT_b[bf], QT_b[bf]], writes=[scb], inc=(j == js[-1]))
                        heads.append((hr, scf, scb, c, nks))
                    st_all[(sg.idx, ti)] = dict(js=js, heads=heads, qc0=qc0)
                    att_par[0] += 1

                def att_AE(sg, ti, st_all=st_all, bf=bf):
                    st = st_all[(sg.idx, ti)]
                    js = st["js"]
                    j0 = js[0]
                    for hh in range(2):
                        hr, scf, scb, c, nks = st["heads"][hh]
                        groups = []
                        if sg.nvalid == 128:
                            groups.append((j0, 5, 128))
                        else:
                            if j0 < 4:
                                groups.append((j0, 4, 128))
                            groups.append((4, 5, sg.nvalid))
                        for (ja, jb, nk) in groups:
                            cs = slice(ja * 128, jb * 128)
                            P.op("dve", ("tensor_tensor", dict(out=stmp[0:nk, c, cs], in0=scf[0:nk, cs], in1=bias2[0:nk, bf, hh, cs], op=ALU.add)),
                                 reads=[scb, bias2_b[bf]], writes=[stmp_b[c]])
                            P.op("act", ("activation", dict(out=PT[0:nk, c, cs], in_=stmp[0:nk, c, cs], func=AF.Exp)),
                                 reads=[stmp_b[c]], writes=[PT_b[c]])

                def att_PV(sg, ti, st_all=st_all, bf=bf):
                    vt0 = sg.idx * 5 if sg.kind == "s" else 0
                    st = st_all[(sg.idx, ti)]
                    js = st["js"]
                    obs = []
                    for hh in range(2):
                        hr, scf, scb, c, nks = st["heads"][hh]
                        ob, obuf = bank()
                        obs.append((ob, obuf))
                        for idx_, j in enumerate(js):
                            kp = sg.Lt + 128 * ti - 512 + 128 * j
                            nk = nks[j]
                            vt = vt0 + kp // 128
                            mm(ob[:, 0:128], VV[0:nk, bf, vt, :], PT[0:nk, c, j * 128:(j + 1) * 128],
                               idx_ == 0, idx_ == len(js) - 1, reads=[VV_b[bf], PT_b[c]], writes=[obuf], inc=False)
                        for idx_, j in enumerate(js):
                            nk = nks[j]
                            mm(ob[:, 128:256], ones_bf[0:nk, :], PT[0:nk, c, j * 128:(j + 1) * 128],
                               idx_ == 0, idx_ == len(js) - 1, reads=[cstbf_b, PT_b[c]], writes=[obuf], inc=(idx_ == len(js) - 1))
                    st["obs"] = obs

                def att_NORM(sg, ti, st_all=st_all, hp=hp):
                    st = st_all[(sg.idx, ti)]
                    qc0 = st["qc0"]
                    for hh in range(2):
                        hr, scf, scb, c, nks = st["heads"][hh]
                        ob, obuf = st["obs"][hh]
                        P.op("act", ("activation", dict(out=rsum[hr, c, :], in_=ob[hr, 128:256], func=AF.Ln)), reads=[obuf], writes=[rsum_b[c]])
                        P.op("act", ("activation", dict(out=rsum[hr, c, :], in_=rsum[hr, c, :], func=AF.Exp, scale=-1.0)),
                             reads=[rsum_b[c]], writes=[rsum_b[c]])
                        P.op("dve", ("tensor_tensor", dict(out=OT[hr, hp, qc0:qc0 + 128], in0=ob[hr, 0:128], in1=rsum[hr, c, :], op=ALU.mult)),
                             reads=[obuf, rsum_b[c]], writes=[OT_b[hp]])
                    del st_all[(sg.idx, ti)]

                chunks = [(sg, ti) for sg in (segs if "band" not in cfg.skip else []) for ti in range(sg.ntiles)]
                if chunks:
                    att_S(*chunks[0])
                    att_AE(*chunks[0])
                for i_, ch_ in enumerate(chunks):
                    if i_ + 1 < len(chunks):
                        att_S(*chunks[i_ + 1])
                    att_PV(*ch_)
                    if i_ + 1 < len(chunks):
                        att_AE(*chunks[i_ + 1])
                    att_NORM(*ch_)
        for ch in range(2 if "oproj" not in cfg.skip else 0):
            w, wb = load_w(w_attn_out.ap()[li, :, ch * 512:(ch + 1) * 512].rearrange("(k p) c -> p k c", p=128), [128, 8, 512])
            for tt in range(ntt):
                pb, pbuf = bank()
                for kc in range(8):
                    mm(pb, OT[:, kc, tt * 128:(tt + 1) * 128], w[:, kc, :], kc == 0, kc == 7, reads=[wb, OT_b[kc]], writes=[pbuf])
                xs_ = xres[:, tt, ch * 512:(ch + 1) * 512]
                P.op("dve", ("scalar_tensor_tensor", dict(out=xs_, in0=xs_, scalar=ALPHA, in1=pb,
                                                                           op0=ALU.mult, op1=ALU.add)),
                     reads=[pbuf, xres_b[tt]], writes=[xres_b[tt]])

    def bc_mid(a, n):
        return bass.AP(a.tensor, a.offset, [list(a.ap[0]), [0, n]] + [list(x) for x in a.ap[1:]])

    def bc_last(a, n):
        return bass.AP(a.tensor, a.offset, [list(x) for x in a.ap] + [[0, n]])

    PWT = max(3 + TB, NSS * 131)
    a_ = arena0
    sv = {}
    for nm in ("dt", "dA", "cum", "ncum", "dw", "ecl", "ecum"):
        sv[nm], a_ = salloc([128, NTT, 32], F32, a_)
    sv_b = {nm: [Buf() for _ in range(NTT)] for nm in sv}
    mdw, a_ = salloc([128, NTT, 3, 32], F32, a_)
    X3, a_ = salloc([128, 2, 3, 256], BF16, a_)
    X3_b = [Buf(), Buf()]
    dAh, a_ = salloc([128, NTT, 32], BF16, a_)
    dAl, a_ = salloc([128, NTT, 32], BF16, a_)
    stmp32, a_ = salloc([128, 2, 32], F32, a_)
    stmp32_b = Buf()
    cw, a_ = salloc([128, 32, 4], F32, a_)
    cb, a_ = salloc([128, 32], F32, a_)
    vecb, a_ = salloc([128, 3, 32], F32, a_)
    abc, a_ = salloc([128, 32], F32, a_)
    par_b = Buf()
    zs, a_ = salloc([128, NTT, 256], F32, a_)
    zs_b = [Buf() for _ in range(NTT)]
    NSEGM = max(1, NSS)
    pre, a_ = salloc([128, 4, PWT], BF16, a_)
    pre_b = [Buf() for _ in range(4)]
    dg, a_ = salloc([128, 16, 128], BF16, a_)
    dg_b = Buf()
    hal, a_ = salloc([128, 4, NSEGM, 3], F32, a_)
    hal_b = [Buf() for _ in range(4)]
    hout3, a_ = salloc([128, 4, NSEGM, 3], F32, a_)
    hout3_b = [Buf() for _ in range(4)]
    xcT, a_ = salloc([128, 4, NT], BF16, a_)
    xcT_b = [Buf() for _ in range(4)]
    tokm, a_ = salloc([128, NTT, 384], BF16, a_)
    tokm_b = [Buf() for _ in range(NTT)]
    ynT, a_ = salloc([128, 4, NT], BF16, a_)
    ynT_b = [Buf() for _ in range(NTT)]
    nwg, a_ = salloc([128, 256], F32, a_)
    nwg_b = Buf()
    cbm, a_ = salloc([128, 2, 128], F32, a_)
    cbm_b = [Buf(), Buf()]
    Rg, a_ = salloc([128, 2, 512], F32, a_)
    Rg_b = [Buf(), Buf()]
    eh, a_ = salloc([128, 2, 512], F32, a_)
    eh_b = [Buf(), Buf()]
    MT, a_ = salloc([128, 2, 512], BF16, a_)
    MT_b = [Buf(), Buf()]
    Xdt, a_ = salloc([128, 2, 256], BF16, a_)
    Xdt_b = [Buf(), Buf()]
    Xw, a_ = salloc([128, 2, 256], BF16, a_)
    Xw_b = [Buf(), Buf()]
    xsD, a_ = salloc([128, 2, 256], BF16, a_)
    xsD_b = [Buf(), Buf()]
    Ysb, a_ = salloc([128, 2, 256], F32, a_)
    Ysb_b = [Buf(), Buf()]
    junk, a_ = salloc([128, 2, 256], F32, a_)
    junk_b = [Buf(), Buf()]
    yn, a_ = salloc([128, 2, 256], BF16, a_)
    yn_b = [Buf(), Buf()]
    gst, a_ = salloc([128, 2, 4], F32, a_)
    gst_b = [Buf(), Buf()]
    HsA, a_ = salloc([128, max(1, NSS), 256], F32, a_)
    HsA_b = [Buf() for _ in range(max(1, NSS))]
    HbA, a_ = salloc([128, max(1, NSS), 256], BF16, a_)
    HbA_b = [Buf() for _ in range(max(1, NSS))]
    hld, a_ = salloc([128, max(1, NSS), 2, 128], F32, a_)
    hld_b = [Buf() for _ in range(max(1, NSS))]
    hout, a_ = salloc([128, 2, 128], F32, a_)
    hout_b = Buf()
    ssm_end = a_
    hscr_gb = [[Buf() for _ in range(8)] for _ in range(2)]
    cscr_gb = [[Buf() for _ in range(8)] for _ in range(2)]
    tri_f = cst_f[:, 1, :]
    ones_f = cst_f[:, 2, :]
    ident_f = cst_f[:, 0, :]

    def ssm_layer(li, kind, segs, ntt):
        nt = ntt * 128
        P.dma("sp", cw, conv_wT.ap()[li], writes=[par_b], arena=True)
        P.dma("sp", cb, conv_bT.ap()[li], writes=[par_b], arena=True)
        P.dma("sp", vecb, bass.AP(ssm_vec, li * 96, [[0, 128], [1, 96]]), writes=[par_b], arena=True)
        P.op("act", ("activation", dict(out=abc, in_=vecb[:, 1, :], func=AF.Exp)), reads=[par_b], writes=[par_b])
        P.op("dve", ("tensor_scalar", dict(out=abc, in0=abc, scalar1=-1.0, scalar2=None, op0=ALU.mult)), reads=[par_b], writes=[par_b])
        wdt, wdt_b = load_w(w_ssm_in.ap()[li, :, 6144:6176].rearrange("(k p) c -> p k c", p=128), [128, 8, 32])
        tile_nv = {}
        for sg in segs:
            for ti in range(sg.ntiles):
                tile_nv[sg.tile0 + ti] = sg.nvalid
        for tt in range(ntt):
            nv = tile_nv[tt]
            pb, pbuf = bank()
            for kc in range(8):
                mm(pb[:, 0:32], xT[:, kc, tt * 128:(tt + 1) * 128], wdt[:, kc, :], kc == 0, kc == 7, reads=[wdt_b, xT_b[tt]], writes=[pbuf])
            dt_, dA_, cum_, ncum_, dw_, ecl_ = (sv[k][:, tt, :] for k in ("dt", "dA", "cum", "ncum", "dw", "ecl"))
            P.op("dve", ("tensor_tensor", dict(out=dt_, in0=pb[:, 0:32], in1=vecb[:, 0, :], op=ALU.add)),
                 reads=[pbuf, par_b], writes=[sv_b["dt"][tt]])
            P.op("act", ("activation", dict(out=dt_, in_=dt_, func=AF.Exp)), reads=[sv_b["dt"][tt]], writes=[sv_b["dt"][tt]])
            P.op("act", ("activation", dict(out=dt_, in_=dt_, func=AF.Ln, bias=onec[:, 0:1])), reads=[sv_b["dt"][tt], epsc_b], writes=[sv_b["dt"][tt]])
            P.op("dve", ("tensor_tensor", dict(out=dA_, in0=dt_, in1=abc, op=ALU.mult)), reads=[sv_b["dt"][tt], par_b], writes=[sv_b["dA"][tt]])
            P.op("dve", ("tensor_copy", dict(out=dAh[:, tt, :], in_=dA_)), reads=[sv_b["dA"][tt]], writes=[sv_b["dA"][tt]])
            P.op("dve", ("tensor_tensor", dict(out=stmp32[:, 0, :], in0=dA_, in1=dAh[:, tt, :], op=ALU.subtract)),
                 reads=[sv_b["dA"][tt]], writes=[stmp32_b])
            P.op("dve", ("tensor_copy", dict(out=dAl[:, tt, :], in_=stmp32[:, 0, :])), reads=[stmp32_b], writes=[sv_b["dA"][tt]])
            pb2, pbuf2 = bank()
            mm(pb2[:, 0:32], tri_f, dA_, True, True, reads=[cst_b, sv_b["dA"][tt]], writes=[pbuf2], inc=False)
            mm(pb2[:, 32:64], ones_f[0:nv, :], sv["dA"][0:nv, tt, :], True, True, reads=[cst_b, sv_b["dA"][tt]], writes=[pbuf2])
            P.op("act", ("activation", dict(out=cum_, in_=pb2[:, 0:32], func=AF.Copy)), reads=[pbuf2], writes=[sv_b["cum"][tt]])
            P.op("act", ("activation", dict(out=ncum_, in_=pb2[:, 0:32], func=AF.Copy, scale=-1.0)), reads=[pbuf2], writes=[sv_b["ncum"][tt]])
            P.op("act", ("activation", dict(out=ecl_, in_=pb2[:, 32:64], func=AF.Exp)), reads=[pbuf2], writes=[sv_b["ecl"][tt]])
            P.op("act", ("activation", dict(out=sv["ecum"][:, tt, :], in_=pb2[:, 0:32], func=AF.Exp)), reads=[pbuf2], writes=[sv_b["ecum"][tt]])
            P.op("dve", ("tensor_tensor", dict(out=dw_, in0=pb2[:, 32:64], in1=cum_, op=ALU.subtract)),
                 reads=[pbuf2, sv_b["cum"][tt]], writes=[sv_b["dw"][tt]])
            P.op("dve", ("tensor_scalar", dict(out=dw_, in0=dw_, scalar1=0.0, scalar2=None, op0=ALU.min)), reads=[sv_b["dw"][tt]], writes=[sv_b["dw"][tt]])
            P.op("act", ("activation", dict(out=dw_, in_=dw_, func=AF.Exp)), reads=[sv_b["dw"][tt]], writes=[sv_b["dw"][tt]])
            P.op("dve", ("tensor_tensor", dict(out=dw_, in0=dw_, in1=dt_, op=ALU.mult)), reads=[sv_b["dw"][tt], sv_b["dt"][tt]], writes=[sv_b["dw"][tt]])
            P.op("dve", ("tensor_copy", dict(out=mdw[:, tt, 0, :], in_=dt_)), reads=[sv_b["dt"][tt]], writes=[sv_b["dw"][tt]])
            P.op("dve", ("tensor_copy", dict(out=mdw[:, tt, 1, :], in_=dw_)), reads=[sv_b["dw"][tt]], writes=[sv_b["dw"][tt]])
            P.op("dve", ("tensor_copy", dict(out=mdw[:, tt, 2, :], in_=vecb[:, 2, :])), reads=[par_b], writes=[sv_b["dw"][tt]])

        for g in range(NG):
            i_ = wslot[0]
            wslot[0] = (i_ + 1) % NSLOT
            wa = wring[i_][:, 0:4096].rearrange("p (a b c) -> p a b c", a=8, b=2)
            wa_b = wring_b[i_]
            for which, c0 in ((0, g * 256), (1, DIN + g * 256)):
                P.dma("pool", wa[:, :, which, :], bass.AP(w_ssm_in, li * D * WIN_COLS + c0, [[WIN_COLS, 128], [128 * WIN_COLS, 8], [1, 256]]),
                      writes=[wa_b], ring=True)
            i_ = wslot[0]
            wslot[0] = (i_ + 1) % NSLOT
            wbc = wring[i_][:, 0:2048].rearrange("p (a b c) -> p a b c", a=8, b=2)
            wbc_b = wring_b[i_]
            for which, c0 in ((0, 2 * DIN + g * 128), (1, 2 * DIN + 1024 + g * 128)):
                P.dma("pool", wbc[:, :, which, :], bass.AP(w_ssm_in, li * D * WIN_COLS + c0, [[WIN_COLS, 128], [128 * WIN_COLS, 8], [1, 128]]),
                      writes=[wbc_b], ring=True)
            P.dma("sp", nwg, bass.AP(ssm_norm_w, li * DIN + g * 256, [[0, 128], [1, 256]]), writes=[nwg_b], arena=True)
            gfts = [2 * g, 2 * g + 1, 16 + g, 24 + g]
            ch0s = [g * 256, g * 256 + 128, DIN + g * 128, DIN + 1024 + g * 128]

            for sg in segs:
                if sg.Lt == 0:
                    continue
                if sg.kind == "s":
                    h_src, h_rb = state_ssm.ap()[li, sg.s], []
                else:
                    h_src, h_rb = hscr.ap()[li], [hscr_gb[li][g]]
                P.dma("sp", hld[:, sg.idx, :, :], h_src[g * 256:(g + 1) * 256, :].rearrange("(k p) n -> p k n", p=128),
                      reads=h_rb, writes=[hld_b[sg.idx]], arena=True)
            for tt in range(ntt):
                pb, pbuf = bank()
                for kc in range(8):
                    mm(pb[:, 0:256], xT[:, kc, tt * 128:(tt + 1) * 128], wa[:, kc, 0, :], kc == 0, kc == 7, reads=[wa_b, xT_b[tt]], writes=[pbuf])
                P.op("act", ("activation", dict(out=zs[:, tt, :], in_=pb[:, 0:256], func=AF.Silu)), reads=[pbuf], writes=[zs_b[tt]])
            for fi in range(4):
                for k in range(4):
                    P.op("dve", ("tensor_scalar", dict(out=dg[:, fi * 4 + k, :], in0=ident_bf, scalar1=cw[:, gfts[fi], k:k + 1], scalar2=None,
                                                       op0=ALU.mult)), reads=[cstbf_b, par_b], writes=[dg_b])
            for sg in segs:
                off = sg.idx * 131 if sg.kind == "s" else 0
                for fi in range(4):
                    if sg.Lt == 0:
                        P.op("dve", ("memset", dict(ap=pre[:, fi, off:off + 3], constant=0.0)), writes=[pre_b[fi]])
                    else:
                        if sg.kind == "s":
                            src_t, base, rb = state_conv, (li * NSS + sg.s) * 3 * CCH, []
                        else:
                            src_t, base, rb = cscr, li * 3 * CCH, [cscr_gb[li][g]]
                        P.dma("sp", hal[:, fi, sg.idx, :], bass.AP(src_t, base + ch0s[fi], [[1, 128], [CCH, 3]]),
                              reads=rb, writes=[hal_b[fi]], arena=True, slow=True)
                        P.op("dve", ("tensor_copy", dict(out=pre[:, fi, off:off + 3], in_=hal[:, fi, sg.idx, :])),
                             reads=[hal_b[fi]], writes=[pre_b[fi]])
            for fi in range(4):
                for t0 in range(0, nt, 512):
                    tn = min(512, nt - t0)
                    pb, pbuf = bank()
                    for kc in range(8):
                        lw = wa[:, kc, 1, fi * 128:(fi + 1) * 128] if fi < 2 else wbc[:, kc, fi - 2, :]
                        mm(pb[:, 0:tn], lw, xT[:, kc, t0:t0 + tn], kc == 0, kc == 7,
                           reads=[wa_b if fi < 2 else wbc_b] + xT_b[t0 // 128:(t0 + tn) // 128], writes=[pbuf])
                    if kind == "p":
                        dst, src = pre[:, fi, 3 + t0:3 + t0 + tn], pb[:, 0:tn]
                        if t0 + tn == nt:
                            P.op("dve", ("tensor_copy", dict(out=hout3[:, fi, 0, :], in_=pb[:, tn - 3:tn])), reads=[pbuf], writes=[hout3_b[fi]])
                    else:
                        n_sg, s0 = tn // 128, t0 // 128
                        dst = pre[:, fi, s0 * 131:(s0 + n_sg) * 131].rearrange("p (s c) -> p s c", c=131)[:, :, 3:131]
                        src = pb[:, 0:tn].rearrange("p (s c) -> p s c", c=128)
                        P.op("dve", ("tensor_copy", dict(out=hout3[:, fi, s0:s0 + n_sg, :], in_=src[:, :, DSEQ - 3:DSEQ])),
                             reads=[pbuf], writes=[hout3_b[fi]])
                    copy_op("act", dst, src, [pbuf], [pre_b[fi]])
            for sg in segs:
                if sg.kind == "s":
                    dst_t, base, wbf = conv_sample, (li * NSS + sg.s) * 3 * CCH, []
                elif sg.final:
                    dst_t, base, wbf = conv_prompt, (li * NPS + sg.s) * 3 * CCH, []
                else:
                    dst_t, base, wbf = cscr, li * 3 * CCH, [cscr_gb[li][g]]
                for fi in range(4):
                    P.dma("sp", bass.AP(dst_t, base + ch0s[fi], [[1, 128], [CCH, 3]]), hout3[:, fi, sg.idx, :],
                          reads=[hout3_b[fi]], writes=wbf, slow=True)
            for fi in range(4):
                gf = gfts[fi]
                for t0 in range(0, nt, 512):
                    tn = min(512, nt - t0)
                    pb, pbuf = bank()
                    for k in range(4):
                        if kind == "p":
                            rhs_ = pre[:, fi, t0 + k:t0 + k + tn]
                            out_ = pb[:, 0:tn]
                        else:
                            n_sg, s0 = tn // 128, t0 // 128
                            rhs_ = pre[:, fi, s0 * 131:(s0 + n_sg) * 131].rearrange("p (s c) -> p s c", c=131)[:, :, k:k + 128]
                            out_ = pb[:, 0:tn].rearrange("p (s c) -> p s c", c=128)
                        mm(out_, dg[:, fi * 4 + k, :], rhs_, k == 0, k == 3, reads=[dg_b, pre_b[fi]], writes=[pbuf])
                    P.op("act", ("activation", dict(out=xcT[:, fi, t0:t0 + tn], in_=pb[:, 0:tn], func=AF.Silu, bias=cb[:, gf:gf + 1])),
                         reads=[pbuf, par_b], writes=[xcT_b[fi]])
            for tt in range(ntt):
                pb, pbuf = bank()
                pbv = pb.bitcast(BF16)
                for fi in range(3):
                    P.op("pe", ("transpose", dict(out=pbv[:, fi * 128:(fi + 1) * 128], in_=xcT[:, fi, tt * 128:(tt + 1) * 128], identity=ident_bf)),
                         reads=[xcT_b[fi], cstbf_b], writes=[pbuf], inc=(fi == 2))
                copy_op(evac_engine(), tokm[:, tt, :], pbv[:, 0:384], [pbuf], [tokm_b[tt]])
            for sg in segs:
                Hs, Hs_b, Hb, Hb_b = HsA[:, sg.idx, :], HsA_b[sg.idx], HbA[:, sg.idx, :], HbA_b[sg.idx]
                if sg.kind == "s":
                    h_src, h_rb = state_ssm.ap()[li, sg.s], []
                else:
                    h_src, h_rb = hscr.ap()[li], [hscr_gb[li][g]]
                if sg.Lt == 0:
                    P.op("dve", ("memset", dict(ap=Hs, constant=0.0)), writes=[Hs_b])
                else:
                    pb, pbuf = bank()
                    for k2 in range(2):
                        P.op("pe", ("transpose", dict(out=pb[:, k2 * 128:(k2 + 1) * 128], in_=hld[:, sg.idx, k2, :], identity=ident_f)),
                             reads=[hld_b[sg.idx], cst_b], writes=[pbuf], inc=(k2 == 1))
                    P.op("dve", ("tensor_copy", dict(out=Hs, in_=pb[:, 0:256])), reads=[pbuf], writes=[Hs_b])
                P.op("act", ("activation", dict(out=Hb, in_=Hs, func=AF.Copy)), reads=[Hs_b], writes=[Hb_b])
            h4 = slice(4 * g, 4 * g + 4)

            def ssd_iter(c1, ca, cb_, cc, h4=h4):
                t1 = ta = tb = tc = None
                if c1 is not None:
                    sg1, t1, k1 = c1
                if cb_ is not None:
                    sgb, tb, kb = cb_
                if cc is not None:
                    sgc, tc, kc_ = cc
                if ca is not None:
                    sga, ta, ka = ca
                    Hs, Hs_b, Hb, Hb_b = HsA[:, sga.idx, :], HsA_b[sga.idx], HbA[:, sga.idx, :], HbA_b[sga.idx]
                    tt = sga.tile0 + ta
                    nv = sga.nvalid
                    p_ = ka % 2
                    cs = slice(tt * 128, (tt + 1) * 128)
                    xs3 = tokm[:, tt, 0:256].rearrange("p (h d) -> p h d", h=4)
                    eh3 = eh[:, p_, :].rearrange("p (h t) -> p h t", h=4)
                    pb, pbuf = psum[:, 0 + p_, :], ps_b[0 + p_]
                    pc, pcbuf = psum[:, 2 + p_, :], ps_b[2 + p_]
                    py, pybuf = psum[:, 4 + p_, :], ps_b[4 + p_]
                if tb is not None:
                    ttb = sgb.tile0 + tb
                    q_ = kb % 2
                    pyb, pybbuf = psum[:, 4 + q_, :], ps_b[4 + q_]
                if tc is not None:
                    ttc = sgc.tile0 + tc
                    c_ = kc_ % 2
                    csc = slice(ttc * 128, (ttc + 1) * 128)
                    pt, ptbuf = psum[:, 6, :], ps_b[6]
                    ptv = pt.bitcast(BF16)
                    for k2 in range(2):
                        P.op("pe", ("transpose", dict(out=ptv[:, k2 * 128:(k2 + 1) * 128], in_=yn[:, c_, k2 * 128:(k2 + 1) * 128], identity=ident_bf)),
                             reads=[yn_b[c_], cstbf_b], writes=[ptbuf], inc=(k2 == 1))
                if t1 is not None:
                    tt1 = sg1.tile0 + t1
                    r_ = k1 % 2
                    cs1 = slice(tt1 * 128, (tt1 + 1) * 128)
                    mm(psum[:, 0 + r_, 0:128], xcT[:, 2, cs1], xcT[:, 3, cs1], True, True, reads=[xcT_b[2], xcT_b[3]], writes=[ps_b[0 + r_]])
                    pcb, pcbbuf = psum[:, 2 + r_, :], ps_b[2 + r_]
                    for h in range(4):
                        hc = 4 * g + h
                        for pi, dX in enumerate((dAh, dAl)):
                            P.op("pe", ("matmul", dict(out=pcb[:, h * 128:(h + 1) * 128], lhsT=bc_last(dX[:, tt1, hc:hc + 1], 128).rearrange("p a b -> p (a b)") if False else bass.AP(dX.tensor, dX[:, tt1, hc:hc + 1].offset, [list(dX[:, tt1, hc:hc + 1].ap[0]), [0, 128]]),
                                                       rhs=tri_bf, start=(pi == 0 and h == 0), stop=False, skip_group_check=True)),
                                 reads=[sv_b["dA"][tt1], negm_b], writes=[pcbbuf], inc=False)
                    pcb3 = pcb.rearrange("p (h t) -> p h t", h=4)
                    for dX in (dAh, dAl):
                        P.op("pe", ("matmul", dict(out=pcb3, lhsT=negtri_bf, rhs=bc_last(dX[:, tt1, h4], 128), start=False, stop=False,
                                                   skip_group_check=True)),
                             reads=[sv_b["dA"][tt1], negm_b], writes=[pcbbuf], inc=False)
                    P.op("pe", ("matmul", dict(out=pcb, lhsT=ident_bf, rhs=negm.rearrange("p h t -> p (h t)"), start=False, stop=True,
                                               skip_group_check=True)),
                         reads=[cstbf_b, negm_b], writes=[pcbbuf])
                if tb is not None:
                    for h in range(4):
                        P.op("act", ("activation", dict(out=junk[:, q_, h * 64:(h + 1) * 64], in_=pyb[:, 256 + h * 64:256 + (h + 1) * 64],
                                                        func=AF.Identity, scale=sv["ecum"][:, ttb, 4 * g + h:4 * g + h + 1])),
                             reads=[pybbuf, sv_b["ecum"][ttb]], writes=[junk_b[q_]])
                if ta is not None:
                    P.op("act", ("activation", dict(out=eh[:, p_, :], in_=pc, func=AF.Exp)), reads=[pcbuf], writes=[eh_b[p_]])
                    if ta > 0:
                        P.op("act", ("activation", dict(out=Hb, in_=Hs, func=AF.Copy)), reads=[Hs_b], writes=[Hb_b])
                    xs_ap = tokm[:, tt, 0:256]
                    in0_ = bass.AP(xs_ap.tensor, xs_ap.offset, [list(xs_ap.ap[0]), [0, 3], [64, 4], [1, 64]])
                    m_ap = mdw[:, tt, :, 4 * g:4 * g + 4]
                    in1_ = bass.AP(m_ap.tensor, m_ap.offset, [list(m_ap.ap[0]), [32, 3], [1, 4], [0, 64]])
                    P.op("dve", ("tensor_tensor", dict(out=X3[:, p_, :, :].rearrange("p j (h d) -> p j h d", h=4), in0=in0_, in1=in1_, op=ALU.mult)),
                         reads=[tokm_b[tt], sv_b["dw"][tt]], writes=[X3_b[p_]])
                if tc is not None:
                    copy_op("act", ynT[:, 2 * (g % 2):2 * (g % 2) + 2, csc], ptv[:, 0:256].rearrange("p (k t) -> p k t", k=2), [ptbuf], [ynT_b[ttc]])
                if tb is not None:
                    P.op("dve", ("tensor_tensor", dict(out=Ysb[:, q_, :], in0=pyb[:, 0:256], in1=junk[:, q_, :], op=ALU.add)),
                         reads=[pybbuf, junk_b[q_]], writes=[Ysb_b[q_]])
                    P.op("dve", ("tensor_tensor", dict(out=Ysb[:, q_, :], in0=Ysb[:, q_, :], in1=zs[:, ttb, :], op=ALU.mult)),
                         reads=[Ysb_b[q_], zs_b[ttb]], writes=[Ysb_b[q_]])
                    P.op("act", ("activation", dict(out=junk[:, q_, :], in_=Ysb[:, q_, :], func=AF.Square, accum_out=gst[:, q_, 0:1])),
                         reads=[Ysb_b[q_]], writes=[junk_b[q_], gst_b[q_]])
                    P.op("act", ("activation", dict(out=gst[:, q_, 1:2], in_=gst[:, q_, 0:1], func=AF.Ln, scale=1.0 / 256.0, bias=epsc[:, 1:2])),
                         reads=[gst_b[q_], epsc_b], writes=[gst_b[q_]])
                    P.op("act", ("activation", dict(out=gst[:, q_, 1:2], in_=gst[:, q_, 1:2], func=AF.Exp, scale=-0.5)),
                         reads=[gst_b[q_]], writes=[gst_b[q_]])
                if ta is not None:
                    P.op("dve", ("tensor_tensor", dict(out=MT[:, p_, :].rearrange("p (h t) -> p h t", h=4), in0=eh3,
                                                       in1=bc_mid(pb[:, 0:128], 4), op=ALU.mult)),
                         reads=[eh_b[p_], pbuf], writes=[MT_b[p_]])
                    for h in range(4):
                        hs_ = slice(h * 64, (h + 1) * 64)
                        mm(py[:, hs_], MT[:, p_, h * 128:(h + 1) * 128], X3[:, p_, 0, hs_], True, False,
                           reads=[MT_b[p_], X3_b[p_]], writes=[pybuf], inc=False)
                        mm(py[:, hs_], ident_bf, X3[:, p_, 2, hs_], False, True,
                           reads=[cstbf_b, X3_b[p_]], writes=[pybuf], inc=False)
                    mm(py[:, 256:512], xcT[:, 3, cs], Hb, True, True, reads=[xcT_b[3], Hb_b], writes=[pybuf])
                    ph, phbuf = psum[:, 7, :], ps_b[7]
                    mm(ph[:, 0:256], tokm[0:nv, tt, 256:384], X3[0:nv, p_, 1, :], True, True, reads=[tokm_b[tt], X3_b[p_]], writes=[phbuf])
                if tb is not None:
                    P.op("dve", ("scalar_tensor_tensor", dict(out=yn[:, q_, :], in0=Ysb[:, q_, :], scalar=gst[:, q_, 1:2], in1=nwg,
                                                              op0=ALU.mult, op1=ALU.mult)),
                         reads=[Ysb_b[q_], gst_b[q_], nwg_b], writes=[yn_b[q_]])
                if ta is not None:
                    Hs3 = Hs.rearrange("p (h d) -> p h d", h=4)
                    P.op("dve", ("tensor_tensor", dict(out=Hs3, in0=Hs3, in1=bc_last(sv["ecl"][:, tt, h4], 64), op=ALU.mult)),
                         reads=[Hs_b, sv_b["ecl"][tt]], writes=[Hs_b])
                    P.op("dve", ("tensor_tensor", dict(out=Hs, in0=Hs, in1=ph[:, 0:256], op=ALU.add)), reads=[Hs_b, phbuf], writes=[Hs_b])

            chunks = []
            for sg in segs:
                for ti in range(sg.ntiles):
                    chunks.append((sg, ti, len(chunks)))
            n_ = len(chunks)
            rng = lambda v: chunks[v] if 0 <= v < n_ else None
            for k in range(n_ + 3):
                ssd_iter(rng(k), rng(k - 1), rng(k - 2), rng(k - 3))
            for sg in segs:
                Hs, Hs_b = HsA[:, sg.idx, :], HsA_b[sg.idx]
                if sg.kind == "s":
                    h_dst, h_wb = ssm_sample.ap()[li, sg.s], []
                elif sg.final:
                    h_dst, h_wb = ssm_prompt.ap()[li, sg.s], []
                else:
                    h_dst, h_wb = hscr.ap()[li], [hscr_gb[li][g]]
                pb, pbuf = bank()
                for k2 in range(2):
                    P.op("pe", ("transpose", dict(out=pb[:, k2 * 128:(k2 + 1) * 128], in_=Hs[:, k2 * 128:(k2 + 1) * 128], identity=ident_f)),
                         reads=[Hs_b, cst_b], writes=[pbuf], inc=(k2 == 1))
                P.op("dve", ("tensor_copy", dict(out=hout, in_=pb[:, 0:256].rearrange("p (k n) -> p k n", k=2))), reads=[pbuf], writes=[hout_b])
                P.dma("sp", h_dst[g * 256:(g + 1) * 256, :].rearrange("(k p) n -> p k n", p=128), hout, reads=[hout_b], writes=h_wb)
            if g % 2 == 1:
                wo, wo_b = load_w(w_ssm_out.ap()[li, (g - 1) * 256:(g + 1) * 256, :].rearrange("(k p) c -> p k c", p=128), [128, 4, 1024])
                for tt in range(ntt):
                    for ch in range(2):
                        pb, pbuf = bank()
                        for k4 in range(4):
                            mm(pb, ynT[:, k4, tt * 128:(tt + 1) * 128], wo[:, k4, ch * 512:(ch + 1) * 512], k4 == 0, k4 == 3,
                               reads=[wo_b, ynT_b[tt]], writes=[pbuf])
                        xs_ = xres[:, tt, ch * 512:(ch + 1) * 512]
                        P.op("dve", ("scalar_tensor_tensor", dict(out=xs_, in0=xs_, scalar=(ALPHA if g == 1 else 1.0), in1=pb,
                                                                  op0=ALU.mult, op1=ALU.add)), reads=[pbuf, xres_b[tt]], writes=[xres_b[tt]])

    blocks = []
    nb = SEQ // TB
    for s_ in range(NPS):
        for b in range(nb):
            blocks.append(("p", [Seg("p", s_, b * TB, TB // 128, 0, 128, 0 if b == 0 else WIN, b == nb - 1, 0)]))
    blocks.append(("s", [Seg("s", s_, 0, 1, s_, DSEQ, WIN, True, s_) for s_ in range(NSS)]))

    for kind, segs in blocks:
        ntt = sum(sg.ntiles for sg in segs)
        nt = ntt * 128
        for sg in segs:
            for ti in range(sg.ntiles):
                tt = sg.tile0 + ti
                if kind == "p":
                    P.dma("sp", xres[:, tt, :], x_prompt.ap()[sg.s, sg.t0 + ti * 128: sg.t0 + (ti + 1) * 128, :],
                          writes=[xres_b[tt]])
                else:
                    P.op("dve", ("memset", dict(ap=xres[:, tt, :], constant=0.0)), writes=[xres_b[tt]])
                    P.dma("sp", xres[0:DSEQ, tt, :], x_sample.ap()[sg.s, :, :], writes=[xres_b[tt]])
                make_xT(tt)
        for layer in range(cfg.layers):
            P.barrier()
            mixed = False
            if layer % 2 == 0 and cfg.stage in ("attn", "full"):
                attention_layer(layer // 2, kind, segs, ntt)
                mixed = True
            if layer % 2 == 1 and cfg.stage in ("ssm", "full"):
                ssm_layer(layer // 2, kind, segs, ntt)
                mixed = True
            P.barrier()
            load_ln(layer, 0)
            ln_phase(list(range(ntt)), not mixed, make_xT)
            P.barrier()
            mlp(layer, nt)
            load_ln(layer, 1)
            last = layer == cfg.layers - 1
            tile_seg = {}
            for sg in segs:
                for ti in range(sg.ntiles):
                    tile_seg[sg.tile0 + ti] = (sg, ti)

            def store_y(tt, tile_seg=tile_seg, kind=kind):
                sg, ti = tile_seg[tt]
                if kind == "p":
                    P.dma("sp", y_prompt.ap()[sg.s, sg.t0 + ti * 128: sg.t0 + (ti + 1) * 128, :], xres[:, tt, :],
                          reads=[xres_b[tt]])
                else:
                    P.dma("sp", y_sample.ap()[sg.s, :, :], xres[0:DSEQ, tt, :], reads=[xres_b[tt]])

            ln_phase(list(range(ntt)), False, store_y if last else make_xT)
            P.barrier()

    P.finish()
    P.emit()
    return nc


def _consts():
    c = np.zeros((6, 128, 128), np.float32)
    i = np.arange(128)
    c[0] = np.eye(128, dtype=np.float32)
    c[1] = (i[:, None] <= i[None, :]).astype(np.float32)
    c[2] = 1.0
    same = (i[:, None] // 64) == (i[None, :] // 64)
    c[3] = c[1] * same
    c[4] = same.astype(np.float32)
    return c


def _bias_index():
    k = np.arange(128)[:, None]
    col = np.arange(640)[None, :]
    j, r = col // 128, col % 128
    q = 128 * (4 - j) + r
    idx = np.minimum(q - k, 256) + 256
    valid = np.where(k < 64, q < 576, q >= 64)
    mask = np.where(valid, 0.0, -1e30).astype(np.float32)
    return idx, mask


def prepare(inputs, cfg):
    f = lambda a: np.ascontiguousarray(np.asarray(a, dtype=np.float32))
    NPS, NSS = cfg.NPS, cfg.NSS
    rel = f(inputs["rel_bias"])
    idx, amask = _bias_index()
    biasT = np.ascontiguousarray(rel[:, :, idx])
    lnp = np.ascontiguousarray(np.stack([f(inputs["ln_mix_g"]), f(inputs["ln_mix_b"]),
                                         f(inputs["ln_ff_g"]), f(inputs["ln_ff_b"])], axis=1))
    conv_wT = np.ascontiguousarray(f(inputs["conv_w"]).reshape(2, 4, 32, 128).transpose(0, 3, 2, 1))
    conv_bT = np.ascontiguousarray(f(inputs["conv_b"]).reshape(2, 32, 128).transpose(0, 2, 1))
    ssm_vec = np.ascontiguousarray(np.stack([f(inputs["dt_bias"]), f(inputs["a_log"]), f(inputs["d_skip"])], axis=1))
    shared = {
        "w_qkv": f(inputs["w_qkv"]), "w_attn_out": f(inputs["w_attn_out"]), "w_ssm_in": f(inputs["w_ssm_in"]),
        "w_ssm_out": f(inputs["w_ssm_out"]), "w_ff_up": f(inputs["w_ff_up"]), "w_ff_down": f(inputs["w_ff_down"]),
        "biasT": biasT, "lnp": lnp, "conv_wT": conv_wT, "conv_bT": conv_bT, "ssm_vec": ssm_vec,
        "ssm_norm_w": f(inputs["ssm_norm_w"]), "consts": _consts(), "amask": amask,
    }
    xp, xs = f(inputs["x_prompt"]), f(inputs["x_sample"])
    ck, cv = f(inputs["cache_k"]), f(inputs["cache_v"])
    ss, sc = f(inputs["state_ssm"]), f(inputs["state_conv"])
    maps = []
    for c in range(NCORES):
        m = dict(shared)
        m["x_prompt"] = np.ascontiguousarray(xp[c * NPS:(c + 1) * NPS])
        m["x_sample"] = np.ascontiguousarray(xs[c * NSS:(c + 1) * NSS])
        m["cache_k"] = np.ascontiguousarray(ck[:, c * NSS:(c + 1) * NSS].reshape(2, NSS, WIN, D))
        m["cache_v"] = np.ascontiguousarray(cv[:, c * NSS:(c + 1) * NSS].reshape(2, NSS, WIN, D))
        m["state_ssm"] = np.ascontiguousarray(ss[:, c * NSS:(c + 1) * NSS].reshape(2, NSS, DIN, DST))
        m["state_conv"] = np.ascontiguousarray(sc[:, c * NSS:(c + 1) * NSS])
        maps.append(m)
    return maps


def assemble(results, cfg):
    NPS, NSS, SEQ, DSEQ, KEEP = cfg.NPS, cfg.NSS, cfg.SEQ, cfg.DSEQ, cfg.KEEP
    cat0 = lambda k: np.concatenate([r[k] for r in results], axis=0)
    cat1 = lambda k: np.concatenate([r[k] for r in results], axis=1)
    return (
        cat0("y_prompt"), cat0("y_sample"),
        cat1("k_prompt").reshape(2, NCORES * NPS, KEEP, NH, HD),
        cat1("v_prompt").reshape(2, NCORES * NPS, KEEP, NH, HD),
        cat1("ssm_prompt").reshape(2, NCORES * NPS, 32, 64, DST),
        cat1("conv_prompt"),
        cat1("k_sample").reshape(2, NCORES * NSS, DSEQ, NH, HD),
        cat1("v_sample").reshape(2, NCORES * NSS, DSEQ, NH, HD),
        cat1("ssm_sample").reshape(2, NCORES * NSS, 32, 64, DST),
        cat1("conv_sample"),
    )


def run(inputs, cfg, trace=False):
    nc = build(cfg)
    maps = prepare(inputs, cfg)
    res = run_bass_kernel_spmd(nc, maps, core_ids=list(range(NCORES)), trace=trace)
    out = assemble(res.results, cfg)
    if trace:
        return out, res
    return out


def kernel(**inputs):
    cfg = Cfg(NPS=4, SEQ=2048, TB=1024, NSS=4)
    return run(inputs, cfg)
```
